# Optimizing a Trainium2 kernel written in Bass

```python
import math
import jax, jax.numpy as jnp
from jax import lax
import numpy as np

D_MODEL = 1024
BATCH = 8
SEQ = 4096
DEPTH = 2

N_MIXERS = 2
N_FNET_LAYERS = (DEPTH + N_MIXERS - 1) // N_MIXERS
N_SSD_LAYERS = DEPTH // N_MIXERS
LN_EPS = 1e-5
DEEPNORM_ALPHA = (2.0 * DEPTH) ** 0.25
DEEPNORM_BETA = (8.0 * DEPTH) ** -0.25

FN_WIDTH = D_MODEL
FN_GROUPS = 8
FN_GROUP_DIM = FN_WIDTH // FN_GROUPS

SSM_EXPAND = 2
D_INNER = SSM_EXPAND * D_MODEL
HEAD_DIM = 64
N_SSM_HEADS = D_INNER // HEAD_DIM
SSM_GROUPS = 8
HEADS_PER_GROUP = N_SSM_HEADS // SSM_GROUPS
D_STATE = 128
SSM_CONV = 5
CHUNK = 128
CONV_DIM = D_INNER + 2 * SSM_GROUPS * D_STATE
D_IN_PROJ = D_INNER + CONV_DIM + 2 * N_SSM_HEADS

D_FF = 2816
FF_CONV = 3

kernel_name = "bidir_fnet_ssd_hybrid_deepnorm"


def layer_norm(x, g, b):
    xf = x.astype(jnp.float32)
    mu = jnp.mean(xf, axis=-1, keepdims=True)
    xc = xf - mu
    var = jnp.mean(xc * xc, axis=-1, keepdims=True)
    return (xc * lax.rsqrt(var + LN_EPS) * g + b).astype(x.dtype)


def depthwise_conv(x, w, b):
    k = w.shape[0]
    y = lax.conv_general_dilated(
        x, w[:, None, :], window_strides=(1,), padding=[(k // 2, k // 2)],
        dimension_numbers=("NWC", "WIO", "NWC"), feature_group_count=x.shape[-1])
    return y + b


def fourier_mixer(x, w_in, b_in, w_out, b_out):
    bsz, s, _ = x.shape
    h = (x @ w_in + b_in).reshape(bsz, s, FN_GROUPS, FN_GROUP_DIM)
    f = jnp.fft.fft2(h.astype(jnp.float32), axes=(1, 3), norm="ortho").real
    return f.reshape(bsz, s, FN_WIDTH).astype(x.dtype) @ w_out + b_out


def ssd_scan(x, dt, a, bm, cm):
    b, s, g, r, p = x.shape
    n = bm.shape[-1]
    nc = s // CHUNK
    xc = x.reshape(b, nc, CHUNK, g, r, p)
    dtc = dt.reshape(b, nc, CHUNK, g, r)
    bc = bm.reshape(b, nc, CHUNK, g, n)
    cc = cm.reshape(b, nc, CHUNK, g, n)
    a_cum = jnp.cumsum(dtc * a, axis=2)
    lower = jnp.tril(jnp.ones((CHUNK, CHUNK), dtype=bool))
    seg = a_cum[:, :, :, None] - a_cum[:, :, None, :]
    decay = jnp.exp(jnp.where(lower[:, :, None, None], seg, -jnp.inf))
    cb = jnp.einsum("bclgn,bcsgn->bclsg", cc, bc)
    y_diag = jnp.einsum("bclsgr,bcsgrp->bclgrp",
                        cb[..., None] * decay * dtc[:, :, None], xc)
    decay_to_end = jnp.exp(a_cum[:, :, -1:] - a_cum)
    states = jnp.einsum("bcsgn,bcsgrp->bcgrpn", bc,
                        xc * (decay_to_end * dtc)[..., None])
    chunk_decay = jnp.exp(a_cum[:, :, -1])

    def _step(h, inp):
        st, dec = inp
        return h * dec[..., None, None] + st, h

    _, prev = lax.scan(_step, jnp.zeros_like(states[:, 0]),
                       (jnp.moveaxis(states, 1, 0), jnp.moveaxis(chunk_decay, 1, 0)))
    prev = jnp.moveaxis(prev, 0, 1)
    y_off = jnp.einsum("bclgn,bcgrpn->bclgrp", cc, prev) * jnp.exp(a_cum)[..., None]
    return (y_diag + y_off).reshape(b, s, g, r, p)


def ssd_mixer(x, w_in, conv_w, conv_b, a_log_f, a_log_b, dt_bias_f, dt_bias_b,
              d_skip, norm_g, w_out):
    bsz, s, _ = x.shape
    f32 = jnp.float32
    proj = x @ w_in
    z, xbc, dt_raw = jnp.split(proj, [D_INNER, D_INNER + CONV_DIM], axis=-1)
    xbc = jax.nn.silu(depthwise_conv(xbc, conv_w, conv_b))
    xs, bm, cm = jnp.split(xbc, [D_INNER, D_INNER + SSM_GROUPS * D_STATE], axis=-1)
    xs = xs.astype(f32).reshape(bsz, s, SSM_GROUPS, HEADS_PER_GROUP, HEAD_DIM)
    bm = bm.astype(f32).reshape(bsz, s, SSM_GROUPS, D_STATE)
    cm = cm.astype(f32).reshape(bsz, s, SSM_GROUPS, D_STATE)
    dt_f, dt_b = jnp.split(dt_raw.astype(f32), 2, axis=-1)
    dt_f = jax.nn.softplus(dt_f + dt_bias_f.astype(f32)).reshape(bsz, s, SSM_GROUPS, HEADS_PER_GROUP)
    dt_b = jax.nn.softplus(dt_b + dt_bias_b.astype(f32)).reshape(bsz, s, SSM_GROUPS, HEADS_PER_GROUP)
    a_f = -jnp.exp(a_log_f.astype(f32)).reshape(SSM_GROUPS, HEADS_PER_GROUP)
    a_b = -jnp.exp(a_log_b.astype(f32)).reshape(SSM_GROUPS, HEADS_PER_GROUP)
    y_f = ssd_scan(xs, dt_f, a_f, bm, cm)
    y_b = jnp.flip(ssd_scan(jnp.flip(xs, 1), jnp.flip(dt_b, 1), a_b,
                            jnp.flip(bm, 1), jnp.flip(cm, 1)), 1)
    y = y_f + y_b + d_skip.astype(f32).reshape(SSM_GROUPS, HEADS_PER_GROUP, 1) * xs
    y = y.reshape(bsz, s, D_INNER) * jax.nn.silu(z.astype(f32))
    yg = y.reshape(bsz, s, SSM_GROUPS, D_INNER // SSM_GROUPS)
    yg = yg * lax.rsqrt(jnp.mean(yg * yg, axis=-1, keepdims=True) + LN_EPS)
    y = yg.reshape(bsz, s, D_INNER) * norm_g
    return y.astype(x.dtype) @ w_out


def conv_ffn(x, w_up, conv_w, conv_b, w_down):
    h = depthwise_conv(x @ w_up, conv_w, conv_b)
    g, v = jnp.split(h, 2, axis=-1)
    return (jax.nn.silu(g) * v) @ w_down


def setup_inputs(seed: int = 0) -> dict:
    key = jax.random.key(seed)
    ks = jax.random.split(key, 32)
    f32 = jnp.float32

    def nrm(k, shape, scale):
        return scale * jax.random.normal(k, shape, f32)

    d = D_MODEL
    dt = jnp.exp(jax.random.uniform(ks[11], (N_SSD_LAYERS, N_SSM_HEADS), f32,
                                    math.log(1e-3), math.log(1e-1)))
    dt2 = jnp.exp(jax.random.uniform(ks[12], (N_SSD_LAYERS, N_SSM_HEADS), f32,
                                     math.log(1e-3), math.log(1e-1)))
    return {
        "x": nrm(ks[0], (BATCH, SEQ, d), 1.0),
        "emb_ln_g": 1.0 + nrm(ks[1], (d,), 0.02),
        "emb_ln_b": nrm(ks[2], (d,), 0.02),
        "fn_w_in": nrm(ks[3], (N_FNET_LAYERS, d, FN_WIDTH), d ** -0.5),
        "fn_b_in": nrm(ks[4], (N_FNET_LAYERS, FN_WIDTH), 0.02),
        "fn_w_out": nrm(ks[5], (N_FNET_LAYERS, FN_WIDTH, d), FN_WIDTH ** -0.5 * DEEPNORM_BETA),
        "fn_b_out": nrm(ks[6], (N_FNET_LAYERS, d), 0.02),
        "ssd_w_in": nrm(ks[7], (N_SSD_LAYERS, d, D_IN_PROJ), d ** -0.5),
        "ssd_conv_w": nrm(ks[8], (N_SSD_LAYERS, SSM_CONV, CONV_DIM), SSM_CONV ** -0.5),
        "ssd_conv_b": nrm(ks[9], (N_SSD_LAYERS, CONV_DIM), 0.02),
        "ssd_a_log_fwd": jnp.log(jax.random.uniform(ks[10], (N_SSD_LAYERS, N_SSM_HEADS), f32, 1.0, 16.0)),
        "ssd_a_log_bwd": jnp.log(jax.random.uniform(ks[13], (N_SSD_LAYERS, N_SSM_HEADS), f32, 1.0, 16.0)),
        "ssd_dt_bias_fwd": dt + jnp.log(-jnp.expm1(-dt)),
        "ssd_dt_bias_bwd": dt2 + jnp.log(-jnp.expm1(-dt2)),
        "ssd_d": 1.0 + nrm(ks[14], (N_SSD_LAYERS, N_SSM_HEADS), 0.1),
        "ssd_norm_g": 1.0 + nrm(ks[15], (N_SSD_LAYERS, D_INNER), 0.02),
        "ssd_w_out": nrm(ks[16], (N_SSD_LAYERS, D_INNER, d), D_INNER ** -0.5 * DEEPNORM_BETA),
        "ln_tok_g": 1.0 + nrm(ks[17], (DEPTH, d), 0.02),
        "ln_tok_b": nrm(ks[18], (DEPTH, d), 0.02),
        "ff_w_up": nrm(ks[19], (DEPTH, d, 2 * D_FF), d ** -0.5),
        "ff_conv_w": nrm(ks[20], (DEPTH, FF_CONV, 2 * D_FF), FF_CONV ** -0.5),
        "ff_conv_b": nrm(ks[21], (DEPTH, 2 * D_FF), 0.02),
        "ff_w_down": nrm(ks[22], (DEPTH, D_FF, d), D_FF ** -0.5 * DEEPNORM_BETA),
        "ln_ffn_g": 1.0 + nrm(ks[23], (DEPTH, d), 0.02),
        "ln_ffn_b": nrm(ks[24], (DEPTH, d), 0.02),
    }


def reference(x, emb_ln_g, emb_ln_b, fn_w_in, fn_b_in, fn_w_out, fn_b_out,
              ssd_w_in, ssd_conv_w, ssd_conv_b, ssd_a_log_fwd, ssd_a_log_bwd,
              ssd_dt_bias_fwd, ssd_dt_bias_bwd, ssd_d, ssd_norm_g, ssd_w_out,
              ln_tok_g, ln_tok_b, ff_w_up, ff_conv_w, ff_conv_b, ff_w_down,
              ln_ffn_g, ln_ffn_b):
    h = layer_norm(x, emb_ln_g, emb_ln_b)
    for i in range(DEPTH):
        j = i // N_MIXERS
        if i % N_MIXERS == 0:
            mix = fourier_mixer(h, fn_w_in[j], fn_b_in[j], fn_w_out[j], fn_b_out[j])
        else:
            mix = ssd_mixer(h, ssd_w_in[j], ssd_conv_w[j], ssd_conv_b[j],
                            ssd_a_log_fwd[j], ssd_a_log_bwd[j],
                            ssd_dt_bias_fwd[j], ssd_dt_bias_bwd[j],
                            ssd_d[j], ssd_norm_g[j], ssd_w_out[j])
        h = layer_norm(DEEPNORM_ALPHA * h + mix, ln_tok_g[i], ln_tok_b[i])
        ffn = conv_ffn(h, ff_w_up[i], ff_conv_w[i], ff_conv_b[i], ff_w_down[i])
        h = layer_norm(DEEPNORM_ALPHA * h + ffn, ln_ffn_g[i], ln_ffn_b[i])
    return h
```

```python
import math
from contextlib import ExitStack

import numpy as np
import ml_dtypes

import concourse.bass as bass
import concourse.mybir as mybir
from concourse.bass_utils import run_bass_kernel_spmd

F32 = mybir.dt.float32
BF16 = mybir.dt.bfloat16
AF = mybir.ActivationFunctionType
ALU = mybir.AluOpType

S = 4096
D = 1024
NT = S // 128
DEPTH = 2
ALPHA = (2.0 * DEPTH) ** 0.25
EPS = 1e-5
D_FF = 2816
NFC = D_FF // 128
D_INNER = 2048
CONV_DIM = 4096
D_IN_PROJ = 6208

SAME_ENGINE_SYNC = True
HOIST_WINDOW = 96


class Buf:
    __slots__ = ("name", "last_w", "readers")

    def __init__(self, name="b"):
        self.name = name
        self.last_w = None
        self.readers = {}


class Op:
    __slots__ = ("eng", "fn", "deps", "signal", "sigval", "sem", "is_dma", "idx", "key", "sp_pos", "sp_need",
                 "is_load", "fence")

    def __init__(self, eng, fn, is_dma, key):
        self.eng = eng
        self.fn = fn
        self.deps = []
        self.signal = False
        self.sigval = None
        self.sem = None
        self.is_dma = is_dma
        self.key = key
        self.sp_pos = -1
        self.sp_need = -1
        self.is_load = False
        self.fence = False


class Sched:
    ENGS = ("pe", "act", "dve", "pool", "sp")

    def __init__(self):
        self.ops = {e: [] for e in self.ENGS}
        self.dma_keys = {}
        self.key_slot = {}
        self.slots = []
        self.free = []
        self.n = 0
        self.fence_pos = -1

    def barrier(self):
        deps = []
        for e in self.ENGS:
            for op in reversed(self.ops[e]):
                if op.fn is not None and not op.is_dma:
                    op.signal = True
                    deps.append(op)
                    break
        for sl in self.slots:
            if sl["last_op"] is not None:
                deps.append(sl["last_op"])
        for e in self.ENGS:
            w = Op(e, None, False, None)
            w.idx = self.n
            self.n += 1
            w.deps = list(deps)
            if e == "sp":
                w.sp_pos = len(self.ops[e])
                w.fence = True
                self.fence_pos = w.sp_pos
            self.ops[e].append(w)
        self.key_slot = {}
        self.free = list(range(len(self.slots)))

    def add(self, eng, fn, reads=(), writes=(), dma=False, key=None, is_load=False):
        op = Op(eng, fn, dma, key)
        op.idx = self.n
        self.n += 1
        deps = {}
        for b in reads:
            if b.last_w is not None:
                deps[id(b.last_w)] = b.last_w
        for b in writes:
            if b.last_w is not None:
                deps[id(b.last_w)] = b.last_w
            for r in b.readers.values():
                deps[id(r)] = r
        if dma:
            assert key is not None
            prev = self.dma_keys.get(key)
            if prev is not None:
                deps[id(prev)] = prev
            self.dma_keys[key] = op
            si = self.key_slot.get(key)
            if si is None:
                if self.free:
                    si = self.free.pop(0)
                else:
                    si = len(self.slots)
                    self.slots.append({"count": 0, "last_op": None, "sem": None})
                self.key_slot[key] = si
            sl = self.slots[si]
            sl["count"] += 1
            sl["last_op"] = op
            op.key = sl
            op.sigval = 16 * sl["count"]
            op.signal = True
        deps.pop(id(op), None)
        for d in deps.values():
            if d.is_dma:
                d.signal = True
            elif d.eng != eng or SAME_ENGINE_SYNC or dma:
                if d.eng == eng and not dma and eng == "pe":
                    continue
                d.signal = True
            else:
                continue
            op.deps.append(d)
        need = self.fence_pos
        for d in deps.values():
            v = d.sp_pos if d.eng == "sp" else d.sp_need
            if v > need:
                need = v
        op.sp_need = need
        op.is_load = is_load
        if eng == "sp":
            op.sp_pos = len(self.ops[eng])
            if fn is None:
                op.fence = True
        for b in reads:
            k = ("dma", id(op)) if dma else eng
            b.readers[k] = op
        for b in writes:
            b.last_w = op
            b.readers = {}
        self.ops[eng].append(op)
        return op

    def emit(self, nc, es):
        eng_sem = {}
        for e in self.ENGS:
            eng_sem[e] = es.enter_context(nc.semaphore("s_" + e))
        if HOIST_WINDOW > 0:
            newl = []
            nh = 0
            for op in self.ops["sp"]:
                if not op.is_load:
                    newl.append(op)
                    continue
                j = len(newl)
                steps = 0
                while j > 0 and steps < HOIST_WINDOW:
                    e = newl[j - 1]
                    if e.is_load or e.fence or e.sp_pos <= op.sp_need:
                        break
                    j -= 1
                    steps += 1
                if j < len(newl):
                    nh += 1
                newl.insert(j, op)
            self.ops["sp"] = newl
            print("sched: hoisted %d loads" % nh)
        for i, sl in enumerate(self.slots):
            sl["sem"] = es.enter_context(nc.semaphore("d_%d" % i))
        print("sched: %d dma sem slots, ops per engine: %s" % (len(self.slots), {e: len(v) for e, v in self.ops.items()}))
        for e in self.ENGS:
            c = 0
            for op in self.ops[e]:
                if op.is_dma:
                    op.sem = op.key["sem"]
                    continue
                if op.signal:
                    c += 1
                    op.sigval = c
                    op.sem = eng_sem[e]
        block = es.enter_context(nc.Block())
        sched = self

        def run(engname):
            def body(eng):
                waited = {}
                for op in sched.ops[engname]:
                    need = {}
                    for d in op.deps:
                        sid = id(d.sem)
                        if need.get(sid, (None, 0))[1] < d.sigval:
                            need[sid] = (d.sem, d.sigval)
                    for sid, (sem, val) in need.items():
                        if waited.get(sid, 0) < val:
                            eng.wait_ge(sem, val)
                            waited[sid] = val
                    if op.fn is None:
                        continue
                    ins = op.fn(eng)
                    if op.signal:
                        ins.then_inc(op.sem, 16 if op.is_dma else 1)
            return body

        block.tensor(run("pe"))
        block.scalar(run("act"))
        block.vector(run("dve"))
        block.gpsimd(run("pool"))
        block.sync(run("sp"))


def rr(lst, i):
    return lst[i % len(lst)]


ARENA_W = 206 * 256


class K:
    def __init__(self, nc, es):
        self.nc = nc
        self.es = es
        self.s = Sched()
        self.arena = es.enter_context(nc.sbuf_tensor("arena", [128, ARENA_W], F32))
        self.top = 0
        self.psum = es.enter_context(nc.psum_tensor("psum", [128, 4096], F32))
        self.pb = [Buf("ps%d" % i) for i in range(8)]

    def alloc(self, free_shape, dtype):
        esz = 4 if dtype == F32 else 2
        n = 1
        for d in free_shape:
            n *= d
        nbytes = (n * esz + 63) // 64 * 64
        off = self.top
        self.top += nbytes
        assert self.top <= ARENA_W * 4, "SBUF arena overflow %d" % self.top
        ap = self.arena[:, off // 4: off // 4 + nbytes // 4]
        if dtype == BF16:
            ap = ap.bitcast(BF16)
        ap = ap[:, 0:n]
        if len(free_shape) == 2:
            ap = ap.rearrange("p (a b) -> p a b", a=free_shape[0])
        elif len(free_shape) == 3:
            ap = ap.rearrange("p (a b c) -> p a b c", a=free_shape[0], b=free_shape[1])
        return ap, Buf()

    def bank(self, i, n=1, dtype=F32):
        ap = self.psum[:, i * 512:(i + n) * 512]
        if dtype == BF16:
            ap = ap.bitcast(BF16)
        return ap

    def barrier(self):
        self.s.barrier()

    def dma(self, out, in_, reads, writes, key, eng="sp"):
        is_load = type(out.tensor).__name__.startswith("SB") and not type(in_.tensor).__name__.startswith("SB")
        return self.s.add(eng, lambda e: e.dma_start(out=out, in_=in_), reads, writes, dma=True, key=key,
                          is_load=is_load)

    def mm(self, out, lhsT, rhs, start, stop, reads, writes):
        return self.s.add("pe", lambda e: e.matmul(out, lhsT, rhs, start=start, stop=stop), reads, writes)

    def tr(self, out, in_, ident, reads, writes):
        return self.s.add("pe", lambda e: e.transpose(out, in_, ident), reads, writes)

    def act(self, out, in_, func, reads, writes, bias=0.0, scale=1.0):
        return self.s.add("act", lambda e: e.activation(out, in_, func, bias=bias, scale=scale), reads, writes)

    def tt(self, eng, out, in0, in1, op, reads, writes):
        return self.s.add(eng, lambda e: e.tensor_tensor(out, in0, in1, op), reads, writes)

    def ts(self, eng, out, in0, s1, s2, op0, op1, reads, writes):
        if s2 is None:
            return self.s.add(eng, lambda e: e.tensor_scalar(out, in0, s1, None, op0), reads, writes)
        return self.s.add(eng, lambda e: e.tensor_scalar(out, in0, s1, s2, op0, op1), reads, writes)

    def stt(self, out, in0, scalar, in1, op0, op1, reads, writes):
        return self.s.add("dve", lambda e: e.scalar_tensor_tensor(out, in0, scalar, in1, op0, op1), reads, writes)

    def copy(self, eng, out, in_, reads, writes):
        if eng == "act":
            return self.s.add("act", lambda e: e.copy(out, in_), reads, writes)
        return self.s.add(eng, lambda e: e.tensor_copy(out, in_), reads, writes)

    def memset(self, eng, ap, val, writes):
        return self.s.add(eng, lambda e: e.memset(ap, val), (), writes)

    def load_cast(self, dst, dstb, src, stg, eng="pool"):
        sa, sbuf_ = stg
        self.dma(sa, src, (), (sbuf_,), key=sbuf_)
        self.copy(eng, dst, sa, (sbuf_,), (dstb,))


class LNRes:
    def __init__(self, k, ident):
        self.k = k
        self.ident = ident
        NB = 3
        self.stats = [k.alloc([2, 6], F32) for i in range(NB)]
        self.mv = [k.alloc([2], F32) for i in range(NB)]
        self.rstd = [k.alloc([1], F32) for i in range(NB)]
        self.t1 = [k.alloc([D], F32) for i in range(2)]
        self.h = [k.alloc([D], F32) for i in range(2)]
        self.hb = [k.alloc([D], BF16) for i in range(NB)]
        self.hT = [k.alloc([8, 128], BF16) for i in range(2)]
        self.g = k.alloc([D], F32)
        self.b = k.alloc([D], F32)
        self.cnt = 0
        self.pb_ = []
        self.pc_ = []

    def load_gb(self, g_dram, b_dram):
        k = self.k
        self.flush()
        k.dma(self.g[0], g_dram.partition_broadcast(128), (), (self.g[1],), key=self.g[1])
        k.dma(self.b[0], b_dram.partition_broadcast(128), (), (self.b[1],), key=self.b[1])

    def _a(self, j):
        k = self.k
        i, r, rbuf = j["i"], j["r"], j["rbuf"]
        st, stb = rr(self.stats, i)
        mv, mvb = rr(self.mv, i)
        rs, rsb = rr(self.rstd, i)
        for q in range(2):
            k.s.add("dve", lambda e, q=q: e.bn_stats(st[:, q, :], r[:, q * 512:(q + 1) * 512]), (rbuf,), (stb,))
        k.s.add("dve", lambda e: e.bn_aggr(mv, st), (stb,), (mvb,))
        k.act(rs, mv[:, 1:2], AF.Sqrt, (mvb,), (rsb,), bias=EPS)
        k.s.add("dve", lambda e: e.reciprocal(rs, rs), (rsb,), (rsb,))

    def _b(self, j):
        k = self.k
        i, r, rbuf = j["i"], j["r"], j["rbuf"]
        mv, mvb = rr(self.mv, i)
        rs, rsb = rr(self.rstd, i)
        t1, t1b = rr(self.t1, i)
        h, hb_ = rr(self.h, i)
        hb, hbb = rr(self.hb, i)
        k.stt(t1, r, mv[:, 0:1], self.g[0], ALU.subtract, ALU.mult, (rbuf, mvb, self.g[1]), (t1b,))
        k.stt(h, t1, rs[:, 0:1], self.b[0], ALU.mult, ALU.add, (t1b, rsb, self.b[1]), (hb_,))
        k.dma(j["H"], h, (hb_,), (j["hbuf"],), key=hb_)
        k.copy("act", hb, h, (hb_,), (hbb,))

    def _c(self, j):
        k = self.k
        i = j["i"]
        hb, hbb = rr(self.hb, i)
        hT, hTb = rr(self.hT, i)
        pst = k.bank(j["bank"], 1, BF16)
        pstb = k.pb[j["bank"]]
        for c in range(8):
            k.tr(pst[:, c * 128:(c + 1) * 128], hb[:, c * 128:(c + 1) * 128], self.ident[0],
                 (hbb, self.ident[1]), (pstb,))
        k.copy("act", hT.rearrange("p c t -> p (c t)"), pst, (pstb,), (hTb,))
        k.dma(j["HT"], hT, (hTb,), (j["htbuf"],), key=hTb)

    def apply(self, r, rbuf, H_dram_tile, HT_dram_tile, tp_bank, hbuf_dram, htbuf_dram):
        j = dict(i=self.cnt, r=r, rbuf=rbuf, H=H_dram_tile, HT=HT_dram_tile, bank=tp_bank, hbuf=hbuf_dram,
                 htbuf=htbuf_dram)
        self.cnt += 1
        self._a(j)
        if self.pc_:
            self._c(self.pc_.pop(0))
        if self.pb_:
            jb = self.pb_.pop(0)
            self._b(jb)
            self.pc_.append(jb)
        self.pb_.append(j)

    def flush(self):
        while self.pb_ or self.pc_:
            if self.pc_:
                self._c(self.pc_.pop(0))
            if self.pb_:
                jb = self.pb_.pop(0)
                self._b(jb)
                self.pc_.append(jb)


def build(nc, n_stages=99, debug=False, only=None):
    es = ExitStack()
    k = K(nc, es)

    def din(name, shape, dt=F32):
        return nc.dram_tensor(name, shape, dt, kind="ExternalInput").ap()

    def scratch(name, shape, dt):
        if debug:
            return nc.dram_tensor("dbg_" + name, shape, dt, kind="ExternalOutput").ap()
        return nc.dram_tensor(name, shape, dt).ap()

    x = din("x", [S, D])
    emb_g = din("emb_ln_g", [D]); emb_b = din("emb_ln_b", [D])
    fn_w_in = din("fn_w_in", [D, D]); fn_b_in = din("fn_b_in", [1, D])
    fn_w_out = din("fn_w_out", [D, D]); fn_b_out = din("fn_b_out", [1, D])
    ln_tok_g = din("ln_tok_g", [2, D]); ln_tok_b = din("ln_tok_b", [2, D])
    ff_w_up = din("ff_w_up", [2, D, 2 * D_FF]); ff_cw = din("ff_cwp", [2, 128, 176])
    ff_w_down = din("ff_w_down", [2, D_FF, D])
    ln_ffn_g = din("ln_ffn_g", [2, D]); ln_ffn_b = din("ln_ffn_b", [2, D])
    ssd_w_in = din("ssd_w_in", [D, D_IN_PROJ]); ssd_cwp = din("ssd_cwp", [128, 192])
    ssd_vec = din("ssd_vec", [160]); ssd_norm_g = din("ssd_norm_g", [2048]); ssd_w_out = din("ssd_w_out", [2048, D])
    tri_d = din("c_tri", [128, 512], F32)
    ident_d = din("c_ident", [128, 128], BF16)
    identf_d = din("c_identf", [128, 128], F32)
    cs128_d = din("c_cs128", [128, 256], BF16)
    ma_d = din("c_ma", [128, 64 * 128], BF16)
    bb_d = din("c_bb", [128, 64], BF16)
    y = nc.dram_tensor("y", [S, D], F32, kind="ExternalOutput").ap()

    H = [scratch("H%d" % i, [S, D], F32) for i in range(5)]
    HT = [nc.dram_tensor("HT%d" % i, [128, 8, S], BF16).ap() for i in range(5)]
    Hb = [Buf("b") for _ in range(5)]
    HTb = [Buf("b") for _ in range(5)]
    Gs = nc.dram_tensor("Gs", [64, 64, 2, 1024], BF16).ap()
    Gsb = Buf("b")
    Ys = nc.dram_tensor("Ys", [64, 128, 1024], BF16).ap()
    Ysb = Buf("b")
    Xs = nc.dram_tensor("Xs", [S, 2048], BF16).ap(); Xsb = Buf()
    Bs = nc.dram_tensor("Bs", [S, 1024], BF16).ap(); Bsb = Buf()
    BTs = nc.dram_tensor("BTs", [128, 8, S], BF16).ap(); BTsb = Buf()
    CTs = nc.dram_tensor("CTs", [128, 8, S], BF16).ap(); CTsb = Buf()
    Zs = nc.dram_tensor("Zs", [S, 2048], F32).ap(); Zsb = Buf()
    DTs = nc.dram_tensor("DTs", [S, 5, 64], F32).ap(); DTsb = Buf()
    ACs = nc.dram_tensor("ACs", [NT, 2, 32, 128], F32).ap(); ACsb = Buf()
    Yf = scratch("Yf", [S, 2048], F32); Yfb = Buf()
    Yb = scratch("Yb", [S, 2048], F32); Ybb = Buf()

    ident = k.alloc([128], BF16)
    k.dma(ident[0], ident_d, (), (ident[1],), key=ident[1])
    identf = k.alloc([128], F32)
    k.dma(identf[0], identf_d, (), (identf[1],), key=identf[1])
    ones = k.alloc([512], BF16)
    k.memset("pool", ones[0], 1.0, (ones[1],))
    ln = LNRes(k, ident)
    persist_top = k.top
    stage = [0]

    def next_stage():
        ln.flush()
        k.barrier()
        k.top = persist_top
        stage[0] += 1
        if only is not None:
            return stage[0] in only
        return stage[0] <= n_stages

    WbUp = [nc.dram_tensor("WbUp%d" % i, [44, 128, 1024], BF16).ap() for i in range(2)]
    WbUpb = [Buf(), Buf()]
    WbIn = nc.dram_tensor("WbIn", [32, 128, 1024], BF16).ap()
    WbInb = Buf()

    def prologue(which):
        stg = [k.alloc([8, 512], F32) for _ in range(3)]
        outb = [k.alloc([8, 512], BF16) for _ in range(3)]
        jobs = []
        if 0 in which:
            wv0 = ff_w_up[0].rearrange("(kc p) m -> p kc m", p=128)
            for pc in range(11):
                jobs.append((wv0[:, :, pc * 512:(pc + 1) * 512], WbUp[0][pc * 4:(pc + 1) * 4], WbUpb[0]))
        if 1 in which:
            wvi = ssd_w_in.rearrange("(kc p) m -> p kc m", p=128)
            for pc in range(8):
                jobs.append((wvi[:, :, 2048 + pc * 512:2048 + (pc + 1) * 512], WbIn[pc * 4:(pc + 1) * 4], WbInb))
            wv1 = ff_w_up[1].rearrange("(kc p) m -> p kc m", p=128)
            for pc in range(11):
                jobs.append((wv1[:, :, pc * 512:(pc + 1) * 512], WbUp[1][pc * 4:(pc + 1) * 4], WbUpb[1]))
        def job(i, src_, dst_, dbuf):
            s_, sb_ = rr(stg, i)
            o_, ob_ = rr(outb, i)
            k.dma(s_, src_, (), (sb_,), key=sb_)
            k.copy("act" if i % 2 == 0 else "dve", o_, s_, (sb_,), (ob_,))
            for c4 in range(4):
                k.dma(dst_[c4].rearrange("p (k m) -> p k m", k=8), o_[:, :, c4 * 128:(c4 + 1) * 128],
                      (ob_,), (dbuf,), key=(id(ob_), c4))
        return [(lambda i=i, jb=jb: job(i, *jb)) for i, jb in enumerate(jobs)]

    pro_jobs = prologue((0,)) if (only is None or 0 in only) else []
    ln.load_gb(emb_g, emb_b)
    xin = [k.alloc([D], F32) for i in range(3)]
    for t in range(NT if (only is None or 0 in only) else 0):
        xt, xb = rr(xin, t)
        k.dma(xt, x[t * 128:(t + 1) * 128, :], (), (xb,), key=xb)
        ln.apply(xt, xb, H[0][t * 128:(t + 1) * 128, :], HT[0][:, :, t * 128:(t + 1) * 128], t % 2, Hb[0], HTb[0])
        if pro_jobs:
            pro_jobs.pop(0)()
    while pro_jobs:
        pro_jobs.pop(0)()
    cur = 0

    if next_stage():
        hT = k.alloc([8, S], BF16)
        for kc in range(8):
            k.dma(hT[0][:, kc, :], HT[0][:, kc, :], (HTb[0],), (hT[1],), key=("hTl", kc % 4))
        stg = k.alloc([4, 1024], F32)
        w = k.alloc([8, 1024], BF16)
        wv = fn_w_in.rearrange("(kc p) m -> p kc m", p=128)
        for hf in range(2):
            k.load_cast(w[0][:, hf * 4:(hf + 1) * 4, :], w[1], wv[:, hf * 4:(hf + 1) * 4, :], stg)
        brow_f = k.alloc([1024], F32)
        brow = k.alloc([1024], BF16)
        k.dma(brow_f[0][0:1, :], fn_b_in, (), (brow_f[1],), key=brow_f[1])
        k.copy("pool", brow[0][0:1, :], brow_f[0][0:1, :], (brow_f[1],), (brow[1],))
        cs = k.alloc([256], BF16)
        k.dma(cs[0], cs128_d, (), (cs[1],), key=cs[1])
        h1 = [k.alloc([8, 512], BF16) for _ in range(2)]
        Gt = [k.alloc([2, 8, 128], BF16) for _ in range(2)]
        gi = 0
        for tb in range(8):
            h1a, h1b = rr(h1, tb)
            for g in range(8):
                bk = g % 2
                ps = k.bank(bk)
                for kc in range(8):
                    k.mm(ps, w[0][:, kc, g * 128:(g + 1) * 128], hT[0][:, kc, tb * 512:(tb + 1) * 512],
                         kc == 0, False, (w[1], hT[1]), (k.pb[bk],))
                k.mm(ps, brow[0][0:1, g * 128:(g + 1) * 128], ones[0][0:1, :], False, True,
                     (brow[1], ones[1]), (k.pb[bk],))
                k.copy("act" if g % 2 == 0 else "dve", h1a[:, g, :], ps, (k.pb[bk],), (h1b,))
            for tt_ in range(4):
                t = tb * 4 + tt_
                gt, gtb = rr(Gt, gi)
                for hf in range(2):
                    b0 = 2 + 2 * (gi % 2) if False else 2 + 2 * ((2 * gi + hf) % 3)
                    psg = k.bank(b0, 2)
                    for gg in range(4):
                        g = hf * 4 + gg
                        k.mm(psg[:, gg * 256:(gg + 1) * 256], h1a[:, g, tt_ * 128:(tt_ + 1) * 128], cs[0],
                             True, True, (h1b, cs[1]), (k.pb[b0], k.pb[b0 + 1]))
                    psv = psg.rearrange("p (g r j) -> p g r j", g=4, r=2)
                    for ri in range(2):
                        k.copy("act" if ri == 0 else "dve", gt[:, ri, hf * 4:(hf + 1) * 4, :], psv[:, :, ri, :],
                               (k.pb[b0], k.pb[b0 + 1]), (gtb,))
                k.dma(Gs[2 * t:2 * t + 2].rearrange("a b r c -> (a b) r c"), gt.rearrange("p r g j -> p r (g j)"),
                      (gtb,), (Gsb,), key=gtb)
                gi += 1

    if next_stage():
        ma = k.alloc([64, 128], BF16)
        k.dma(ma[0].rearrange("p a b -> p (a b)"), ma_d, (), (ma[1],), key=ma[1])
        NZ = 4
        Z = [k.alloc([1024], BF16) for _ in range(NZ)]
        Yt = [k.alloc([1024], BF16) for _ in range(NZ)]
        late_jobs = prologue((1,))
        for t2 in range(64):
            if t2 % 3 == 1 and late_jobs:
                late_jobs.pop(0)()
            z, zb = rr(Z, t2)
            yt, ytb = rr(Yt, t2)
            for ri in range(2):
                k.dma(z[ri * 64:(ri + 1) * 64, :], Gs[:, t2, ri, :], (Gsb,), (zb,), key=(id(zb), ri))
            for hf in range(2):
                bk = (2 * t2 + hf) % 8
                ps = k.bank(bk)
                k.mm(ps, ma[0][:, t2, :], z[:, hf * 512:(hf + 1) * 512], True, True, (ma[1], zb), (k.pb[bk],))
                k.copy("act" if hf == 0 else "dve", yt[:, hf * 512:(hf + 1) * 512], ps, (k.pb[bk],), (ytb,))
            k.dma(Ys[t2], yt, (ytb,), (Ysb,), key=ytb)
        while late_jobs:
            late_jobs.pop(0)()

    if next_stage():
        bb = k.alloc([64], BF16)
        k.dma(bb[0], bb_d, (), (bb[1],), key=bb[1])
        fT = k.alloc([8, S], BF16)
        stg = k.alloc([4, 1024], F32)
        w = k.alloc([8, 1024], BF16)
        wv = fn_w_out.rearrange("(kc p) m -> p kc m", p=128)
        for hf in range(2):
            k.load_cast(w[0][:, hf * 4:(hf + 1) * 4, :], w[1], wv[:, hf * 4:(hf + 1) * 4, :], stg)
        brow_f = k.alloc([1024], F32)
        brow = k.alloc([1024], BF16)
        k.dma(brow_f[0][0:1, :], fn_b_out, (), (brow_f[1],), key=brow_f[1])
        k.copy("pool", brow[0][0:1, :], brow_f[0][0:1, :], (brow_f[1],), (brow[1],))
        ln.load_gb(ln_tok_g[0], ln_tok_b[0])
        NZ = 4
        YY = [k.alloc([1024], BF16) for _ in range(NZ)]
        for k1 in range(64):
            yy, yyb = rr(YY, k1)
            for ro in range(2):
                k.dma(yy[ro * 64:(ro + 1) * 64, :], Ys[:, ro * 64 + k1, :], (Ysb,), (yyb,), key=(id(yyb), ro))
            bk = k1 % 4
            ps = k.bank(bk)
            for cc in range(8):
                k.mm(ps[:, cc * 64:(cc + 1) * 64], yy[:, cc * 128:(cc + 1) * 128], bb[0], True, True,
                     (yyb, bb[1]), (k.pb[bk],))
            k.copy("act" if k1 % 2 == 0 else "dve", fT[0][:, :, k1::64], ps.rearrange("p (c k) -> p c k", c=8),
                   (k.pb[bk],), (fT[1],))
        h0t = [k.alloc([D], F32) for _ in range(2)]
        rt = [k.alloc([D], F32) for _ in range(2)]
        for t in range(NT):
            ht, htb = rr(h0t, t)
            r, rb = rr(rt, t)
            k.dma(ht, H[0][t * 128:(t + 1) * 128, :], (Hb[0],), (htb,), key=htb)
            b0 = 4 + 2 * (t % 2)
            ps = k.bank(b0, 2)
            for hf in range(2):
                for cc in range(8):
                    k.mm(ps[:, hf * 512:(hf + 1) * 512], fT[0][:, cc, t * 128:(t + 1) * 128],
                         w[0][:, cc, hf * 512:(hf + 1) * 512], cc == 0, False, (fT[1], w[1]), (k.pb[b0 + hf],))
                k.mm(ps[:, hf * 512:(hf + 1) * 512], ones[0][0:1, 0:128], brow[0][0:1, hf * 512:(hf + 1) * 512],
                     False, True, (ones[1], brow[1]), (k.pb[b0 + hf],))
            k.stt(r, ht, ALPHA, ps, ALU.mult, ALU.add, (htb, k.pb[b0], k.pb[b0 + 1]), (rb,))
            ln.apply(r, rb, H[1][t * 128:(t + 1) * 128, :], HT[1][:, :, t * 128:(t + 1) * 128], t % 2, Hb[1], HTb[1])
        cur = 1

    def ffn_stage(layer, src, dst):
        TB = 512
        cw = k.alloc([44, 4], F32)
        k.dma(cw[0].rearrange("p c k -> p (c k)"), ff_cw[layer], (), (cw[1],), key=cw[1])
        wd = k.alloc([NFC, D], BF16)
        stg = [k.alloc([2, D], F32) for _ in range(2)]
        wdv = ff_w_down[layer].rearrange("(fc p) m -> p fc m", p=128)
        for i in range(NFC // 2):
            k.load_cast(wd[0][:, 2 * i:2 * i + 2, :], wd[1], wdv[:, 2 * i:2 * i + 2, :], rr(stg, i),
                        eng="act" if i % 2 == 0 else "dve")
        ln.load_gb(ln_ffn_g[layer], ln_ffn_b[layer])
        hblk = [k.alloc([8, TB + 2], BF16) for _ in range(2)]
        aT = [k.alloc([NFC, TB], BF16) for _ in range(1)]
        wbf = [k.alloc([2, 8, 128], BF16) for _ in range(4)]
        acc = [[k.alloc([TB], F32) for _ in range(3)] for _ in range(2)]
        pend_gate = []
        sg = [k.alloc([TB], F32) for _ in range(2)]
        h0t = [k.alloc([D], F32) for _ in range(3)]
        rt = [k.alloc([D], F32) for _ in range(3)]
        it = 0
        pend_ln = []

        def flush_ln():
            while pend_ln:
                pend_ln.pop(0)()

        for sbk in range(S // TB):
            T0 = sbk * TB
            hb, hbb = rr(hblk, sbk)
            lo = max(T0 - 1, 0); hi = min(T0 + TB + 1, S)
            if sbk == 0:
                k.memset("pool", hb[:, :, 0:1], 0.0, (hbb,))
            if sbk == S // TB - 1:
                k.memset("pool", hb[:, :, TB + 1:TB + 2], 0.0, (hbb,))
            for kc in range(8):
                k.dma(hb[:, kc, lo - (T0 - 1):hi - (T0 - 1)], HT[src][:, kc, lo:hi], (HTb[src],), (hbb,),
                      key=(id(hbb), kc % 4))
            a, ab = rr(aT, sbk)
            for fc in range(NFC):
                wb, wbb = rr(wbf, it)
                for gv in range(2):
                    k.dma(wb[:, gv], WbUp[layer][gv * NFC + fc].rearrange("p (k m) -> p k m", k=8),
                          (WbUpb[layer],), (wbb,), key=(id(wbb), gv))
                accs = []
                for gv in range(2):
                    b0 = 4 * (it % 2) + 2 * gv
                    ps = k.bank(b0, 2)
                    for (c0, c1, bk) in ((0, 512, b0), (512, 514, b0 + 1)):
                        for kc in range(8):
                            k.mm(ps[:, c0:c1], wb[:, gv, kc, :], hb[:, kc, c0:c1],
                                 kc == 0, kc == 7, (wbb, hbb), (k.pb[bk],))
                    ac, acb = acc[gv][it % 3]
                    ch = gv * NFC + fc
                    pbs = (k.pb[b0], k.pb[b0 + 1])
                    k.act(ac, ps[:, 1:TB + 1], AF.Identity, pbs + (cw[1],), (acb,),
                          bias=cw[0][:, ch, 3:4], scale=cw[0][:, ch, 1:2])
                    if gv == 1:
                        while pend_gate:
                            pend_gate.pop(0)()
                    k.stt(ac, ps[:, 0:TB], cw[0][:, ch, 0:1], ac, ALU.mult, ALU.add, pbs + (cw[1], acb), (acb,))
                    k.stt(ac, ps[:, 2:TB + 2], cw[0][:, ch, 2:3], ac, ALU.mult, ALU.add, pbs + (cw[1], acb), (acb,))
                    accs.append((ac, acb))
                def gate(accs=accs, it=it, fc=fc, a=a, ab=ab):
                    s_, sb_ = rr(sg, it)
                    k.act(s_, accs[0][0], AF.Silu, (accs[0][1],), (sb_,))
                    k.tt("pool", a[:, fc, :], s_, accs[1][0], ALU.mult, (sb_, accs[1][1]), (ab,))
                pend_gate.append(gate)
                it += 1
                if fc == 1:
                    flush_ln()
            while pend_gate:
                pend_gate.pop(0)()
            for tt_ in range(TB // 128):
                t = sbk * (TB // 128) + tt_
                ht, htb = rr(h0t, t)
                r, rb = rr(rt, t)
                k.dma(ht, H[src][t * 128:(t + 1) * 128, :], (Hb[src],), (htb,), key=htb)
                b0 = 2 * (tt_ % 2) + (0 if (it % 2 == 0) else 4)
                ps = k.bank(b0, 2)
                for hf in range(2):
                    for fc in range(NFC):
                        k.mm(ps[:, hf * 512:(hf + 1) * 512], a[:, fc, tt_ * 128:(tt_ + 1) * 128],
                             wd[0][:, fc, hf * 512:(hf + 1) * 512], fc == 0, fc == NFC - 1, (ab, wd[1]),
                             (k.pb[b0 + hf],))
                k.stt(r, ht, ALPHA, ps, ALU.mult, ALU.add, (htb, k.pb[b0], k.pb[b0 + 1]), (rb,))
                flush_ln()
                tpb = (b0 + 4) % 8
                pend_ln.append(lambda r=r, rb=rb, t=t, tpb=tpb: ln.apply(
                    r, rb, H[dst][t * 128:(t + 1) * 128, :], HT[dst][:, :, t * 128:(t + 1) * 128], tpb,
                    Hb[dst], HTb[dst]))
        flush_ln()

    if next_stage():
        ffn_stage(0, 1, 2)
        cur = 2

    def bc(ap, dims, off=0):
        return bass.AP(ap.tensor, ap.offset + off, [list(ap.ap[0])] + [list(d) for d in dims])

    NSL = 5

    def ssd_s1(src):
        TB = 512
        wv = ssd_w_in.rearrange("(kc p) m -> p kc m", p=128)
        tri = k.alloc([4, 128], F32)
        k.dma(tri[0].rearrange("p a b -> p (a b)"), tri_d, (), (tri[1],), key=tri[1])
        tri16 = k.alloc([4, 128], BF16)
        k.copy("dve", tri16[0], tri[0], (tri[1],), (tri16[1],))
        vec = k.alloc([160], F32)
        k.dma(vec[0], ssd_vec.partition_broadcast(128), (), (vec[1],), key=vec[1])
        abc_ = k.alloc([64], F32)
        k.act(abc_[0], vec[0][:, 64:128], AF.Exp, (vec[1],), (abc_[1],))
        k.ts("dve", abc_[0], abc_[0], -1.0, None, ALU.mult, None, (abc_[1],), (abc_[1],))
        cwp = k.alloc([32, 6], F32)
        k.dma(cwp[0].rearrange("p c k -> p (c k)"), ssd_cwp, (), (cwp[1],), key=cwp[1])
        wz = k.alloc([8, 2048], BF16)
        wdt = k.alloc([8, 64], BF16)
        stg = [k.alloc([8, 512], F32) for _ in range(1)]
        uS = [k.alloc([TB + 4], F32) for _ in range(3)]
        for q in range(4):
            k.load_cast(wz[0][:, :, q * 512:(q + 1) * 512], wz[1], wv[:, :, q * 512:(q + 1) * 512], rr(stg, q),
                        eng="act" if q % 2 == 0 else "dve")
        k.load_cast(wdt[0], wdt[1], wv[:, :, 6144:6208], (stg[0][0][:, :, 0:64], stg[0][1]), eng="dve")
        hblk = [k.alloc([8, TB + 4], BF16) for _ in range(2)]
        wbf = [k.alloc([8, 128], BF16) for _ in range(4)]
        acc = [k.alloc([TB], F32) for _ in range(3)]
        xo = [k.alloc([TB], BF16) for _ in range(3)]
        Xblk = [k.alloc([4, 2048], BF16) for _ in range(1)]
        Bblk = [k.alloc([4, 1024], BF16) for _ in range(1)]
        zt = [k.alloc([2048], F32) for _ in range(2)]
        sm = k.alloc([4, 8, 64], F32)
        csb = k.alloc([4, 4, 32], F32)
        hl = k.alloc([2, 4, 64], BF16)
        DT = k.alloc([4, NSL, 64], F32)
        actT = k.alloc([1024], F32)
        it = 0
        for tb in range(S // TB):
            T0 = tb * TB
            hb, hbb = rr(hblk, tb)
            lo = max(T0 - 2, 0); hi = min(T0 + TB + 2, S)
            if tb == 0:
                k.memset("pool", hb[:, :, 0:2], 0.0, (hbb,))
            if tb == S // TB - 1:
                k.memset("pool", hb[:, :, TB + 2:TB + 4], 0.0, (hbb,))
            for kc in range(8):
                k.dma(hb[:, kc, lo - (T0 - 2):hi - (T0 - 2)], HT[src][:, kc, lo:hi], (HTb[src],), (hbb,),
                      key=(id(hbb), kc % 4))
            Xb_, Xbb = rr(Xblk, tb)
            Bb_, Bbb = rr(Bblk, tb)
            pend = []
            for ch in range(32):
                wb, wbb = rr(wbf, it)
                k.dma(wb, WbIn[ch].rearrange("p (k m) -> p k m", k=8), (WbInb,), (wbb,), key=wbb)
                b0 = (0, 2, 5)[it % 3]
                ps = k.bank(b0, 2)
                for (c0, c1, bk) in ((0, 512, b0), (512, 516, b0 + 1)):
                    for kc in range(8):
                        k.mm(ps[:, c0:c1], wb[:, kc, :], hb[:, kc, c0:c1], kc == 0, kc == 7, (wbb, hbb), (k.pb[bk],))
                ac, acb = rr(acc, it)
                us_, usb = rr(uS, it)
                pbs = (k.pb[b0], k.pb[b0 + 1])
                k.copy("act", us_, ps[:, 0:TB + 4], pbs, (usb,))
                k.act(ac, us_[:, 2:TB + 2], AF.Identity, (usb, cwp[1]), (acb,),
                      bias=cwp[0][:, ch, 5:6], scale=cwp[0][:, ch, 2:3])
                while pend:
                    pend.pop(0)()
                for tap in (0, 1, 3, 4):
                    k.stt(ac, us_[:, tap:TB + tap], cwp[0][:, ch, tap:tap + 1], ac, ALU.mult, ALU.add,
                          (usb, cwp[1], acb), (acb,))
                o, ob = rr(xo, it)

                def tail(ch=ch, o=o, ob=ob, T0=T0, ac=ac, acb=acb):
                    k.act(o, ac, AF.Silu, (acb,), (ob,))
                    if ch < 24:
                        pst = k.bank(4, 1, BF16)
                        for tt_ in range(4):
                            k.tr(pst[:, tt_ * 128:(tt_ + 1) * 128], o[:, tt_ * 128:(tt_ + 1) * 128], ident[0],
                                 (ob, ident[1]), (k.pb[4],))
                        if ch < 16:
                            k.copy("act", Xb_[:, :, ch * 128:(ch + 1) * 128],
                                   pst[:, 0:512].rearrange("p (t c) -> p t c", t=4), (k.pb[4],), (Xbb,))
                        else:
                            g = ch - 16
                            k.copy("act", Bb_[:, :, g * 128:(g + 1) * 128],
                                   pst[:, 0:512].rearrange("p (t c) -> p t c", t=4), (k.pb[4],), (Bbb,))
                    if 16 <= ch < 24:
                        k.dma(BTs[:, ch - 16, T0:T0 + TB], o, (ob,), (BTsb,), key=ob)
                    if ch >= 24:
                        k.dma(CTs[:, ch - 24, T0:T0 + TB], o, (ob,), (CTsb,), key=ob)
                pend.append(tail)
                it += 1
            for tt_ in range(4):
                t = tb * 4 + tt_
                tok = slice(2 + tt_ * 128, 2 + (tt_ + 1) * 128)
                z_, zb_ = rr(zt, t)
                for q in range(4):
                    bk = 1 + 2 * (q % 2)
                    ps = k.bank(bk)
                    for kc in range(8):
                        k.mm(ps, hb[:, kc, tok], wz[0][:, kc, q * 512:(q + 1) * 512], kc == 0, kc == 7,
                             (hbb, wz[1]), (k.pb[bk],))
                    if tt_ == 0 and q == 0:
                        while pend:
                            pend.pop(0)()
                        k.dma(Xs[T0:T0 + TB, :].rearrange("(t p) c -> p t c", p=128), Xb_, (Xbb,), (Xsb,), key=Xbb)
                        k.dma(Bs[T0:T0 + TB, :].rearrange("(t p) c -> p t c", p=128), Bb_, (Bbb,), (Bsb,), key=Bbb)
                    k.act(z_[:, q * 512:(q + 1) * 512], ps, AF.Silu, (k.pb[bk],), (zb_,))
                k.dma(Zs[t * 128:(t + 1) * 128, :], z_, (zb_,), (Zsb,), key=zb_)
            p7 = k.bank(7)
            pb7 = k.pb[7]
            for tt_ in range(4):
                tok = slice(2 + tt_ * 128, 2 + (tt_ + 1) * 128)
                for kc in range(8):
                    k.mm(p7[:, tt_ * 64:(tt_ + 1) * 64], hb[:, kc, tok], wdt[0][:, kc, :], kc == 0, kc == 7,
                         (hbb, wdt[1]), (pb7,))
            s_, sb_ = sm
            d1 = s_[:, :, 0, :]; dtv = s_[:, :, 1, :]; lndt = s_[:, :, 2, :]; dA = s_[:, :, 3, :]
            tmp = s_[:, :, 4, :]; tmp2 = s_[:, :, 5, :]
            k.tt("dve", d1, p7[:, 0:256].rearrange("p (t c) -> p t c", t=4), bc(vec[0][:, 0:64], [(0, 4), (1, 64)]),
                 ALU.add, (pb7, vec[1]), (sb_,))
            k.act(d1, d1, AF.Exp, (sb_,), (sb_,))
            k.act(dtv, d1, AF.Ln, (sb_,), (sb_,), bias=1.0)
            k.act(lndt, dtv, AF.Ln, (sb_,), (sb_,))
            k.tt("dve", dA, dtv, bc(abc_[0], [(0, 4), (1, 64)]), ALU.mult, (sb_, abc_[1]), (sb_,))
            k.copy("dve", hl[0][:, 0], dA, (sb_,), (hl[1],))
            k.tt("dve", hl[0][:, 1], dA, hl[0][:, 0], ALU.subtract, (sb_, hl[1]), (hl[1],))
            pcs = k.bank(1)
            pT = k.bank(2, 2)
            for tt_ in range(4):
                c0 = tt_ * 128
                for (dst_c, lhs_sel, col_sel) in ((0, 0, slice(0, 32)), (32, 1, slice(32, 64))):
                    for pi in range(2):
                        k.mm(pcs[:, c0 + dst_c:c0 + dst_c + 32], tri16[0][:, lhs_sel, :], hl[0][:, pi, tt_, col_sel],
                             pi == 0, pi == 1, (tri16[1], hl[1]), (k.pb[1],))
                for pi in range(2):
                    k.mm(pcs[:, c0 + 64:c0 + 128], ones[0][:, 0:128], hl[0][:, pi, tt_, :], pi == 0, pi == 1,
                         (ones[1], hl[1]), (k.pb[1],))
                t0c = tt_ * 256
                for (dst_c, col_sel, rsel) in ((0, slice(0, 32), 0), (128, slice(32, 64), 2)):
                    for pi in range(2):
                        k.mm(pT[0:32, t0c + dst_c:t0c + dst_c + 128], hl[0][:, pi, tt_, col_sel], tri16[0][:, rsel, :],
                             pi == 0, pi == 1, (tri16[1], hl[1]), (k.pb[2 + tt_ // 2],))
            k.copy("act", csb[0].rearrange("p t q c -> p (t q c)"), pcs, (k.pb[1],), (csb[1],))
            k.copy("act", actT[0][0:32, :], pT[0:32, :], (k.pb[2], k.pb[3]), (actT[1],))
            k.dma(ACs[tb * 4:(tb + 1) * 4].rearrange("t d h l -> h t d l"),
                  actT[0][0:32, :].rearrange("p (t d l) -> p t d l", t=4, d=2), (actT[1],), (ACsb,), key=actT[1])
            cs_ = csb[0]
            acf = cs_[:, :, 0, :]; ecb = cs_[:, :, 1, :]; totf = cs_[:, :, 2, :]; totb = cs_[:, :, 3, :]
            dt_, dtb_ = DT
            cr = (csb[1],)
            k.copy("dve", dt_[:, :, 0, 0:32], acf, cr, (dtb_,))
            k.ts("dve", dt_[:, :, 0, 32:64], ecb, -1.0, None, ALU.mult, None, cr, (dtb_,))
            k.tt("dve", tmp[:, :, 0:32], totf, acf, ALU.subtract, cr, (sb_,))
            k.tt("dve", tmp[:, :, 0:32], tmp[:, :, 0:32], lndt[:, :, 0:32], ALU.add, (sb_,), (sb_,))
            k.tt("dve", tmp[:, :, 32:64], ecb, lndt[:, :, 32:64], ALU.add, cr + (sb_,), (sb_,))
            k.act(dt_[:, :, 1, :], tmp, AF.Exp, (sb_,), (dtb_,))
            k.copy("dve", tmp2[:, :, 0:32], acf, cr, (sb_,))
            k.tt("dve", tmp2[:, :, 32:64], totb, ecb, ALU.subtract, cr, (sb_,))
            k.act(dt_[:, :, 2, :], tmp2, AF.Exp, (sb_,), (dtb_,))
            k.act(dt_[:, :, 3, :], cs_[:, :, 2:4, :], AF.Exp, cr, (dtb_,))
            k.copy("dve", dt_[:, :, 4, :], dtv, (sb_,), (dtb_,))
            k.dma(DTs[T0:T0 + TB].rearrange("(t p) s c -> p t s c", p=128), dt_, (dtb_,), (DTsb,), key=dtb_)

    def ssd_scan(d):
        tri = k.alloc([4, 128], F32)
        k.dma(tri[0].rearrange("p a b -> p (a b)"), tri_d, (), (tri[1],), key=tri[1])
        mask = tri[0][:, 0, :] if d == 0 else tri[0][:, 3, :]
        Hf = k.alloc([2048], F32)
        Hbf = k.alloc([2048], BF16)
        NB = 3
        Xt = [k.alloc([2048], BF16) for _ in range(NB)]
        BTt = [k.alloc([8, 128], BF16) for _ in range(NB)]
        CTt = [k.alloc([8, 128], BF16) for _ in range(NB)]
        Bt = [k.alloc([1024], BF16) for _ in range(NB)]
        DTt = [k.alloc([NSL, 64], F32) for _ in range(NB)]
        Abc = [k.alloc([32, 128], F32) for _ in range(2)]
        Abc2 = [Buf(), Buf()]
        Eb = [k.alloc([32, 128], BF16) for _ in range(2)]
        Mt = [k.alloc([32, 128], BF16) for _ in range(3)]
        cbm = [k.alloc([8, 128], BF16) for _ in range(2)]
        yo = k.alloc([2048], F32)
        yv = [k.alloc([2048], F32) for _ in range(2)]
        Xw = [k.alloc([2048], BF16) for _ in range(2)]
        Xd = [k.alloc([2048], BF16) for _ in range(2)]
        order = list(range(NT)) if d == 0 else list(range(NT - 1, -1, -1))
        Yout = Yf if d == 0 else Yb
        Youtb = Yfb if d == 0 else Ybb
        def P1(i):
            c = order[i]
            last = (i == NT - 1)
            tok = slice(c * 128, (c + 1) * 128)
            X, Xb = rr(Xt, i); BT, BTb = rr(BTt, i); CT, CTb = rr(CTt, i); B_, Bb = rr(Bt, i)
            DTc, DTb_ = rr(DTt, i); A_, Ab = rr(Abc, i)
            k.dma(A_.rearrange("p h l -> p (h l)"),
                  ACs[c, d].rearrange("h l -> (h l)").partition_broadcast(128), (ACsb,), (Ab, rr(Abc2, i)), key=Ab)
            k.dma(DTc, DTs[tok], (DTsb,), (DTb_,), key=DTb_)
            k.dma(BT, BTs[:, :, tok], (BTsb,), (BTb,), key=BTb)
            k.dma(CT, CTs[:, :, tok], (CTsb,), (CTb,), key=CTb)
            k.dma(X, Xs[tok, :], (Xsb,), (Xb,), key=Xb)
            k.dma(B_, Bs[tok, :], (Bsb,), (Bb,), key=Bb)
            dsl = slice(d * 32, (d + 1) * 32)
            p45 = k.bank(4, 2)
            for g in range(8):
                k.mm(p45[:, g * 128:(g + 1) * 128], BT[:, g, :], CT[:, g, :], True, True, (BTb, CTb),
                     (k.pb[4 + g // 4],))
            Ab2 = rr(Abc2, i)
            k.tt("pool", A_[:, 0:18, :], bc(DTc[:, 0, d * 32:d * 32 + 18], [(1, 18), (0, 128)]), A_[:, 0:18, :],
                 ALU.subtract, (Ab, DTb_), (Ab,))
            k.tt("dve", A_[:, 18:32, :], bc(DTc[:, 0, d * 32 + 18:d * 32 + 32], [(1, 14), (0, 128)]), A_[:, 18:32, :],
                 ALU.subtract, (Ab2, DTb_), (Ab2,))
            k.act(A_, A_, AF.Relu, (Ab, Ab2), (Ab, Ab2))
            E_, Ebb = rr(Eb, i)
            k.act(E_, A_, AF.Exp, (Ab, Ab2), (Ebb,), scale=-1.0)
            cb_, cbb = rr(cbm, i)
            k.tt("dve", cb_, p45.rearrange("p (g l) -> p g l", g=8), bc(mask, [(0, 8), (1, 128)]), ALU.mult,
                 (k.pb[4], k.pb[5], tri[1]), (cbb,))
            M_, Mb = rr(Mt, i)
            k.tt("dve", M_.rearrange("p (g r) l -> p g r l", g=8), E_.rearrange("p (g r) l -> p g r l", g=8),
                 bc(cb_[:, 0, :], [(128, 8), (0, 4), (1, 128)]), ALU.mult, (Ebb, cbb), (Mb,))
            xd_, xdb = rr(Xd, i)
            k.tt("pool", xd_.rearrange("p (h q) -> p h q", h=32), X.rearrange("p (h q) -> p h q", h=32),
                 bc(DTc[:, 4, dsl], [(1, 32), (0, 64)]), ALU.mult, (Xb, DTb_), (xdb,))
            xw_, xwb = rr(Xw, i)
            if not last:
                k.tt("pool", xw_.rearrange("p (h q) -> p h q", h=32), X.rearrange("p (h q) -> p h q", h=32),
                     bc(DTc[:, 1, dsl], [(1, 32), (0, 64)]), ALU.mult, (Xb, DTb_), (xwb,))

        def P2(i):
            c = order[i]
            first = (i == 0)
            last = (i == NT - 1)
            tok = slice(c * 128, (c + 1) * 128)
            CT, CTb = rr(CTt, i); B_, Bb = rr(Bt, i)
            DTc, DTb_ = rr(DTt, i)
            M_, Mb = rr(Mt, i)
            xd_, xdb = rr(Xd, i)
            xw_, xwb = rr(Xw, i)
            dsl = slice(d * 32, (d + 1) * 32)
            p03 = k.bank(0, 4)
            pb03 = tuple(k.pb[0:4])
            if not first:
                for g in range(8):
                    k.mm(p03[:, g * 256:(g + 1) * 256], CT[:, g, :], Hbf[0][:, g * 256:(g + 1) * 256], True, True,
                         (CTb, Hbf[1]), (k.pb[g // 2],))
                k.tt("dve", yo[0].rearrange("p (h q) -> p h q", h=32), p03.rearrange("p (h q) -> p h q", h=32),
                     bc(DTc[:, 2, dsl], [(1, 32), (0, 64)]), ALU.mult, pb03 + (DTb_,), (yo[1],))

        def P2b(i):
            c = order[i]
            first = (i == 0)
            last = (i == NT - 1)
            tok = slice(c * 128, (c + 1) * 128)
            CT, CTb = rr(CTt, i); B_, Bb = rr(Bt, i)
            DTc, DTb_ = rr(DTt, i)
            M_, Mb = rr(Mt, i)
            xd_, xdb = rr(Xd, i)
            xw_, xwb = rr(Xw, i)
            dsl = slice(d * 32, (d + 1) * 32)
            p03 = k.bank(0, 4)
            pb03 = tuple(k.pb[0:4])
            for h in range(32):
                k.mm(p03[:, h * 64:(h + 1) * 64], M_[:, h, :], xd_[:, h * 64:(h + 1) * 64], True, True, (Mb, xdb),
                     (k.pb[h // 8],))
            y_, yb_ = rr(yv, i)
            if first:
                k.copy("act", y_, p03, pb03, (yb_,))
            else:
                k.tt("dve", y_, p03, yo[0], ALU.add, pb03 + (yo[1],), (yb_,))
            k.dma(Yout[tok, :], y_, (yb_,), (Youtb,), key=yb_)
            if not last:
                p47 = k.bank(4, 4)
                pb47 = tuple(k.pb[4:8])
                for g in range(8):
                    k.mm(p47[:, g * 256:(g + 1) * 256], B_[:, g * 128:(g + 1) * 128], xw_[:, g * 256:(g + 1) * 256],
                         True, True, (Bb, xwb), (k.pb[4 + g // 2],))
                if first:
                    k.copy("act", Hf[0], p47, pb47, (Hf[1],))
                else:
                    k.tt("pool", Hf[0].rearrange("p (h q) -> p h q", h=32), Hf[0].rearrange("p (h q) -> p h q", h=32),
                         bc(DTc[:, 3, dsl], [(1, 32), (0, 64)]), ALU.mult, (Hf[1], DTb_), (Hf[1],))
                    k.tt("dve", Hf[0], Hf[0], p47, ALU.add, pb47 + (Hf[1],), (Hf[1],))
                k.copy("act", Hbf[0], Hf[0], (Hf[1],), (Hbf[1],))

        P1(0)
        for i in range(NT):
            P2(i)
            if i + 1 < NT:
                P1(i + 1)
            P2b(i)

    def ssd_s3(src, dst, layer):
        vec = k.alloc([160], F32)
        k.dma(vec[0], ssd_vec.partition_broadcast(128), (), (vec[1],), key=vec[1])
        ng = k.alloc([2048], F32)
        k.dma(ng[0], ssd_norm_g.partition_broadcast(128), (), (ng[1],), key=ng[1])
        wo = k.alloc([16, D], BF16)
        stg = [k.alloc([2, D], F32) for _ in range(2)]
        wov = ssd_w_out.rearrange("(kc p) m -> p kc m", p=128)
        for i in range(8):
            k.load_cast(wo[0][:, 2 * i:2 * i + 2, :], wo[1], wov[:, 2 * i:2 * i + 2, :], rr(stg, i),
                        eng="act" if i % 2 == 0 else "dve")
        ln.load_gb(ln_tok_g[layer], ln_tok_b[layer])
        NB = 2
        Xt = [k.alloc([2048], BF16) for _ in range(NB)]
        Yt = [k.alloc([2048], F32) for _ in range(NB)]
        Y2t = [k.alloc([2048], F32) for _ in range(NB)]
        Zt = [k.alloc([2048], F32) for _ in range(NB)]
        t1 = k.alloc([2048], F32)
        ss = [k.alloc([8], F32) for _ in range(2)]
        ynb = [k.alloc([2048], BF16) for _ in range(3)]
        yT = [k.alloc([16, 128], BF16) for _ in range(2)]
        h0t = [k.alloc([D], F32) for _ in range(2)]
        rt = [k.alloc([D], F32) for _ in range(3)]

        def phaseA(t):
            tok = slice(t * 128, (t + 1) * 128)
            X, Xb = rr(Xt, t); Y, Yb_ = rr(Yt, t); Y2, Y2b = rr(Y2t, t); Z, Zb = rr(Zt, t)
            k.dma(X, Xs[tok, :], (Xsb,), (Xb,), key=Xb)
            k.dma(Y, Yf[tok, :], (Yfb,), (Yb_,), key=Yb_)
            k.dma(Y2, Yb[tok, :], (Ybb,), (Y2b,), key=Y2b)
            k.dma(Z, Zs[tok, :], (Zsb,), (Zb,), key=Zb)
            k.tt("pool", t1[0].rearrange("p (h q) -> p h q", h=32), X.rearrange("p (h q) -> p h q", h=32),
                 bc(vec[0][:, 128:160], [(1, 32), (0, 64)]), ALU.mult, (Xb, vec[1]), (t1[1],))
            k.tt("dve", Y, Y, Y2, ALU.add, (Yb_, Y2b), (Yb_,))
            k.tt("dve", Y, Y, t1[0], ALU.add, (Yb_, t1[1]), (Yb_,))
            k.tt("pool", Y, Y, Z, ALU.mult, (Yb_, Zb), (Yb_,))
            k.tt("dve", t1[0], Y, Y, ALU.mult, (Yb_,), (t1[1],))
            s_, sb_ = rr(ss, t)
            k.s.add("dve", lambda e, s_=s_: e.tensor_reduce(s_, t1[0].rearrange("p (g q) -> p g q", g=8),
                                                            mybir.AxisListType.X, ALU.add), (t1[1],), (sb_,))
            k.act(s_, s_, AF.Sqrt, (sb_,), (sb_,), bias=EPS, scale=1.0 / 256.0)
            k.s.add("dve", lambda e, s_=s_: e.reciprocal(s_, s_), (sb_,), (sb_,))
            k.tt("dve", Y.rearrange("p (g q) -> p g q", g=8), Y.rearrange("p (g q) -> p g q", g=8),
                 bc(s_, [(1, 8), (0, 256)]), ALU.mult, (Yb_, sb_), (Yb_,))
            yb16, yb16b = rr(ynb, t)
            k.tt("pool", yb16, Y, ng[0], ALU.mult, (Yb_, ng[1]), (yb16b,))

        def phaseB(t):
            tok = slice(t * 128, (t + 1) * 128)
            yb16, yb16b = rr(ynb, t)
            b0 = 2 * (t % 2)
            pst = k.bank(b0, 2, BF16)
            for c in range(16):
                k.tr(pst[:, c * 128:(c + 1) * 128], yb16[:, c * 128:(c + 1) * 128], ident[0], (yb16b, ident[1]),
                     (k.pb[b0 + c // 8],))
            yT_, yTb = rr(yT, t)
            k.copy("act", yT_.rearrange("p c t -> p (c t)"), pst, (k.pb[b0], k.pb[b0 + 1]), (yTb,))
            ht, htb = rr(h0t, t)
            r, rb = rr(rt, t)
            k.dma(ht, H[src][tok, :], (Hb[src],), (htb,), key=htb)
            b1 = 4 + 2 * (t % 2)
            ps = k.bank(b1, 2)
            for hf in range(2):
                for kc in range(16):
                    k.mm(ps[:, hf * 512:(hf + 1) * 512], yT_[:, kc, :], wo[0][:, kc, hf * 512:(hf + 1) * 512],
                         kc == 0, kc == 15, (yTb, wo[1]), (k.pb[b1 + hf],))
            k.stt(r, ht, ALPHA, ps, ALU.mult, ALU.add, (htb, k.pb[b1], k.pb[b1 + 1]), (rb,))

        def phaseC(t):
            tok = slice(t * 128, (t + 1) * 128)
            r, rb = rr(rt, t)
            ln.apply(r, rb, H[dst][tok, :], HT[dst][:, :, tok], 2 * (t % 2), Hb[dst], HTb[dst])

        for step in range(NT + 2):
            if step < NT:
                phaseA(step)
            if 0 <= step - 1 < NT:
                phaseB(step - 1)
            if 0 <= step - 2 < NT:
                phaseC(step - 2)

    if next_stage():
        ssd_s1(2)
    if next_stage():
        ssd_scan(0)
    if next_stage():
        ssd_scan(1)
    if next_stage():
        ssd_s3(2, 3, 1)
        cur = 3
    if next_stage():
        ffn_stage(1, 3, 4)
        cur = 4

    ln.flush()
    k.barrier()
    k.top = persist_top
    ft, fb = k.alloc([8, D], F32)
    for q in range(S // 1024):
        k.dma(ft, H[cur][q * 1024:(q + 1) * 1024, :].rearrange("(p a) d -> p a d", a=8), (Hb[cur],), (fb,), key=fb)
        k.dma(y[q * 1024:(q + 1) * 1024, :].rearrange("(p a) d -> p a d", a=8), ft, (fb,), (), key=fb)
    k.s.add("sp", None, (fb,), (fb,))

    k.s.emit(nc, es)
    es.close()
    return nc


def make_consts():
    c = {}
    bf = ml_dtypes.bfloat16
    c["c_ident"] = np.eye(128, dtype=np.float32).astype(bf)
    c["c_identf"] = np.eye(128, dtype=np.float32)
    j = np.arange(128, dtype=np.float64)
    ang = 2 * np.pi * np.outer(j, j) / 128.0
    sc = 1.0 / math.sqrt(128.0 * S)
    c["c_cs128"] = np.concatenate([np.cos(ang) * sc, -np.sin(ang) * sc], axis=1).astype(np.float32).astype(bf)
    t1 = np.arange(64, dtype=np.float64)[:, None, None]
    t2 = np.arange(64, dtype=np.float64)[None, :, None]
    k1 = np.arange(64, dtype=np.float64)[None, None, :]
    th = -2 * np.pi * (t2 * k1 / 4096.0 + t1 * k1 / 64.0)
    Pr = np.cos(th); Pi = np.sin(th)
    ma = np.zeros((2, 64, 64, 2, 64), dtype=np.float64)
    ma[0, :, :, 0, :] = Pr
    ma[1, :, :, 0, :] = -Pi
    ma[0, :, :, 1, :] = Pi
    ma[1, :, :, 1, :] = Pr
    c["c_ma"] = ma.reshape(128, 64 * 128).astype(np.float32).astype(bf)
    a = 2 * np.pi * np.outer(np.arange(64.0), np.arange(64.0)) / 64.0
    c["c_bb"] = np.concatenate([np.cos(a), np.sin(a)], axis=0).astype(np.float32).astype(bf)
    r = np.arange(128)[:, None]; cl = np.arange(128)[None, :]
    tri = np.stack([(cl >= r), (cl > r), -(cl > r).astype(np.float32), (cl <= r)], axis=1).astype(np.float32)
    c["c_tri"] = tri.reshape(128, 512)
    return c


def prep_inputs(inputs):
    f = lambda a: np.ascontiguousarray(a, dtype=np.float32)
    shared = {}
    shared["emb_ln_g"] = f(inputs["emb_ln_g"]); shared["emb_ln_b"] = f(inputs["emb_ln_b"])
    shared["fn_w_in"] = f(inputs["fn_w_in"][0]); shared["fn_b_in"] = f(inputs["fn_b_in"][0:1])
    shared["fn_w_out"] = f(inputs["fn_w_out"][0]); shared["fn_b_out"] = f(inputs["fn_b_out"][0:1])
    shared["ln_tok_g"] = f(inputs["ln_tok_g"]); shared["ln_tok_b"] = f(inputs["ln_tok_b"])
    shared["ff_w_up"] = f(inputs["ff_w_up"])
    cw4 = np.concatenate([inputs["ff_conv_w"], inputs["ff_conv_b"][:, None, :]], axis=1)
    shared["ff_cwp"] = f(cw4.reshape(cw4.shape[0], 4, 44, 128).transpose(0, 3, 2, 1).reshape(cw4.shape[0], 128, 176))
    shared["ff_w_down"] = f(inputs["ff_w_down"])
    shared["ln_ffn_g"] = f(inputs["ln_ffn_g"]); shared["ln_ffn_b"] = f(inputs["ln_ffn_b"])
    shared["ssd_w_in"] = f(inputs["ssd_w_in"][0])
    cw6 = np.concatenate([inputs["ssd_conv_w"][0], inputs["ssd_conv_b"][0][None, :]], axis=0)
    shared["ssd_cwp"] = f(cw6.reshape(6, 32, 128).transpose(2, 1, 0).reshape(128, 192))
    shared["ssd_vec"] = f(np.concatenate([inputs["ssd_dt_bias_fwd"][0], inputs["ssd_dt_bias_bwd"][0],
                                          inputs["ssd_a_log_fwd"][0], inputs["ssd_a_log_bwd"][0], inputs["ssd_d"][0]]))
    shared["ssd_norm_g"] = f(inputs["ssd_norm_g"][0]); shared["ssd_w_out"] = f(inputs["ssd_w_out"][0])
    shared.update(make_consts())
    x = f(inputs["x"])
    return [dict(shared, x=x[b]) for b in range(x.shape[0])]


def run(inputs, n_stages=99, debug=False):
    nc = bass.Bass("TRN2", target_bir_lowering=False)
    build(nc, n_stages=n_stages, debug=debug)
    in_maps = prep_inputs(inputs)
    res = run_bass_kernel_spmd(nc, in_maps, core_ids=list(range(len(in_maps))))
    return res.results


def kernel(**inputs):
    results = run(inputs)
    out = np.stack([np.asarray(r["y"]) for r in results], axis=0)
    return out.astype(np.float32)
```

```python
import math
from contextlib import ExitStack

import numpy as np
import ml_dtypes

import concourse.bass as bass
import concourse.mybir as mybir
from concourse.bass_utils import run_bass_kernel_spmd

F32 = mybir.dt.float32
BF16 = mybir.dt.bfloat16
AF = mybir.ActivationFunctionType
ALU = mybir.AluOpType

S = 4096
D = 1024
NT = S // 128
DEPTH = 2
ALPHA = (2.0 * DEPTH) ** 0.25
EPS = 1e-5
D_FF = 2816
NFC = D_FF // 128
D_INNER = 2048
CONV_DIM = 4096
D_IN_PROJ = 6208

SAME_ENGINE_SYNC = True
HOIST_WINDOW = 96


class Buf:
    __slots__ = ("name", "last_w", "readers")

    def __init__(self, name="b"):
        self.name = name
        self.last_w = None
        self.readers = {}


class Op:
    __slots__ = ("eng", "fn", "deps", "signal", "sigval", "sem", "is_dma", "idx", "key", "sp_pos", "sp_need",
                 "is_load", "fence")

    def __init__(self, eng, fn, is_dma, key):
        self.eng = eng
        self.fn = fn
        self.deps = []
        self.signal = False
        self.sigval = None
        self.sem = None
        self.is_dma = is_dma
        self.key = key
        self.sp_pos = -1
        self.sp_need = -1
        self.is_load = False
        self.fence = False


class Sched:
    ENGS = ("pe", "act", "dve", "pool", "sp")

    def __init__(self):
        self.ops = {e: [] for e in self.ENGS}
        self.dma_keys = {}
        self.key_slot = {}
        self.slots = []
        self.free = []
        self.n = 0
        self.fence_pos = -1

    def barrier(self):
        deps = []
        for e in self.ENGS:
            for op in reversed(self.ops[e]):
                if op.fn is not None and not op.is_dma:
                    op.signal = True
                    deps.append(op)
                    break
        for sl in self.slots:
            if sl["last_op"] is not None:
                deps.append(sl["last_op"])
        for e in self.ENGS:
            w = Op(e, None, False, None)
            w.idx = self.n
            self.n += 1
            w.deps = list(deps)
            if e == "sp":
                w.sp_pos = len(self.ops[e])
                w.fence = True
                self.fence_pos = w.sp_pos
            self.ops[e].append(w)
        self.key_slot = {}
        self.free = list(range(len(self.slots)))

    def add(self, eng, fn, reads=(), writes=(), dma=False, key=None, is_load=False):
        op = Op(eng, fn, dma, key)
        op.idx = self.n
        self.n += 1
        deps = {}
        for b in reads:
            if b.last_w is not None:
                deps[id(b.last_w)] = b.last_w
        for b in writes:
            if b.last_w is not None:
                deps[id(b.last_w)] = b.last_w
            for r in b.readers.values():
                deps[id(r)] = r
        if dma:
            assert key is not None
            prev = self.dma_keys.get(key)
            if prev is not None:
                deps[id(prev)] = prev
            self.dma_keys[key] = op
            si = self.key_slot.get(key)
            if si is None:
                if self.free:
                    si = self.free.pop(0)
                else:
                    si = len(self.slots)
                    self.slots.append({"count": 0, "last_op": None, "sem": None})
                self.key_slot[key] = si
            sl = self.slots[si]
            sl["count"] += 1
            sl["last_op"] = op
            op.key = sl
            op.sigval = 16 * sl["count"]
            op.signal = True
        deps.pop(id(op), None)
        for d in deps.values():
            if d.is_dma:
                d.signal = True
            elif d.eng != eng or SAME_ENGINE_SYNC or dma:
                if d.eng == eng and not dma and eng == "pe":
                    continue
                d.signal = True
            else:
                continue
            op.deps.append(d)
        need = self.fence_pos
        for d in deps.values():
            v = d.sp_pos if d.eng == "sp" else d.sp_need
            if v > need:
                need = v
        op.sp_need = need
        op.is_load = is_load
        if eng == "sp":
            op.sp_pos = len(self.ops[eng])
            if fn is None:
                op.fence = True
        for b in reads:
            k = ("dma", id(op)) if dma else eng
            b.readers[k] = op
        for b in writes:
            b.last_w = op
            b.readers = {}
        self.ops[eng].append(op)
        return op

    def emit(self, nc, es):
        eng_sem = {}
        for e in self.ENGS:
            eng_sem[e] = es.enter_context(nc.semaphore("s_" + e))
        if HOIST_WINDOW > 0:
            newl = []
            nh = 0
            for op in self.ops["sp"]:
                if not op.is_load:
                    newl.append(op)
                    continue
                j = len(newl)
                steps = 0
                while j > 0 and steps < HOIST_WINDOW:
                    e = newl[j - 1]
                    if e.is_load or e.fence or e.sp_pos <= op.sp_need:
                        break
                    j -= 1
                    steps += 1
                if j < len(newl):
                    nh += 1
                newl.insert(j, op)
            self.ops["sp"] = newl
            print("sched: hoisted %d loads" % nh)
        for i, sl in enumerate(self.slots):
            sl["sem"] = es.enter_context(nc.semaphore("d_%d" % i))
        print("sched: %d dma sem slots, ops per engine: %s" % (len(self.slots), {e: len(v) for e, v in self.ops.items()}))
        for e in self.ENGS:
            c = 0
            for op in self.ops[e]:
                if op.is_dma:
                    op.sem = op.key["sem"]
                    continue
                if op.signal:
                    c += 1
                    op.sigval = c
                    op.sem = eng_sem[e]
        block = es.enter_context(nc.Block())
        sched = self

        def run(engname):
            def body(eng):
                waited = {}
                for op in sched.ops[engname]:
                    need = {}
                    for d in op.deps:
                        sid = id(d.sem)
                        if need.get(sid, (None, 0))[1] < d.sigval:
                            need[sid] = (d.sem, d.sigval)
                    for sid, (sem, val) in need.items():
                        if waited.get(sid, 0) < val:
                            eng.wait_ge(sem, val)
                            waited[sid] = val
                    if op.fn is None:
                        continue
                    ins = op.fn(eng)
                    if op.signal:
                        ins.then_inc(op.sem, 16 if op.is_dma else 1)
            return body

        block.tensor(run("pe"))
        block.scalar(run("act"))
        block.vector(run("dve"))
        block.gpsimd(run("pool"))
        block.sync(run("sp"))


def rr(lst, i):
    return lst[i % len(lst)]


ARENA_W = 206 * 256


class K:
    def __init__(self, nc, es):
        self.nc = nc
        self.es = es
        self.s = Sched()
        self.arena = es.enter_context(nc.sbuf_tensor("arena", [128, ARENA_W], F32))
        self.top = 0
        self.psum = es.enter_context(nc.psum_tensor("psum", [128, 4096], F32))
        self.pb = [Buf("ps%d" % i) for i in range(8)]

    def alloc(self, free_shape, dtype):
        esz = 4 if dtype == F32 else 2
        n = 1
        for d in free_shape:
            n *= d
        nbytes = (n * esz + 63) // 64 * 64
        off = self.top
        self.top += nbytes
        assert self.top <= ARENA_W * 4, "SBUF arena overflow %d" % self.top
        ap = self.arena[:, off // 4: off // 4 + nbytes // 4]
        if dtype == BF16:
            ap = ap.bitcast(BF16)
        ap = ap[:, 0:n]
        if len(free_shape) == 2:
            ap = ap.rearrange("p (a b) -> p a b", a=free_shape[0])
        elif len(free_shape) == 3:
            ap = ap.rearrange("p (a b c) -> p a b c", a=free_shape[0], b=free_shape[1])
        return ap, Buf()

    def bank(self, i, n=1, dtype=F32):
        ap = self.psum[:, i * 512:(i + n) * 512]
        if dtype == BF16:
            ap = ap.bitcast(BF16)
        return ap

    def barrier(self):
        self.s.barrier()

    def dma(self, out, in_, reads, writes, key, eng="sp"):
        is_load = type(out.tensor).__name__.startswith("SB") and not type(in_.tensor).__name__.startswith("SB")
        return self.s.add(eng, lambda e: e.dma_start(out=out, in_=in_), reads, writes, dma=True, key=key,
                          is_load=is_load)

    def mm(self, out, lhsT, rhs, start, stop, reads, writes):
        return self.s.add("pe", lambda e: e.matmul(out, lhsT, rhs, start=start, stop=stop), reads, writes)

    def tr(self, out, in_, ident, reads, writes):
        return self.s.add("pe", lambda e: e.transpose(out, in_, ident), reads, writes)

    def act(self, out, in_, func, reads, writes, bias=0.0, scale=1.0):
        return self.s.add("act", lambda e: e.activation(out, in_, func, bias=bias, scale=scale), reads, writes)

    def tt(self, eng, out, in0, in1, op, reads, writes):
        return self.s.add(eng, lambda e: e.tensor_tensor(out, in0, in1, op), reads, writes)

    def ts(self, eng, out, in0, s1, s2, op0, op1, reads, writes):
        if s2 is None:
            return self.s.add(eng, lambda e: e.tensor_scalar(out, in0, s1, None, op0), reads, writes)
        return self.s.add(eng, lambda e: e.tensor_scalar(out, in0, s1, s2, op0, op1), reads, writes)

    def stt(self, out, in0, scalar, in1, op0, op1, reads, writes):
        return self.s.add("dve", lambda e: e.scalar_tensor_tensor(out, in0, scalar, in1, op0, op1), reads, writes)

    def copy(self, eng, out, in_, reads, writes):
        if eng == "act":
            return self.s.add("act", lambda e: e.copy(out, in_), reads, writes)
        return self.s.add(eng, lambda e: e.tensor_copy(out, in_), reads, writes)

    def memset(self, eng, ap, val, writes):
        return self.s.add(eng, lambda e: e.memset(ap, val), (), writes)

    def load_cast(self, dst, dstb, src, stg, eng="pool"):
        sa, sbuf_ = stg
        self.dma(sa, src, (), (sbuf_,), key=sbuf_)
        self.copy(eng, dst, sa, (sbuf_,), (dstb,))


class LNRes:
    def __init__(self, k, ident):
        self.k = k
        self.ident = ident
        NB = 3
        self.stats = [k.alloc([2, 6], F32) for i in range(NB)]
        self.mv = [k.alloc([2], F32) for i in range(NB)]
        self.rstd = [k.alloc([1], F32) for i in range(NB)]
        self.t1 = [k.alloc([D], F32) for i in range(2)]
        self.h = [k.alloc([D], F32) for i in range(2)]
        self.hb = [k.alloc([D], BF16) for i in range(NB)]
        self.hT = [k.alloc([8, 128], BF16) for i in range(2)]
        self.g = k.alloc([D], F32)
        self.b = k.alloc([D], F32)
        self.cnt = 0
        self.pb_ = []
        self.pc_ = []

    def load_gb(self, g_dram, b_dram):
        k = self.k
        self.flush()
        k.dma(self.g[0], g_dram.partition_broadcast(128), (), (self.g[1],), key=self.g[1])
        k.dma(self.b[0], b_dram.partition_broadcast(128), (), (self.b[1],), key=self.b[1])

    def _a(self, j):
        k = self.k
        i, r, rbuf = j["i"], j["r"], j["rbuf"]
        st, stb = rr(self.stats, i)
        mv, mvb = rr(self.mv, i)
        rs, rsb = rr(self.rstd, i)
        for q in range(2):
            k.s.add("dve", lambda e, q=q: e.bn_stats(st[:, q, :], r[:, q * 512:(q + 1) * 512]), (rbuf,), (stb,))
        k.s.add("dve", lambda e: e.bn_aggr(mv, st), (stb,), (mvb,))
        k.act(rs, mv[:, 1:2], AF.Sqrt, (mvb,), (rsb,), bias=EPS)
        k.s.add("dve", lambda e: e.reciprocal(rs, rs), (rsb,), (rsb,))

    def _b(self, j):
        k = self.k
        i, r, rbuf = j["i"], j["r"], j["rbuf"]
        mv, mvb = rr(self.mv, i)
        rs, rsb = rr(self.rstd, i)
        t1, t1b = rr(self.t1, i)
        h, hb_ = rr(self.h, i)
        hb, hbb = rr(self.hb, i)
        k.stt(t1, r, mv[:, 0:1], self.g[0], ALU.subtract, ALU.mult, (rbuf, mvb, self.g[1]), (t1b,))
        k.stt(h, t1, rs[:, 0:1], self.b[0], ALU.mult, ALU.add, (t1b, rsb, self.b[1]), (hb_,))
        k.dma(j["H"], h, (hb_,), (j["hbuf"],), key=hb_)
        k.copy("act", hb, h, (hb_,), (hbb,))

    def _c(self, j):
        k = self.k
        i = j["i"]
        hb, hbb = rr(self.hb, i)
        hT, hTb = rr(self.hT, i)
        pst = k.bank(j["bank"], 1, BF16)
        pstb = k.pb[j["bank"]]
        for c in range(8):
            k.tr(pst[:, c * 128:(c + 1) * 128], hb[:, c * 128:(c + 1) * 128], self.ident[0],
                 (hbb, self.ident[1]), (pstb,))
        k.copy("act", hT.rearrange("p c t -> p (c t)"), pst, (pstb,), (hTb,))
        k.dma(j["HT"], hT, (hTb,), (j["htbuf"],), key=hTb)

    def apply(self, r, rbuf, H_dram_tile, HT_dram_tile, tp_bank, hbuf_dram, htbuf_dram):
        j = dict(i=self.cnt, r=r, rbuf=rbuf, H=H_dram_tile, HT=HT_dram_tile, bank=tp_bank, hbuf=hbuf_dram,
                 htbuf=htbuf_dram)
        self.cnt += 1
        self._a(j)
        if self.pc_:
            self._c(self.pc_.pop(0))
        if self.pb_:
            jb = self.pb_.pop(0)
            self._b(jb)
            self.pc_.append(jb)
        self.pb_.append(j)

    def flush(self):
        while self.pb_ or self.pc_:
            if self.pc_:
                self._c(self.pc_.pop(0))
            if self.pb_:
                jb = self.pb_.pop(0)
                self._b(jb)
                self.pc_.append(jb)


def build(nc, n_stages=99, debug=False, only=None):
    es = ExitStack()
    k = K(nc, es)

    def din(name, shape, dt=F32):
        return nc.dram_tensor(name, shape, dt, kind="ExternalInput").ap()

    def scratch(name, shape, dt):
        if debug:
            return nc.dram_tensor("dbg_" + name, shape, dt, kind="ExternalOutput").ap()
        return nc.dram_tensor(name, shape, dt).ap()

    x = din("x", [S, D])
    emb_g = din("emb_ln_g", [D]); emb_b = din("emb_ln_b", [D])
    fn_w_in = din("fn_w_in", [D, D]); fn_b_in = din("fn_b_in", [1, D])
    fn_w_out = din("fn_w_out", [D, D]); fn_b_out = din("fn_b_out", [1, D])
    ln_tok_g = din("ln_tok_g", [2, D]); ln_tok_b = din("ln_tok_b", [2, D])
    ff_w_up = din("ff_w_up", [2, D, 2 * D_FF]); ff_cw = din("ff_cwp", [2, 128, 176])
    ff_w_down = din("ff_w_down", [2, D_FF, D])
    ln_ffn_g = din("ln_ffn_g", [2, D]); ln_ffn_b = din("ln_ffn_b", [2, D])
    ssd_w_in = din("ssd_w_in", [D, D_IN_PROJ]); ssd_cwp = din("ssd_cwp", [128, 192])
    ssd_vec = din("ssd_vec", [160]); ssd_norm_g = din("ssd_norm_g", [2048]); ssd_w_out = din("ssd_w_out", [2048, D])
    tri_d = din("c_tri", [128, 512], F32)
    ident_d = din("c_ident", [128, 128], BF16)
    identf_d = din("c_identf", [128, 128], F32)
    cs128_d = din("c_cs128", [128, 256], BF16)
    ma_d = din("c_ma", [128, 64 * 128], BF16)
    bb_d = din("c_bb", [128, 64], BF16)
    y = nc.dram_tensor("y", [S, D], F32, kind="ExternalOutput").ap()

    H = [scratch("H%d" % i, [S, D], F32) for i in range(5)]
    HT = [nc.dram_tensor("HT%d" % i, [128, 8, S], BF16).ap() for i in range(5)]
    Hb = [Buf("b") for _ in range(5)]
    HTb = [Buf("b") for _ in range(5)]
    Gs = nc.dram_tensor("Gs", [64, 64, 2, 1024], BF16).ap()
    Gsb = Buf("b")
    Ys = nc.dram_tensor("Ys", [64, 128, 1024], BF16).ap()
    Ysb = Buf("b")
    Xs = nc.dram_tensor("Xs", [S, 2048], BF16).ap(); Xsb = Buf()
    Bs = nc.dram_tensor("Bs", [S, 1024], BF16).ap(); Bsb = Buf()
    BTs = nc.dram_tensor("BTs", [128, 8, S], BF16).ap(); BTsb = Buf()
    CTs = nc.dram_tensor("CTs", [128, 8, S], BF16).ap(); CTsb = Buf()
    Zs = nc.dram_tensor("Zs", [S, 2048], F32).ap(); Zsb = Buf()
    DTs = nc.dram_tensor("DTs", [S, 5, 64], F32).ap(); DTsb = Buf()
    ACs = nc.dram_tensor("ACs", [NT, 2, 32, 128], F32).ap(); ACsb = Buf()
    Yf = scratch("Yf", [S, 2048], F32); Yfb = Buf()
    Yb = scratch("Yb", [S, 2048], F32); Ybb = Buf()

    ident = k.alloc([128], BF16)
    k.dma(ident[0], ident_d, (), (ident[1],), key=ident[1])
    identf = k.alloc([128], F32)
    k.dma(identf[0], identf_d, (), (identf[1],), key=identf[1])
    ones = k.alloc([512], BF16)
    k.memset("pool", ones[0], 1.0, (ones[1],))
    ln = LNRes(k, ident)
    persist_top = k.top
    stage = [0]

    def next_stage():
        ln.flush()
        k.barrier()
        k.top = persist_top
        stage[0] += 1
        if only is not None:
            return stage[0] in only
        return stage[0] <= n_stages

    WbUp = [nc.dram_tensor("WbUp%d" % i, [44, 128, 1024], BF16).ap() for i in range(2)]
    WbUpb = [Buf(), Buf()]
    WbIn = nc.dram_tensor("WbIn", [32, 128, 1024], BF16).ap()
    WbInb = Buf()

    def prologue(which):
        stg = [k.alloc([8, 512], F32) for _ in range(3)]
        outb = [k.alloc([8, 512], BF16) for _ in range(3)]
        jobs = []
        if 0 in which:
            wv0 = ff_w_up[0].rearrange("(kc p) m -> p kc m", p=128)
            for pc in range(11):
                jobs.append((wv0[:, :, pc * 512:(pc + 1) * 512], WbUp[0][pc * 4:(pc + 1) * 4], WbUpb[0]))
        if 1 in which:
            wvi = ssd_w_in.rearrange("(kc p) m -> p kc m", p=128)
            for pc in range(8):
                jobs.append((wvi[:, :, 2048 + pc * 512:2048 + (pc + 1) * 512], WbIn[pc * 4:(pc + 1) * 4], WbInb))
            wv1 = ff_w_up[1].rearrange("(kc p) m -> p kc m", p=128)
            for pc in range(11):
                jobs.append((wv1[:, :, pc * 512:(pc + 1) * 512], WbUp[1][pc * 4:(pc + 1) * 4], WbUpb[1]))
        def job(i, src_, dst_, dbuf):
            s_, sb_ = rr(stg, i)
            o_, ob_ = rr(outb, i)
            k.dma(s_, src_, (), (sb_,), key=sb_)
            k.copy("act" if i % 2 == 0 else "dve", o_, s_, (sb_,), (ob_,))
            for c4 in range(4):
                k.dma(dst_[c4].rearrange("p (k m) -> p k m", k=8), o_[:, :, c4 * 128:(c4 + 1) * 128],
                      (ob_,), (dbuf,), key=(id(ob_), c4))
        return [(lambda i=i, jb=jb: job(i, *jb)) for i, jb in enumerate(jobs)]

    pro_jobs = prologue((0,)) if (only is None or 0 in only) else []
    ln.load_gb(emb_g, emb_b)
    xin = [k.alloc([D], F32) for i in range(3)]
    for t in range(NT if (only is None or 0 in only) else 0):
        xt, xb = rr(xin, t)
        k.dma(xt, x[t * 128:(t + 1) * 128, :], (), (xb,), key=xb)
        ln.apply(xt, xb, H[0][t * 128:(t + 1) * 128, :], HT[0][:, :, t * 128:(t + 1) * 128], t % 2, Hb[0], HTb[0])
        if pro_jobs:
            pro_jobs.pop(0)()
    while pro_jobs:
        pro_jobs.pop(0)()
    cur = 0

    if next_stage():
        hT = k.alloc([8, S], BF16)
        for kc in range(8):
            k.dma(hT[0][:, kc, :], HT[0][:, kc, :], (HTb[0],), (hT[1],), key=("hTl", kc % 4))
        stg = k.alloc([4, 1024], F32)
        w = k.alloc([8, 1024], BF16)
        wv = fn_w_in.rearrange("(kc p) m -> p kc m", p=128)
        for hf in range(2):
            k.load_cast(w[0][:, hf * 4:(hf + 1) * 4, :], w[1], wv[:, hf * 4:(hf + 1) * 4, :], stg)
        brow_f = k.alloc([1024], F32)
        brow = k.alloc([1024], BF16)
        k.dma(brow_f[0][0:1, :], fn_b_in, (), (brow_f[1],), key=brow_f[1])
        k.copy("pool", brow[0][0:1, :], brow_f[0][0:1, :], (brow_f[1],), (brow[1],))
        cs = k.alloc([256], BF16)
        k.dma(cs[0], cs128_d, (), (cs[1],), key=cs[1])
        h1 = [k.alloc([8, 512], BF16) for _ in range(2)]
        Gt = [k.alloc([2, 8, 128], BF16) for _ in range(2)]
        gi = 0
        for tb in range(8):
            h1a, h1b = rr(h1, tb)
            for g in range(8):
                bk = g % 2
                ps = k.bank(bk)
                for kc in range(8):
                    k.mm(ps, w[0][:, kc, g * 128:(g + 1) * 128], hT[0][:, kc, tb * 512:(tb + 1) * 512],
                         kc == 0, False, (w[1], hT[1]), (k.pb[bk],))
                k.mm(ps, brow[0][0:1, g * 128:(g + 1) * 128], ones[0][0:1, :], False, True,
                     (brow[1], ones[1]), (k.pb[bk],))
                k.copy("act" if g % 2 == 0 else "dve", h1a[:, g, :], ps, (k.pb[bk],), (h1b,))
            for tt_ in range(4):
                t = tb * 4 + tt_
                gt, gtb = rr(Gt, gi)
                for hf in range(2):
                    b0 = 2 + 2 * (gi % 2) if False else 2 + 2 * ((2 * gi + hf) % 3)
                    psg = k.bank(b0, 2)
                    for gg in range(4):
                        g = hf * 4 + gg
                        k.mm(psg[:, gg * 256:(gg + 1) * 256], h1a[:, g, tt_ * 128:(tt_ + 1) * 128], cs[0],
                             True, True, (h1b, cs[1]), (k.pb[b0], k.pb[b0 + 1]))
                    psv = psg.rearrange("p (g r j) -> p g r j", g=4, r=2)
                    for ri in range(2):
                        k.copy("act" if ri == 0 else "dve", gt[:, ri, hf * 4:(hf + 1) * 4, :], psv[:, :, ri, :],
                               (k.pb[b0], k.pb[b0 + 1]), (gtb,))
                k.dma(Gs[2 * t:2 * t + 2].rearrange("a b r c -> (a b) r c"), gt.rearrange("p r g j -> p r (g j)"),
                      (gtb,), (Gsb,), key=gtb)
                gi += 1

    if next_stage():
        ma = k.alloc([64, 128], BF16)
        k.dma(ma[0].rearrange("p a b -> p (a b)"), ma_d, (), (ma[1],), key=ma[1])
        NZ = 4
        Z = [k.alloc([1024], BF16) for _ in range(NZ)]
        Yt = [k.alloc([1024], BF16) for _ in range(NZ)]
        late_jobs = prologue((1,))
        for t2 in range(64):
            if t2 % 3 == 1 and late_jobs:
                late_jobs.pop(0)()
            z, zb = rr(Z, t2)
            yt, ytb = rr(Yt, t2)
            for ri in range(2):
                k.dma(z[ri * 64:(ri + 1) * 64, :], Gs[:, t2, ri, :], (Gsb,), (zb,), key=(id(zb), ri))
            for hf in range(2):
                bk = (2 * t2 + hf) % 8
                ps = k.bank(bk)
                k.mm(ps, ma[0][:, t2, :], z[:, hf * 512:(hf + 1) * 512], True, True, (ma[1], zb), (k.pb[bk],))
                k.copy("act" if hf == 0 else "dve", yt[:, hf * 512:(hf + 1) * 512], ps, (k.pb[bk],), (ytb,))
            k.dma(Ys[t2], yt, (ytb,), (Ysb,), key=ytb)
        while late_jobs:
            late_jobs.pop(0)()

    if next_stage():
        bb = k.alloc([64], BF16)
        k.dma(bb[0], bb_d, (), (bb[1],), key=bb[1])
        fT = k.alloc([8, S], BF16)
        stg = k.alloc([4, 1024], F32)
        w = k.alloc([8, 1024], BF16)
        wv = fn_w_out.rearrange("(kc p) m -> p kc m", p=128)
        for hf in range(2):
            k.load_cast(w[0][:, hf * 4:(hf + 1) * 4, :], w[1], wv[:, hf * 4:(hf + 1) * 4, :], stg)
        brow_f = k.alloc([1024], F32)
        brow = k.alloc([1024], BF16)
        k.dma(brow_f[0][0:1, :], fn_b_out, (), (brow_f[1],), key=brow_f[1])
        k.copy("pool", brow[0][0:1, :], brow_f[0][0:1, :], (brow_f[1],), (brow[1],))
        ln.load_gb(ln_tok_g[0], ln_tok_b[0])
        NZ = 4
        YY = [k.alloc([1024], BF16) for _ in range(NZ)]
        for k1 in range(64):
            yy, yyb = rr(YY, k1)
            for ro in range(2):
                k.dma(yy[ro * 64:(ro + 1) * 64, :], Ys[:, ro * 64 + k1, :], (Ysb,), (yyb,), key=(id(yyb), ro))
            bk = k1 % 4
            ps = k.bank(bk)
            for cc in range(8):
                k.mm(ps[:, cc * 64:(cc + 1) * 64], yy[:, cc * 128:(cc + 1) * 128], bb[0], True, True,
                     (yyb, bb[1]), (k.pb[bk],))
            k.copy("act" if k1 % 2 == 0 else "dve", fT[0][:, :, k1::64], ps.rearrange("p (c k) -> p c k", c=8),
                   (k.pb[bk],), (fT[1],))
        h0t = [k.alloc([D], F32) for _ in range(2)]
        rt = [k.alloc([D], F32) for _ in range(2)]
        for t in range(NT):
            ht, htb = rr(h0t, t)
            r, rb = rr(rt, t)
            k.dma(ht, H[0][t * 128:(t + 1) * 128, :], (Hb[0],), (htb,), key=htb)
            b0 = 4 + 2 * (t % 2)
            ps = k.bank(b0, 2)
            for hf in range(2):
                for cc in range(8):
                    k.mm(ps[:, hf * 512:(hf + 1) * 512], fT[0][:, cc, t * 128:(t + 1) * 128],
                         w[0][:, cc, hf * 512:(hf + 1) * 512], cc == 0, False, (fT[1], w[1]), (k.pb[b0 + hf],))
                k.mm(ps[:, hf * 512:(hf + 1) * 512], ones[0][0:1, 0:128], brow[0][0:1, hf * 512:(hf + 1) * 512],
                     False, True, (ones[1], brow[1]), (k.pb[b0 + hf],))
            k.stt(r, ht, ALPHA, ps, ALU.mult, ALU.add, (htb, k.pb[b0], k.pb[b0 + 1]), (rb,))
            ln.apply(r, rb, H[1][t * 128:(t + 1) * 128, :], HT[1][:, :, t * 128:(t + 1) * 128], t % 2, Hb[1], HTb[1])
        cur = 1

    def ffn_stage(layer, src, dst):
        TB = 512
        cw = k.alloc([44, 4], F32)
        k.dma(cw[0].rearrange("p c k -> p (c k)"), ff_cw[layer], (), (cw[1],), key=cw[1])
        wd = k.alloc([NFC, D], BF16)
        stg = [k.alloc([2, D], F32) for _ in range(2)]
        wdv = ff_w_down[layer].rearrange("(fc p) m -> p fc m", p=128)
        for i in range(NFC // 2):
            k.load_cast(wd[0][:, 2 * i:2 * i + 2, :], wd[1], wdv[:, 2 * i:2 * i + 2, :], rr(stg, i),
                        eng="act" if i % 2 == 0 else "dve")
        ln.load_gb(ln_ffn_g[layer], ln_ffn_b[layer])
        hblk = [k.alloc([8, TB + 2], BF16) for _ in range(2)]
        aT = [k.alloc([NFC, TB], BF16) for _ in range(1)]
        wbf = [k.alloc([2, 8, 128], BF16) for _ in range(4)]
        acc = [[k.alloc([TB], F32) for _ in range(3)] for _ in range(2)]
        pend_gate = []
        sg = [k.alloc([TB], F32) for _ in range(2)]
        h0t = [k.alloc([D], F32) for _ in range(3)]
        rt = [k.alloc([D], F32) for _ in range(3)]
        it = 0
        pend_ln = []

        def flush_ln():
            while pend_ln:
                pend_ln.pop(0)()

        for sbk in range(S // TB):
            T0 = sbk * TB
            hb, hbb = rr(hblk, sbk)
            lo = max(T0 - 1, 0); hi = min(T0 + TB + 1, S)
            if sbk == 0:
                k.memset("pool", hb[:, :, 0:1], 0.0, (hbb,))
            if sbk == S // TB - 1:
                k.memset("pool", hb[:, :, TB + 1:TB + 2], 0.0, (hbb,))
            for kc in range(8):
                k.dma(hb[:, kc, lo - (T0 - 1):hi - (T0 - 1)], HT[src][:, kc, lo:hi], (HTb[src],), (hbb,),
                      key=(id(hbb), kc % 4))
            a, ab = rr(aT, sbk)
            for fc in range(NFC):
                wb, wbb = rr(wbf, it)
                for gv in range(2):
                    k.dma(wb[:, gv], WbUp[layer][gv * NFC + fc].rearrange("p (k m) -> p k m", k=8),
                          (WbUpb[layer],), (wbb,), key=(id(wbb), gv))
                accs = []
                for gv in range(2):
                    b0 = 4 * (it % 2) + 2 * gv
                    ps = k.bank(b0, 2)
                    for (c0, c1, bk) in ((0, 512, b0), (512, 514, b0 + 1)):
                        for kc in range(8):
                            k.mm(ps[:, c0:c1], wb[:, gv, kc, :], hb[:, kc, c0:c1],
                                 kc == 0, kc == 7, (wbb, hbb), (k.pb[bk],))
                    ac, acb = acc[gv][it % 3]
                    ch = gv * NFC + fc
                    pbs = (k.pb[b0], k.pb[b0 + 1])
                    k.act(ac, ps[:, 1:TB + 1], AF.Identity, pbs + (cw[1],), (acb,),
                          bias=cw[0][:, ch, 3:4], scale=cw[0][:, ch, 1:2])
                    if gv == 1:
                        while pend_gate:
                            pend_gate.pop(0)()
                    k.stt(ac, ps[:, 0:TB], cw[0][:, ch, 0:1], ac, ALU.mult, ALU.add, pbs + (cw[1], acb), (acb,))
                    k.stt(ac, ps[:, 2:TB + 2], cw[0][:, ch, 2:3], ac, ALU.mult, ALU.add, pbs + (cw[1], acb), (acb,))
                    accs.append((ac, acb))
                def gate(accs=accs, it=it, fc=fc, a=a, ab=ab):
                    s_, sb_ = rr(sg, it)
                    k.act(s_, accs[0][0], AF.Silu, (accs[0][1],), (sb_,))
                    k.tt("pool", a[:, fc, :], s_, accs[1][0], ALU.mult, (sb_, accs[1][1]), (ab,))
                pend_gate.append(gate)
                it += 1
                if fc == 1:
                    flush_ln()
            while pend_gate:
                pend_gate.pop(0)()
            for tt_ in range(TB // 128):
                t = sbk * (TB // 128) + tt_
                ht, htb = rr(h0t, t)
                r, rb = rr(rt, t)
                k.dma(ht, H[src][t * 128:(t + 1) * 128, :], (Hb[src],), (htb,), key=htb)
                b0 = 2 * (tt_ % 2) + (0 if (it % 2 == 0) else 4)
                ps = k.bank(b0, 2)
                for hf in range(2):
                    for fc in range(NFC):
                        k.mm(ps[:, hf * 512:(hf + 1) * 512], a[:, fc, tt_ * 128:(tt_ + 1) * 128],
                             wd[0][:, fc, hf * 512:(hf + 1) * 512], fc == 0, fc == NFC - 1, (ab, wd[1]),
                             (k.pb[b0 + hf],))
                k.stt(r, ht, ALPHA, ps, ALU.mult, ALU.add, (htb, k.pb[b0], k.pb[b0 + 1]), (rb,))
                flush_ln()
                tpb = (b0 + 4) % 8
                pend_ln.append(lambda r=r, rb=rb, t=t, tpb=tpb: ln.apply(
                    r, rb, H[dst][t * 128:(t + 1) * 128, :], HT[dst][:, :, t * 128:(t + 1) * 128], tpb,
                    Hb[dst], HTb[dst]))
        flush_ln()

    if next_stage():
        ffn_stage(0, 1, 2)
        cur = 2

    def bc(ap, dims, off=0):
        return bass.AP(ap.tensor, ap.offset + off, [list(ap.ap[0])] + [list(d) for d in dims])

    NSL = 5

    def ssd_s1(src):
        TB = 512
        wv = ssd_w_in.rearrange("(kc p) m -> p kc m", p=128)
        tri = k.alloc([4, 128], F32)
        k.dma(tri[0].rearrange("p a b -> p (a b)"), tri_d, (), (tri[1],), key=tri[1])
        tri16 = k.alloc([4, 128], BF16)
        k.copy("dve", tri16[0], tri[0], (tri[1],), (tri16[1],))
        vec = k.alloc([160], F32)
        k.dma(vec[0], ssd_vec.partition_broadcast(128), (), (vec[1],), key=vec[1])
        abc_ = k.alloc([64], F32)
        k.act(abc_[0], vec[0][:, 64:128], AF.Exp, (vec[1],), (abc_[1],))
        k.ts("dve", abc_[0], abc_[0], -1.0, None, ALU.mult, None, (abc_[1],), (abc_[1],))
        cwp = k.alloc([32, 6], F32)
        k.dma(cwp[0].rearrange("p c k -> p (c k)"), ssd_cwp, (), (cwp[1],), key=cwp[1])
        wz = k.alloc([8, 2048], BF16)
        wdt = k.alloc([8, 64], BF16)
        stg = [k.alloc([8, 512], F32) for _ in range(1)]
        uS = [k.alloc([TB + 4], F32) for _ in range(3)]
        for q in range(4):
            k.load_cast(wz[0][:, :, q * 512:(q + 1) * 512], wz[1], wv[:, :, q * 512:(q + 1) * 512], rr(stg, q),
                        eng="act" if q % 2 == 0 else "dve")
        k.load_cast(wdt[0], wdt[1], wv[:, :, 6144:6208], (stg[0][0][:, :, 0:64], stg[0][1]), eng="dve")
        hblk = [k.alloc([8, TB + 4], BF16) for _ in range(2)]
        wbf = [k.alloc([8, 128], BF16) for _ in range(4)]
        acc = [k.alloc([TB], F32) for _ in range(3)]
        xo = [k.alloc([TB], BF16) for _ in range(4)]
        Xblk = [k.alloc([4, 2048], BF16) for _ in range(1)]
        Bblk = [k.alloc([4, 1024], BF16) for _ in range(1)]
        zt = [k.alloc([2048], F32) for _ in range(2)]
        sm = k.alloc([4, 8, 64], F32)
        csb = k.alloc([4, 4, 32], F32)
        hl = k.alloc([2, 4, 64], BF16)
        DT = k.alloc([4, NSL, 64], F32)
        actT = k.alloc([1024], F32)
        it = 0
        for tb in range(S // TB):
            T0 = tb * TB
            hb, hbb = rr(hblk, tb)
            lo = max(T0 - 2, 0); hi = min(T0 + TB + 2, S)
            if tb == 0:
                k.memset("pool", hb[:, :, 0:2], 0.0, (hbb,))
            if tb == S // TB - 1:
                k.memset("pool", hb[:, :, TB + 2:TB + 4], 0.0, (hbb,))
            for kc in range(8):
                k.dma(hb[:, kc, lo - (T0 - 2):hi - (T0 - 2)], HT[src][:, kc, lo:hi], (HTb[src],), (hbb,),
                      key=(id(hbb), kc % 4))
            Xb_, Xbb = rr(Xblk, tb)
            Bb_, Bbb = rr(Bblk, tb)
            pend_a = []
            pend_b = []
            for ch in range(32):
                wb, wbb = rr(wbf, it)
                k.dma(wb, WbIn[ch].rearrange("p (k m) -> p k m", k=8), (WbInb,), (wbb,), key=wbb)
                b0 = (0, 2, 5)[it % 3]
                ps = k.bank(b0, 2)
                for (c0, c1, bk) in ((0, 512, b0), (512, 516, b0 + 1)):
                    for kc in range(8):
                        k.mm(ps[:, c0:c1], wb[:, kc, :], hb[:, kc, c0:c1], kc == 0, kc == 7, (wbb, hbb), (k.pb[bk],))
                ac, acb = rr(acc, it)
                us_, usb = rr(uS, it)
                pbs = (k.pb[b0], k.pb[b0 + 1])
                k.copy("act", us_, ps[:, 0:TB + 4], pbs, (usb,))
                k.act(ac, us_[:, 2:TB + 2], AF.Identity, (usb, cwp[1]), (acb,),
                      bias=cwp[0][:, ch, 5:6], scale=cwp[0][:, ch, 2:3])
                while pend_a:
                    pend_a.pop(0)()
                for tap in (0, 1, 3, 4):
                    k.stt(ac, us_[:, tap:TB + tap], cwp[0][:, ch, tap:tap + 1], ac, ALU.mult, ALU.add,
                          (usb, cwp[1], acb), (acb,))
                while len(pend_b) > 1:
                    pend_b.pop(0)()
                o, ob = rr(xo, it)

                def tail_a(ch=ch, o=o, ob=ob, T0=T0, ac=ac, acb=acb):
                    k.act(o, ac, AF.Silu, (acb,), (ob,))
                    if 16 <= ch < 24:
                        k.dma(BTs[:, ch - 16, T0:T0 + TB], o, (ob,), (BTsb,), key=ob)
                    if ch >= 24:
                        k.dma(CTs[:, ch - 24, T0:T0 + TB], o, (ob,), (CTsb,), key=ob)

                def tail(ch=ch, o=o, ob=ob, T0=T0, ac=ac, acb=acb):
                    if ch < 24:
                        pst = k.bank(4, 1, BF16)
                        for tt_ in range(4):
                            k.tr(pst[:, tt_ * 128:(tt_ + 1) * 128], o[:, tt_ * 128:(tt_ + 1) * 128], ident[0],
                                 (ob, ident[1]), (k.pb[4],))
                        if ch < 16:
                            k.copy("act", Xb_[:, :, ch * 128:(ch + 1) * 128],
                                   pst[:, 0:512].rearrange("p (t c) -> p t c", t=4), (k.pb[4],), (Xbb,))
                        else:
                            g = ch - 16
                            k.copy("act", Bb_[:, :, g * 128:(g + 1) * 128],
                                   pst[:, 0:512].rearrange("p (t c) -> p t c", t=4), (k.pb[4],), (Bbb,))
                pend_a.append(tail_a)
                pend_b.append(tail)
                it += 1
            for tt_ in range(4):
                t = tb * 4 + tt_
                tok = slice(2 + tt_ * 128, 2 + (tt_ + 1) * 128)
                z_, zb_ = rr(zt, t)
                for q in range(4):
                    bk = 1 + 2 * (q % 2)
                    ps = k.bank(bk)
                    for kc in range(8):
                        k.mm(ps, hb[:, kc, tok], wz[0][:, kc, q * 512:(q + 1) * 512], kc == 0, kc == 7,
                             (hbb, wz[1]), (k.pb[bk],))
                    if tt_ == 0 and q == 0:
                        while pend_a:
                            pend_a.pop(0)()
                    if tt_ == 0 and q == 1:
                        while pend_b:
                            pend_b.pop(0)()
                        k.dma(Xs[T0:T0 + TB, :].rearrange("(t p) c -> p t c", p=128), Xb_, (Xbb,), (Xsb,), key=Xbb)
                        k.dma(Bs[T0:T0 + TB, :].rearrange("(t p) c -> p t c", p=128), Bb_, (Bbb,), (Bsb,), key=Bbb)
                    k.act(z_[:, q * 512:(q + 1) * 512], ps, AF.Silu, (k.pb[bk],), (zb_,))
                k.dma(Zs[t * 128:(t + 1) * 128, :], z_, (zb_,), (Zsb,), key=zb_)
            p7 = k.bank(7)
            pb7 = k.pb[7]
            for tt_ in range(4):
                tok = slice(2 + tt_ * 128, 2 + (tt_ + 1) * 128)
                for kc in range(8):
                    k.mm(p7[:, tt_ * 64:(tt_ + 1) * 64], hb[:, kc, tok], wdt[0][:, kc, :], kc == 0, kc == 7,
                         (hbb, wdt[1]), (pb7,))
            s_, sb_ = sm
            d1 = s_[:, :, 0, :]; dtv = s_[:, :, 1, :]; lndt = s_[:, :, 2, :]; dA = s_[:, :, 3, :]
            tmp = s_[:, :, 4, :]; tmp2 = s_[:, :, 5, :]
            k.tt("dve", d1, p7[:, 0:256].rearrange("p (t c) -> p t c", t=4), bc(vec[0][:, 0:64], [(0, 4), (1, 64)]),
                 ALU.add, (pb7, vec[1]), (sb_,))
            k.act(d1, d1, AF.Exp, (sb_,), (sb_,))
            k.act(dtv, d1, AF.Ln, (sb_,), (sb_,), bias=1.0)
            k.act(lndt, dtv, AF.Ln, (sb_,), (sb_,))
            k.tt("dve", dA, dtv, bc(abc_[0], [(0, 4), (1, 64)]), ALU.mult, (sb_, abc_[1]), (sb_,))
            k.copy("dve", hl[0][:, 0], dA, (sb_,), (hl[1],))
            k.tt("dve", hl[0][:, 1], dA, hl[0][:, 0], ALU.subtract, (sb_, hl[1]), (hl[1],))
            pcs = k.bank(1)
            pT = k.bank(2, 2)
            for tt_ in range(4):
                c0 = tt_ * 128
                for (dst_c, lhs_sel, col_sel) in ((0, 0, slice(0, 32)), (32, 1, slice(32, 64))):
                    for pi in range(2):
                        k.mm(pcs[:, c0 + dst_c:c0 + dst_c + 32], tri16[0][:, lhs_sel, :], hl[0][:, pi, tt_, col_sel],
                             pi == 0, pi == 1, (tri16[1], hl[1]), (k.pb[1],))
                for pi in range(2):
                    k.mm(pcs[:, c0 + 64:c0 + 128], ones[0][:, 0:128], hl[0][:, pi, tt_, :], pi == 0, pi == 1,
                         (ones[1], hl[1]), (k.pb[1],))
                t0c = tt_ * 256
                for (dst_c, col_sel, rsel) in ((0, slice(0, 32), 0), (128, slice(32, 64), 2)):
                    for pi in range(2):
                        k.mm(pT[0:32, t0c + dst_c:t0c + dst_c + 128], hl[0][:, pi, tt_, col_sel], tri16[0][:, rsel, :],
                             pi == 0, pi == 1, (tri16[1], hl[1]), (k.pb[2 + tt_ // 2],))
            k.copy("act", csb[0].rearrange("p t q c -> p (t q c)"), pcs, (k.pb[1],), (csb[1],))
            k.copy("act", actT[0][0:32, :], pT[0:32, :], (k.pb[2], k.pb[3]), (actT[1],))
            k.dma(ACs[tb * 4:(tb + 1) * 4].rearrange("t d h l -> h t d l"),
                  actT[0][0:32, :].rearrange("p (t d l) -> p t d l", t=4, d=2), (actT[1],), (ACsb,), key=actT[1])
            cs_ = csb[0]
            acf = cs_[:, :, 0, :]; ecb = cs_[:, :, 1, :]; totf = cs_[:, :, 2, :]; totb = cs_[:, :, 3, :]
            dt_, dtb_ = DT
            cr = (csb[1],)
            k.copy("dve", dt_[:, :, 0, 0:32], acf, cr, (dtb_,))
            k.ts("dve", dt_[:, :, 0, 32:64], ecb, -1.0, None, ALU.mult, None, cr, (dtb_,))
            k.tt("dve", tmp[:, :, 0:32], totf, acf, ALU.subtract, cr, (sb_,))
            k.tt("dve", tmp[:, :, 0:32], tmp[:, :, 0:32], lndt[:, :, 0:32], ALU.add, (sb_,), (sb_,))
            k.tt("dve", tmp[:, :, 32:64], ecb, lndt[:, :, 32:64], ALU.add, cr + (sb_,), (sb_,))
            k.act(dt_[:, :, 1, :], tmp, AF.Exp, (sb_,), (dtb_,))
            k.copy("dve", tmp2[:, :, 0:32], acf, cr, (sb_,))
            k.tt("dve", tmp2[:, :, 32:64], totb, ecb, ALU.subtract, cr, (sb_,))
            k.act(dt_[:, :, 2, :], tmp2, AF.Exp, (sb_,), (dtb_,))
            k.act(dt_[:, :, 3, :], cs_[:, :, 2:4, :], AF.Exp, cr, (dtb_,))
            k.copy("dve", dt_[:, :, 4, :], dtv, (sb_,), (dtb_,))
            k.dma(DTs[T0:T0 + TB].rearrange("(t p) s c -> p t s c", p=128), dt_, (dtb_,), (DTsb,), key=dtb_)

    def ssd_scan(d):
        tri = k.alloc([4, 128], F32)
        k.dma(tri[0].rearrange("p a b -> p (a b)"), tri_d, (), (tri[1],), key=tri[1])
        mask = tri[0][:, 0, :] if d == 0 else tri[0][:, 3, :]
        Hf = k.alloc([2048], F32)
        Hbf = k.alloc([2048], BF16)
        NB = 3
        Xt = [k.alloc([2048], BF16) for _ in range(NB)]
        BTt = [k.alloc([8, 128], BF16) for _ in range(NB)]
        CTt = [k.alloc([8, 128], BF16) for _ in range(NB)]
        Bt = [k.alloc([1024], BF16) for _ in range(NB)]
        DTt = [k.alloc([NSL, 64], F32) for _ in range(NB)]
        Abc = [k.alloc([32, 128], F32) for _ in range(2)]
        Abc2 = [Buf(), Buf()]
        Eb = [k.alloc([32, 128], BF16) for _ in range(2)]
        Mt = [k.alloc([32, 128], BF16) for _ in range(3)]
        cbm = [k.alloc([8, 128], BF16) for _ in range(2)]
        yo = k.alloc([2048], F32)
        yv = [k.alloc([2048], F32) for _ in range(2)]
        Xw = [k.alloc([2048], BF16) for _ in range(2)]
        Xd = [k.alloc([2048], BF16) for _ in range(2)]
        order = list(range(NT)) if d == 0 else list(range(NT - 1, -1, -1))
        Yout = Yf if d == 0 else Yb
        Youtb = Yfb if d == 0 else Ybb
        def P1(i):
            c = order[i]
            last = (i == NT - 1)
            tok = slice(c * 128, (c + 1) * 128)
            X, Xb = rr(Xt, i); BT, BTb = rr(BTt, i); CT, CTb = rr(CTt, i); B_, Bb = rr(Bt, i)
            DTc, DTb_ = rr(DTt, i); A_, Ab = rr(Abc, i)
            k.dma(A_.rearrange("p h l -> p (h l)"),
                  ACs[c, d].rearrange("h l -> (h l)").partition_broadcast(128), (ACsb,), (Ab, rr(Abc2, i)), key=Ab)
            k.dma(DTc, DTs[tok], (DTsb,), (DTb_,), key=DTb_)
            k.dma(BT, BTs[:, :, tok], (BTsb,), (BTb,), key=BTb)
            k.dma(CT, CTs[:, :, tok], (CTsb,), (CTb,), key=CTb)
            k.dma(X, Xs[tok, :], (Xsb,), (Xb,), key=Xb)
            k.dma(B_, Bs[tok, :], (Bsb,), (Bb,), key=Bb)
            dsl = slice(d * 32, (d + 1) * 32)
            p45 = k.bank(4, 2)
            for g in range(8):
                k.mm(p45[:, g * 128:(g + 1) * 128], BT[:, g, :], CT[:, g, :], True, True, (BTb, CTb),
                     (k.pb[4 + g // 4],))
            Ab2 = rr(Abc2, i)
            k.tt("pool", A_[:, 0:18, :], bc(DTc[:, 0, d * 32:d * 32 + 18], [(1, 18), (0, 128)]), A_[:, 0:18, :],
                 ALU.subtract, (Ab, DTb_), (Ab,))
            k.tt("dve", A_[:, 18:32, :], bc(DTc[:, 0, d * 32 + 18:d * 32 + 32], [(1, 14), (0, 128)]), A_[:, 18:32, :],
                 ALU.subtract, (Ab2, DTb_), (Ab2,))
            k.act(A_, A_, AF.Relu, (Ab, Ab2), (Ab, Ab2))
            E_, Ebb = rr(Eb, i)
            k.act(E_, A_, AF.Exp, (Ab, Ab2), (Ebb,), scale=-1.0)
            cb_, cbb = rr(cbm, i)
            k.tt("dve", cb_, p45.rearrange("p (g l) -> p g l", g=8), bc(mask, [(0, 8), (1, 128)]), ALU.mult,
                 (k.pb[4], k.pb[5], tri[1]), (cbb,))
            M_, Mb = rr(Mt, i)
            k.tt("dve", M_.rearrange("p (g r) l -> p g r l", g=8), E_.rearrange("p (g r) l -> p g r l", g=8),
                 bc(cb_[:, 0, :], [(128, 8), (0, 4), (1, 128)]), ALU.mult, (Ebb, cbb), (Mb,))
            xd_, xdb = rr(Xd, i)
            k.tt("pool", xd_.rearrange("p (h q) -> p h q", h=32), X.rearrange("p (h q) -> p h q", h=32),
                 bc(DTc[:, 4, dsl], [(1, 32), (0, 64)]), ALU.mult, (Xb, DTb_), (xdb,))
            xw_, xwb = rr(Xw, i)
            if not last:
                k.tt("pool", xw_.rearrange("p (h q) -> p h q", h=32), X.rearrange("p (h q) -> p h q", h=32),
                     bc(DTc[:, 1, dsl], [(1, 32), (0, 64)]), ALU.mult, (Xb, DTb_), (xwb,))

        def P2(i):
            c = order[i]
            first = (i == 0)
            last = (i == NT - 1)
            tok = slice(c * 128, (c + 1) * 128)
            CT, CTb = rr(CTt, i); B_, Bb = rr(Bt, i)
            DTc, DTb_ = rr(DTt, i)
            M_, Mb = rr(Mt, i)
            xd_, xdb = rr(Xd, i)
            xw_, xwb = rr(Xw, i)
            dsl = slice(d * 32, (d + 1) * 32)
            p03 = k.bank(0, 4)
            pb03 = tuple(k.pb[0:4])
            if not first:
                for g in range(8):
                    k.mm(p03[:, g * 256:(g + 1) * 256], CT[:, g, :], Hbf[0][:, g * 256:(g + 1) * 256], True, True,
                         (CTb, Hbf[1]), (k.pb[g // 2],))
                k.tt("dve", yo[0].rearrange("p (h q) -> p h q", h=32), p03.rearrange("p (h q) -> p h q", h=32),
                     bc(DTc[:, 2, dsl], [(1, 32), (0, 64)]), ALU.mult, pb03 + (DTb_,), (yo[1],))

        def P2b(i):
            c = order[i]
            first = (i == 0)
            last = (i == NT - 1)
            tok = slice(c * 128, (c + 1) * 128)
            CT, CTb = rr(CTt, i); B_, Bb = rr(Bt, i)
            DTc, DTb_ = rr(DTt, i)
            M_, Mb = rr(Mt, i)
            xd_, xdb = rr(Xd, i)
            xw_, xwb = rr(Xw, i)
            dsl = slice(d * 32, (d + 1) * 32)
            p03 = k.bank(0, 4)
            pb03 = tuple(k.pb[0:4])
            for h in range(32):
                k.mm(p03[:, h * 64:(h + 1) * 64], M_[:, h, :], xd_[:, h * 64:(h + 1) * 64], True, True, (Mb, xdb),
                     (k.pb[h // 8],))
            y_, yb_ = rr(yv, i)
            if first:
                k.copy("act", y_, p03, pb03, (yb_,))
            else:
                k.tt("dve", y_, p03, yo[0], ALU.add, pb03 + (yo[1],), (yb_,))
            k.dma(Yout[tok, :], y_, (yb_,), (Youtb,), key=yb_)
            if not last:
                p47 = k.bank(4, 4)
                pb47 = tuple(k.pb[4:8])
                for g in range(8):
                    k.mm(p47[:, g * 256:(g + 1) * 256], B_[:, g * 128:(g + 1) * 128], xw_[:, g * 256:(g + 1) * 256],
                         True, True, (Bb, xwb), (k.pb[4 + g // 2],))
                if first:
                    k.copy("act", Hf[0], p47, pb47, (Hf[1],))
                else:
                    k.tt("pool", Hf[0].rearrange("p (h q) -> p h q", h=32), Hf[0].rearrange("p (h q) -> p h q", h=32),
                         bc(DTc[:, 3, dsl], [(1, 32), (0, 64)]), ALU.mult, (Hf[1], DTb_), (Hf[1],))
                    k.tt("dve", Hf[0], Hf[0], p47, ALU.add, pb47 + (Hf[1],), (Hf[1],))
                k.copy("act", Hbf[0], Hf[0], (Hf[1],), (Hbf[1],))

        P1(0)
        for i in range(NT):
            if i + 1 < NT:
                P1(i + 1)
            P2(i)
            P2b(i)

    def ssd_s3(src, dst, layer):
        vec = k.alloc([160], F32)
        k.dma(vec[0], ssd_vec.partition_broadcast(128), (), (vec[1],), key=vec[1])
        ng = k.alloc([2048], F32)
        k.dma(ng[0], ssd_norm_g.partition_broadcast(128), (), (ng[1],), key=ng[1])
        wo = k.alloc([16, D], BF16)
        stg = [k.alloc([2, D], F32) for _ in range(2)]
        wov = ssd_w_out.rearrange("(kc p) m -> p kc m", p=128)
        for i in range(8):
            k.load_cast(wo[0][:, 2 * i:2 * i + 2, :], wo[1], wov[:, 2 * i:2 * i + 2, :], rr(stg, i),
                        eng="act" if i % 2 == 0 else "dve")
        ln.load_gb(ln_tok_g[layer], ln_tok_b[layer])
        NB = 2
        Xt = [k.alloc([2048], BF16) for _ in range(NB)]
        Yt = [k.alloc([2048], F32) for _ in range(NB)]
        Y2t = [k.alloc([2048], F32) for _ in range(NB)]
        Zt = [k.alloc([2048], F32) for _ in range(NB)]
        t1 = k.alloc([2048], F32)
        ss = [k.alloc([8], F32) for _ in range(2)]
        ynb = [k.alloc([2048], BF16) for _ in range(3)]
        yT = [k.alloc([16, 128], BF16) for _ in range(2)]
        h0t = [k.alloc([D], F32) for _ in range(2)]
        rt = [k.alloc([D], F32) for _ in range(3)]

        def phaseA(t):
            tok = slice(t * 128, (t + 1) * 128)
            X, Xb = rr(Xt, t); Y, Yb_ = rr(Yt, t); Y2, Y2b = rr(Y2t, t); Z, Zb = rr(Zt, t)
            k.dma(X, Xs[tok, :], (Xsb,), (Xb,), key=Xb)
            k.dma(Y, Yf[tok, :], (Yfb,), (Yb_,), key=Yb_)
            k.dma(Y2, Yb[tok, :], (Ybb,), (Y2b,), key=Y2b)
            k.dma(Z, Zs[tok, :], (Zsb,), (Zb,), key=Zb)
            k.tt("pool", t1[0].rearrange("p (h q) -> p h q", h=32), X.rearrange("p (h q) -> p h q", h=32),
                 bc(vec[0][:, 128:160], [(1, 32), (0, 64)]), ALU.mult, (Xb, vec[1]), (t1[1],))
            k.tt("dve", Y, Y, Y2, ALU.add, (Yb_, Y2b), (Yb_,))
            k.tt("dve", Y, Y, t1[0], ALU.add, (Yb_, t1[1]), (Yb_,))
            k.tt("pool", Y, Y, Z, ALU.mult, (Yb_, Zb), (Yb_,))
            k.tt("dve", t1[0], Y, Y, ALU.mult, (Yb_,), (t1[1],))
            s_, sb_ = rr(ss, t)
            k.s.add("dve", lambda e, s_=s_: e.tensor_reduce(s_, t1[0].rearrange("p (g q) -> p g q", g=8),
                                                            mybir.AxisListType.X, ALU.add), (t1[1],), (sb_,))
            k.act(s_, s_, AF.Sqrt, (sb_,), (sb_,), bias=EPS, scale=1.0 / 256.0)
            k.s.add("dve", lambda e, s_=s_: e.reciprocal(s_, s_), (sb_,), (sb_,))
            k.tt("dve", Y.rearrange("p (g q) -> p g q", g=8), Y.rearrange("p (g q) -> p g q", g=8),
                 bc(s_, [(1, 8), (0, 256)]), ALU.mult, (Yb_, sb_), (Yb_,))
            yb16, yb16b = rr(ynb, t)
            k.tt("pool", yb16, Y, ng[0], ALU.mult, (Yb_, ng[1]), (yb16b,))

        def phaseB(t):
            tok = slice(t * 128, (t + 1) * 128)
            yb16, yb16b = rr(ynb, t)
            b0 = 2 * (t % 2)
            pst = k.bank(b0, 2, BF16)
            for c in range(16):
                k.tr(pst[:, c * 128:(c + 1) * 128], yb16[:, c * 128:(c + 1) * 128], ident[0], (yb16b, ident[1]),
                     (k.pb[b0 + c // 8],))
            yT_, yTb = rr(yT, t)
            k.copy("act", yT_.rearrange("p c t -> p (c t)"), pst, (k.pb[b0], k.pb[b0 + 1]), (yTb,))
            ht, htb = rr(h0t, t)
            r, rb = rr(rt, t)
            k.dma(ht, H[src][tok, :], (Hb[src],), (htb,), key=htb)
            b1 = 4 + 2 * (t % 2)
            ps = k.bank(b1, 2)
            for hf in range(2):
                for kc in range(16):
                    k.mm(ps[:, hf * 512:(hf + 1) * 512], yT_[:, kc, :], wo[0][:, kc, hf * 512:(hf + 1) * 512],
                         kc == 0, kc == 15, (yTb, wo[1]), (k.pb[b1 + hf],))
            k.stt(r, ht, ALPHA, ps, ALU.mult, ALU.add, (htb, k.pb[b1], k.pb[b1 + 1]), (rb,))

        def phaseC(t):
            tok = slice(t * 128, (t + 1) * 128)
            r, rb = rr(rt, t)
            ln.apply(r, rb, H[dst][tok, :], HT[dst][:, :, tok], 2 * (t % 2), Hb[dst], HTb[dst])

        for step in range(NT + 2):
            if step < NT:
                phaseA(step)
            if 0 <= step - 1 < NT:
                phaseB(step - 1)
            if 0 <= step - 2 < NT:
                phaseC(step - 2)

    if next_stage():
        ssd_s1(2)
    if next_stage():
        ssd_scan(0)
    if next_stage():
        ssd_scan(1)
    if next_stage():
        ssd_s3(2, 3, 1)
        cur = 3
    if next_stage():
        ffn_stage(1, 3, 4)
        cur = 4

    ln.flush()
    k.barrier()
    k.top = persist_top
    ft, fb = k.alloc([8, D], F32)
    for q in range(S // 1024):
        k.dma(ft, H[cur][q * 1024:(q + 1) * 1024, :].rearrange("(p a) d -> p a d", a=8), (Hb[cur],), (fb,), key=fb)
        k.dma(y[q * 1024:(q + 1) * 1024, :].rearrange("(p a) d -> p a d", a=8), ft, (fb,), (), key=fb)
    k.s.add("sp", None, (fb,), (fb,))

    k.s.emit(nc, es)
    es.close()
    return nc


def make_consts():
    c = {}
    bf = ml_dtypes.bfloat16
    c["c_ident"] = np.eye(128, dtype=np.float32).astype(bf)
    c["c_identf"] = np.eye(128, dtype=np.float32)
    j = np.arange(128, dtype=np.float64)
    ang = 2 * np.pi * np.outer(j, j) / 128.0
    sc = 1.0 / math.sqrt(128.0 * S)
    c["c_cs128"] = np.concatenate([np.cos(ang) * sc, -np.sin(ang) * sc], axis=1).astype(np.float32).astype(bf)
    t1 = np.arange(64, dtype=np.float64)[:, None, None]
    t2 = np.arange(64, dtype=np.float64)[None, :, None]
    k1 = np.arange(64, dtype=np.float64)[None, None, :]
    th = -2 * np.pi * (t2 * k1 / 4096.0 + t1 * k1 / 64.0)
    Pr = np.cos(th); Pi = np.sin(th)
    ma = np.zeros((2, 64, 64, 2, 64), dtype=np.float64)
    ma[0, :, :, 0, :] = Pr
    ma[1, :, :, 0, :] = -Pi
    ma[0, :, :, 1, :] = Pi
    ma[1, :, :, 1, :] = Pr
    c["c_ma"] = ma.reshape(128, 64 * 128).astype(np.float32).astype(bf)
    a = 2 * np.pi * np.outer(np.arange(64.0), np.arange(64.0)) / 64.0
    c["c_bb"] = np.concatenate([np.cos(a), np.sin(a)], axis=0).astype(np.float32).astype(bf)
    r = np.arange(128)[:, None]; cl = np.arange(128)[None, :]
    tri = np.stack([(cl >= r), (cl > r), -(cl > r).astype(np.float32), (cl <= r)], axis=1).astype(np.float32)
    c["c_tri"] = tri.reshape(128, 512)
    return c


def prep_inputs(inputs):
    f = lambda a: np.ascontiguousarray(a, dtype=np.float32)
    shared = {}
    shared["emb_ln_g"] = f(inputs["emb_ln_g"]); shared["emb_ln_b"] = f(inputs["emb_ln_b"])
    shared["fn_w_in"] = f(inputs["fn_w_in"][0]); shared["fn_b_in"] = f(inputs["fn_b_in"][0:1])
    shared["fn_w_out"] = f(inputs["fn_w_out"][0]); shared["fn_b_out"] = f(inputs["fn_b_out"][0:1])
    shared["ln_tok_g"] = f(inputs["ln_tok_g"]); shared["ln_tok_b"] = f(inputs["ln_tok_b"])
    shared["ff_w_up"] = f(inputs["ff_w_up"])
    cw4 = np.concatenate([inputs["ff_conv_w"], inputs["ff_conv_b"][:, None, :]], axis=1)
    shared["ff_cwp"] = f(cw4.reshape(cw4.shape[0], 4, 44, 128).transpose(0, 3, 2, 1).reshape(cw4.shape[0], 128, 176))
    shared["ff_w_down"] = f(inputs["ff_w_down"])
    shared["ln_ffn_g"] = f(inputs["ln_ffn_g"]); shared["ln_ffn_b"] = f(inputs["ln_ffn_b"])
    shared["ssd_w_in"] = f(inputs["ssd_w_in"][0])
    cw6 = np.concatenate([inputs["ssd_conv_w"][0], inputs["ssd_conv_b"][0][None, :]], axis=0)
    shared["ssd_cwp"] = f(cw6.reshape(6, 32, 128).transpose(2, 1, 0).reshape(128, 192))
    shared["ssd_vec"] = f(np.concatenate([inputs["ssd_dt_bias_fwd"][0], inputs["ssd_dt_bias_bwd"][0],
                                          inputs["ssd_a_log_fwd"][0], inputs["ssd_a_log_bwd"][0], inputs["ssd_d"][0]]))
    shared["ssd_norm_g"] = f(inputs["ssd_norm_g"][0]); shared["ssd_w_out"] = f(inputs["ssd_w_out"][0])
    shared.update(make_consts())
    x = f(inputs["x"])
    return [dict(shared, x=x[b]) for b in range(x.shape[0])]


def run(inputs, n_stages=99, debug=False):
    nc = bass.Bass("TRN2", target_bir_lowering=False)
    build(nc, n_stages=n_stages, debug=debug)
    in_maps = prep_inputs(inputs)
    res = run_bass_kernel_spmd(nc, in_maps, core_ids=list(range(len(in_maps))))
    return res.results


def kernel(**inputs):
    results = run(inputs)
    out = np.stack([np.asarray(r["y"]) for r in results], axis=0)
    return out.astype(np.float32)
```

```python
import math
from contextlib import ExitStack

import numpy as np
import ml_dtypes

import concourse.bass as bass
import concourse.mybir as mybir
from concourse.bass_utils import run_bass_kernel_spmd

F32 = mybir.dt.float32
BF16 = mybir.dt.bfloat16
AF = mybir.ActivationFunctionType
ALU = mybir.AluOpType

S = 4096
D = 1024
NT = S // 128
DEPTH = 2
ALPHA = (2.0 * DEPTH) ** 0.25
EPS = 1e-5
D_FF = 2816
NFC = D_FF // 128
D_INNER = 2048
CONV_DIM = 4096
D_IN_PROJ = 6208

SAME_ENGINE_SYNC = True
HOIST_WINDOW = 96


class Buf:
    __slots__ = ("name", "last_w", "readers")

    def __init__(self, name="b"):
        self.name = name
        self.last_w = None
        self.readers = {}


class Op:
    __slots__ = ("eng", "fn", "deps", "signal", "sigval", "sem", "is_dma", "idx", "key", "sp_pos", "sp_need",
                 "is_load", "fence", "max_dep")

    def __init__(self, eng, fn, is_dma, key):
        self.eng = eng
        self.fn = fn
        self.deps = []
        self.signal = False
        self.sigval = None
        self.sem = None
        self.is_dma = is_dma
        self.key = key
        self.sp_pos = -1
        self.sp_need = -1
        self.is_load = False
        self.fence = False
        self.max_dep = -1


class Sched:
    ENGS = ("pe", "act", "dve", "pool", "sp")

    def __init__(self):
        self.ops = {e: [] for e in self.ENGS}
        self.dma_keys = {}
        self.key_slot = {}
        self.slots = []
        self.free = []
        self.n = 0
        self.fence_pos = -1

    def barrier(self):
        deps = []
        for e in self.ENGS:
            for op in reversed(self.ops[e]):
                if op.fn is not None and not op.is_dma:
                    op.signal = True
                    deps.append(op)
                    break
        for sl in self.slots:
            if sl["last_op"] is not None:
                deps.append(sl["last_op"])
        for e in self.ENGS:
            w = Op(e, None, False, None)
            w.idx = self.n
            self.n += 1
            w.deps = list(deps)
            if e == "sp":
                w.sp_pos = len(self.ops[e])
                w.fence = True
                self.fence_pos = w.sp_pos
            self.ops[e].append(w)
        self.key_slot = {}
        self.free = list(range(len(self.slots)))

    def add(self, eng, fn, reads=(), writes=(), dma=False, key=None, is_load=False):
        op = Op(eng, fn, dma, key)
        op.idx = self.n
        self.n += 1
        deps = {}
        for b in reads:
            if b.last_w is not None:
                deps[id(b.last_w)] = b.last_w
        for b in writes:
            if b.last_w is not None:
                deps[id(b.last_w)] = b.last_w
            for r in b.readers.values():
                deps[id(r)] = r
        if dma:
            assert key is not None
            prev = self.dma_keys.get(key)
            if prev is not None:
                deps[id(prev)] = prev
            self.dma_keys[key] = op
            si = self.key_slot.get(key)
            if si is None:
                if self.free:
                    si = self.free.pop(0)
                else:
                    si = len(self.slots)
                    self.slots.append({"count": 0, "last_op": None, "sem": None})
                self.key_slot[key] = si
            sl = self.slots[si]
            sl["count"] += 1
            sl["last_op"] = op
            op.key = sl
            op.sigval = 16 * sl["count"]
            op.signal = True
        deps.pop(id(op), None)
        for d in deps.values():
            if d.is_dma:
                d.signal = True
            elif d.eng != eng or SAME_ENGINE_SYNC or dma:
                if d.eng == eng and not dma and eng == "pe":
                    continue
                d.signal = True
            else:
                continue
            op.deps.append(d)
        need = self.fence_pos
        for d in deps.values():
            v = d.sp_pos if d.eng == "sp" else d.sp_need
            if v > need:
                need = v
            if d.idx > op.max_dep:
                op.max_dep = d.idx
        op.sp_need = need
        op.is_load = is_load
        if eng == "sp":
            op.sp_pos = len(self.ops[eng])
            if fn is None:
                op.fence = True
        for b in reads:
            k = ("dma", id(op)) if dma else eng
            b.readers[k] = op
        for b in writes:
            b.last_w = op
            b.readers = {}
        self.ops[eng].append(op)
        return op

    def emit(self, nc, es):
        eng_sem = {}
        for e in self.ENGS:
            eng_sem[e] = es.enter_context(nc.semaphore("s_" + e))
        if HOIST_WINDOW > 0:
            newl = []
            nh = 0
            for op in self.ops["sp"]:
                if not op.is_load:
                    newl.append(op)
                    continue
                j = len(newl)
                steps = 0
                while j > 0 and steps < HOIST_WINDOW:
                    e = newl[j - 1]
                    if e.is_load or e.fence or e.sp_pos <= op.sp_need or op.max_dep >= e.idx:
                        break
                    j -= 1
                    steps += 1
                if j < len(newl):
                    nh += 1
                newl.insert(j, op)
            self.ops["sp"] = newl
            print("sched: hoisted %d loads" % nh)
        for i, sl in enumerate(self.slots):
            sl["sem"] = es.enter_context(nc.semaphore("d_%d" % i))
        print("sched: %d dma sem slots, ops per engine: %s" % (len(self.slots), {e: len(v) for e, v in self.ops.items()}))
        for e in self.ENGS:
            c = 0
            for op in self.ops[e]:
                if op.is_dma:
                    op.sem = op.key["sem"]
                    continue
                if op.signal:
                    c += 1
                    op.sigval = c
                    op.sem = eng_sem[e]
        block = es.enter_context(nc.Block())
        sched = self

        def run(engname):
            def body(eng):
                waited = {}
                for op in sched.ops[engname]:
                    need = {}
                    for d in op.deps:
                        sid = id(d.sem)
                        if need.get(sid, (None, 0))[1] < d.sigval:
                            need[sid] = (d.sem, d.sigval)
                    for sid, (sem, val) in need.items():
                        if waited.get(sid, 0) < val:
                            eng.wait_ge(sem, val)
                            waited[sid] = val
                    if op.fn is None:
                        continue
                    ins = op.fn(eng)
                    if op.signal:
                        ins.then_inc(op.sem, 16 if op.is_dma else 1)
            return body

        block.tensor(run("pe"))
        block.scalar(run("act"))
        block.vector(run("dve"))
        block.gpsimd(run("pool"))
        block.sync(run("sp"))


def rr(lst, i):
    return lst[i % len(lst)]


ARENA_W = 206 * 256


class K:
    def __init__(self, nc, es):
        self.nc = nc
        self.es = es
        self.s = Sched()
        self.arena = es.enter_context(nc.sbuf_tensor("arena", [128, ARENA_W], F32))
        self.top = 0
        self.psum = es.enter_context(nc.psum_tensor("psum", [128, 4096], F32))
        self.pb = [Buf("ps%d" % i) for i in range(8)]

    def alloc(self, free_shape, dtype):
        esz = 4 if dtype == F32 else 2
        n = 1
        for d in free_shape:
            n *= d
        nbytes = (n * esz + 63) // 64 * 64
        off = self.top
        self.top += nbytes
        assert self.top <= ARENA_W * 4, "SBUF arena overflow %d" % self.top
        ap = self.arena[:, off // 4: off // 4 + nbytes // 4]
        if dtype == BF16:
            ap = ap.bitcast(BF16)
        ap = ap[:, 0:n]
        if len(free_shape) == 2:
            ap = ap.rearrange("p (a b) -> p a b", a=free_shape[0])
        elif len(free_shape) == 3:
            ap = ap.rearrange("p (a b c) -> p a b c", a=free_shape[0], b=free_shape[1])
        return ap, Buf()

    def bank(self, i, n=1, dtype=F32):
        ap = self.psum[:, i * 512:(i + n) * 512]
        if dtype == BF16:
            ap = ap.bitcast(BF16)
        return ap

    def barrier(self):
        self.s.barrier()

    def dma(self, out, in_, reads, writes, key, eng="sp"):
        is_load = type(out.tensor).__name__.startswith("SB") and not type(in_.tensor).__name__.startswith("SB")
        return self.s.add(eng, lambda e: e.dma_start(out=out, in_=in_), reads, writes, dma=True, key=key,
                          is_load=is_load)

    def mm(self, out, lhsT, rhs, start, stop, reads, writes):
        return self.s.add("pe", lambda e: e.matmul(out, lhsT, rhs, start=start, stop=stop), reads, writes)

    def tr(self, out, in_, ident, reads, writes):
        return self.s.add("pe", lambda e: e.transpose(out, in_, ident), reads, writes)

    def act(self, out, in_, func, reads, writes, bias=0.0, scale=1.0):
        return self.s.add("act", lambda e: e.activation(out, in_, func, bias=bias, scale=scale), reads, writes)

    def tt(self, eng, out, in0, in1, op, reads, writes):
        return self.s.add(eng, lambda e: e.tensor_tensor(out, in0, in1, op), reads, writes)

    def ts(self, eng, out, in0, s1, s2, op0, op1, reads, writes):
        if s2 is None:
            return self.s.add(eng, lambda e: e.tensor_scalar(out, in0, s1, None, op0), reads, writes)
        return self.s.add(eng, lambda e: e.tensor_scalar(out, in0, s1, s2, op0, op1), reads, writes)

    def stt(self, out, in0, scalar, in1, op0, op1, reads, writes):
        return self.s.add("dve", lambda e: e.scalar_tensor_tensor(out, in0, scalar, in1, op0, op1), reads, writes)

    def copy(self, eng, out, in_, reads, writes):
        if eng == "act":
            return self.s.add("act", lambda e: e.copy(out, in_), reads, writes)
        return self.s.add(eng, lambda e: e.tensor_copy(out, in_), reads, writes)

    def memset(self, eng, ap, val, writes):
        return self.s.add(eng, lambda e: e.memset(ap, val), (), writes)

    def load_cast(self, dst, dstb, src, stg, eng="pool"):
        sa, sbuf_ = stg
        self.dma(sa, src, (), (sbuf_,), key=sbuf_)
        self.copy(eng, dst, sa, (sbuf_,), (dstb,))


class LNRes:
    def __init__(self, k, ident):
        self.k = k
        self.ident = ident
        NB = 3
        self.stats = [k.alloc([2, 6], F32) for i in range(NB)]
        self.mv = [k.alloc([2], F32) for i in range(NB)]
        self.rstd = [k.alloc([1], F32) for i in range(NB)]
        self.t1 = [k.alloc([D], F32) for i in range(2)]
        self.h = [k.alloc([D], F32) for i in range(2)]
        self.hb = [k.alloc([D], BF16) for i in range(NB)]
        self.hT = [k.alloc([8, 128], BF16) for i in range(2)]
        self.g = k.alloc([D], F32)
        self.b = k.alloc([D], F32)
        self.cnt = 0
        self.pb_ = []
        self.pc_ = []

    def load_gb(self, g_dram, b_dram):
        k = self.k
        self.flush()
        k.dma(self.g[0], g_dram.partition_broadcast(128), (), (self.g[1],), key=self.g[1])
        k.dma(self.b[0], b_dram.partition_broadcast(128), (), (self.b[1],), key=self.b[1])

    def _a(self, j):
        k = self.k
        i, r, rbuf = j["i"], j["r"], j["rbuf"]
        st, stb = rr(self.stats, i)
        mv, mvb = rr(self.mv, i)
        rs, rsb = rr(self.rstd, i)
        for q in range(2):
            k.s.add("dve", lambda e, q=q: e.bn_stats(st[:, q, :], r[:, q * 512:(q + 1) * 512]), (rbuf,), (stb,))
        k.s.add("dve", lambda e: e.bn_aggr(mv, st), (stb,), (mvb,))
        k.act(rs, mv[:, 1:2], AF.Sqrt, (mvb,), (rsb,), bias=EPS)
        k.s.add("dve", lambda e: e.reciprocal(rs, rs), (rsb,), (rsb,))

    def _b(self, j):
        k = self.k
        i, r, rbuf = j["i"], j["r"], j["rbuf"]
        mv, mvb = rr(self.mv, i)
        rs, rsb = rr(self.rstd, i)
        t1, t1b = rr(self.t1, i)
        h, hb_ = rr(self.h, i)
        hb, hbb = rr(self.hb, i)
        k.stt(t1, r, mv[:, 0:1], self.g[0], ALU.subtract, ALU.mult, (rbuf, mvb, self.g[1]), (t1b,))
        k.stt(h, t1, rs[:, 0:1], self.b[0], ALU.mult, ALU.add, (t1b, rsb, self.b[1]), (hb_,))
        k.dma(j["H"], h, (hb_,), (j["hbuf"],), key=hb_)
        k.copy("act", hb, h, (hb_,), (hbb,))

    def _c(self, j):
        k = self.k
        i = j["i"]
        hb, hbb = rr(self.hb, i)
        hT, hTb = rr(self.hT, i)
        pst = k.bank(j["bank"], 1, BF16)
        pstb = k.pb[j["bank"]]
        for c in range(8):
            k.tr(pst[:, c * 128:(c + 1) * 128], hb[:, c * 128:(c + 1) * 128], self.ident[0],
                 (hbb, self.ident[1]), (pstb,))
        k.copy("act", hT.rearrange("p c t -> p (c t)"), pst, (pstb,), (hTb,))
        k.dma(j["HT"], hT, (hTb,), (j["htbuf"],), key=hTb)

    def apply(self, r, rbuf, H_dram_tile, HT_dram_tile, tp_bank, hbuf_dram, htbuf_dram):
        j = dict(i=self.cnt, r=r, rbuf=rbuf, H=H_dram_tile, HT=HT_dram_tile, bank=tp_bank, hbuf=hbuf_dram,
                 htbuf=htbuf_dram)
        self.cnt += 1
        self._a(j)
        if self.pc_:
            self._c(self.pc_.pop(0))
        if self.pb_:
            jb = self.pb_.pop(0)
            self._b(jb)
            self.pc_.append(jb)
        self.pb_.append(j)

    def flush(self):
        while self.pb_ or self.pc_:
            if self.pc_:
                self._c(self.pc_.pop(0))
            if self.pb_:
                jb = self.pb_.pop(0)
                self._b(jb)
                self.pc_.append(jb)


def build(nc, n_stages=99, debug=False, only=None):
    es = ExitStack()
    k = K(nc, es)

    def din(name, shape, dt=F32):
        return nc.dram_tensor(name, shape, dt, kind="ExternalInput").ap()

    def scratch(name, shape, dt):
        if debug:
            return nc.dram_tensor("dbg_" + name, shape, dt, kind="ExternalOutput").ap()
        return nc.dram_tensor(name, shape, dt).ap()

    x = din("x", [S, D])
    emb_g = din("emb_ln_g", [D]); emb_b = din("emb_ln_b", [D])
    fn_w_in = din("fn_w_in", [D, D]); fn_b_in = din("fn_b_in", [1, D])
    fn_w_out = din("fn_w_out", [D, D]); fn_b_out = din("fn_b_out", [1, D])
    ln_tok_g = din("ln_tok_g", [2, D]); ln_tok_b = din("ln_tok_b", [2, D])
    ff_w_up = din("ff_w_up", [2, D, 2 * D_FF]); ff_cw = din("ff_cwp", [2, 128, 176])
    ff_w_down = din("ff_w_down", [2, D_FF, D])
    ln_ffn_g = din("ln_ffn_g", [2, D]); ln_ffn_b = din("ln_ffn_b", [2, D])
    ssd_w_in = din("ssd_w_in", [D, D_IN_PROJ]); ssd_cwp = din("ssd_cwp", [128, 192])
    ssd_vec = din("ssd_vec", [160]); ssd_norm_g = din("ssd_norm_g", [2048]); ssd_w_out = din("ssd_w_out", [2048, D])
    tri_d = din("c_tri", [128, 512], F32)
    ident_d = din("c_ident", [128, 128], BF16)
    identf_d = din("c_identf", [128, 128], F32)
    cs128_d = din("c_cs128", [128, 256], BF16)
    ma_d = din("c_ma", [128, 64 * 128], BF16)
    bb_d = din("c_bb", [128, 64], BF16)
    y = nc.dram_tensor("y", [S, D], F32, kind="ExternalOutput").ap()

    H = [scratch("H%d" % i, [S, D], F32) for i in range(5)]
    HT = [nc.dram_tensor("HT%d" % i, [128, 8, S], BF16).ap() for i in range(5)]
    Hb = [Buf("b") for _ in range(5)]
    HTb = [Buf("b") for _ in range(5)]
    Gs = nc.dram_tensor("Gs", [64, 64, 2, 1024], BF16).ap()
    Gsb = Buf("b")
    Ys = nc.dram_tensor("Ys", [64, 128, 1024], BF16).ap()
    Ysb = Buf("b")
    Xs = nc.dram_tensor("Xs", [S, 2048], BF16).ap(); Xsb = Buf()
    Bs = nc.dram_tensor("Bs", [S, 1024], BF16).ap(); Bsb = Buf()
    BTs = nc.dram_tensor("BTs", [128, 8, S], BF16).ap(); BTsb = Buf()
    CTs = nc.dram_tensor("CTs", [128, 8, S], BF16).ap(); CTsb = Buf()
    Zs = nc.dram_tensor("Zs", [S, 2048], F32).ap(); Zsb = Buf()
    DTs = nc.dram_tensor("DTs", [S, 5, 64], F32).ap(); DTsb = Buf()
    ACs = nc.dram_tensor("ACs", [NT, 2, 32, 128], F32).ap(); ACsb = Buf()
    Yf = scratch("Yf", [S, 2048], F32); Yfb = Buf()
    Yb = scratch("Yb", [S, 2048], F32); Ybb = Buf()

    ident = k.alloc([128], BF16)
    k.dma(ident[0], ident_d, (), (ident[1],), key=ident[1])
    identf = k.alloc([128], F32)
    k.dma(identf[0], identf_d, (), (identf[1],), key=identf[1])
    ones = k.alloc([512], BF16)
    k.memset("pool", ones[0], 1.0, (ones[1],))
    ln = LNRes(k, ident)
    persist_top = k.top
    stage = [0]

    def next_stage():
        ln.flush()
        k.barrier()
        k.top = persist_top
        stage[0] += 1
        if only is not None:
            return stage[0] in only
        return stage[0] <= n_stages

    WbUp = [nc.dram_tensor("WbUp%d" % i, [44, 128, 1024], BF16).ap() for i in range(2)]
    WbUpb = [Buf(), Buf()]
    WbIn = nc.dram_tensor("WbIn", [32, 128, 1024], BF16).ap()
    WbInb = Buf()

    def prologue(which):
        stg = [k.alloc([8, 512], F32) for _ in range(3)]
        outb = [k.alloc([8, 512], BF16) for _ in range(3)]
        jobs = []
        if 0 in which:
            wv0 = ff_w_up[0].rearrange("(kc p) m -> p kc m", p=128)
            for pc in range(11):
                jobs.append((wv0[:, :, pc * 512:(pc + 1) * 512], WbUp[0][pc * 4:(pc + 1) * 4], WbUpb[0]))
        if 1 in which:
            wvi = ssd_w_in.rearrange("(kc p) m -> p kc m", p=128)
            for pc in range(8):
                jobs.append((wvi[:, :, 2048 + pc * 512:2048 + (pc + 1) * 512], WbIn[pc * 4:(pc + 1) * 4], WbInb))
            wv1 = ff_w_up[1].rearrange("(kc p) m -> p kc m", p=128)
            for pc in range(11):
                jobs.append((wv1[:, :, pc * 512:(pc + 1) * 512], WbUp[1][pc * 4:(pc + 1) * 4], WbUpb[1]))
        def job(i, src_, dst_, dbuf):
            s_, sb_ = rr(stg, i)
            o_, ob_ = rr(outb, i)
            k.dma(s_, src_, (), (sb_,), key=sb_)
            k.copy("act" if i % 2 == 0 else "dve", o_, s_, (sb_,), (ob_,))
            for c4 in range(4):
                k.dma(dst_[c4].rearrange("p (k m) -> p k m", k=8), o_[:, :, c4 * 128:(c4 + 1) * 128],
                      (ob_,), (dbuf,), key=(id(ob_), c4))
        return [(lambda i=i, jb=jb: job(i, *jb)) for i, jb in enumerate(jobs)]

    pro_jobs = prologue((0,)) if (only is None or 0 in only) else []
    ln.load_gb(emb_g, emb_b)
    xin = [k.alloc([D], F32) for i in range(3)]
    for t in range(NT if (only is None or 0 in only) else 0):
        xt, xb = rr(xin, t)
        k.dma(xt, x[t * 128:(t + 1) * 128, :], (), (xb,), key=xb)
        ln.apply(xt, xb, H[0][t * 128:(t + 1) * 128, :], HT[0][:, :, t * 128:(t + 1) * 128], t % 2, Hb[0], HTb[0])
        if pro_jobs:
            pro_jobs.pop(0)()
    while pro_jobs:
        pro_jobs.pop(0)()
    cur = 0

    if next_stage():
        hT = k.alloc([8, S], BF16)
        for kc in range(8):
            k.dma(hT[0][:, kc, :], HT[0][:, kc, :], (HTb[0],), (hT[1],), key=("hTl", kc % 4))
        stg = k.alloc([4, 1024], F32)
        w = k.alloc([8, 1024], BF16)
        wv = fn_w_in.rearrange("(kc p) m -> p kc m", p=128)
        for hf in range(2):
            k.load_cast(w[0][:, hf * 4:(hf + 1) * 4, :], w[1], wv[:, hf * 4:(hf + 1) * 4, :], stg,
                        eng="act" if hf == 0 else "dve")
        brow_f = k.alloc([1024], F32)
        brow = k.alloc([1024], BF16)
        k.dma(brow_f[0][0:1, :], fn_b_in, (), (brow_f[1],), key=brow_f[1])
        k.copy("pool", brow[0][0:1, :], brow_f[0][0:1, :], (brow_f[1],), (brow[1],))
        cs = k.alloc([256], BF16)
        k.dma(cs[0], cs128_d, (), (cs[1],), key=cs[1])
        h1 = [k.alloc([8, 512], BF16) for _ in range(2)]
        Gt = [k.alloc([2, 8, 128], BF16) for _ in range(2)]
        gi = 0
        for tb in range(8):
            h1a, h1b = rr(h1, tb)
            for g in range(8):
                bk = g % 2
                ps = k.bank(bk)
                for kc in range(8):
                    k.mm(ps, w[0][:, kc, g * 128:(g + 1) * 128], hT[0][:, kc, tb * 512:(tb + 1) * 512],
                         kc == 0, False, (w[1], hT[1]), (k.pb[bk],))
                k.mm(ps, brow[0][0:1, g * 128:(g + 1) * 128], ones[0][0:1, :], False, True,
                     (brow[1], ones[1]), (k.pb[bk],))
                k.copy("act" if g % 2 == 0 else "dve", h1a[:, g, :], ps, (k.pb[bk],), (h1b,))
            for tt_ in range(4):
                t = tb * 4 + tt_
                gt, gtb = rr(Gt, gi)
                for hf in range(2):
                    b0 = 2 + 2 * (gi % 2) if False else 2 + 2 * ((2 * gi + hf) % 3)
                    psg = k.bank(b0, 2)
                    for gg in range(4):
                        g = hf * 4 + gg
                        k.mm(psg[:, gg * 256:(gg + 1) * 256], h1a[:, g, tt_ * 128:(tt_ + 1) * 128], cs[0],
                             True, True, (h1b, cs[1]), (k.pb[b0], k.pb[b0 + 1]))
                    psv = psg.rearrange("p (g r j) -> p g r j", g=4, r=2)
                    for ri in range(2):
                        k.copy("act" if ri == 0 else "dve", gt[:, ri, hf * 4:(hf + 1) * 4, :], psv[:, :, ri, :],
                               (k.pb[b0], k.pb[b0 + 1]), (gtb,))
                k.dma(Gs[2 * t:2 * t + 2].rearrange("a b r c -> (a b) r c"), gt.rearrange("p r g j -> p r (g j)"),
                      (gtb,), (Gsb,), key=gtb)
                gi += 1

    if next_stage():
        ma = k.alloc([64, 128], BF16)
        k.dma(ma[0].rearrange("p a b -> p (a b)"), ma_d, (), (ma[1],), key=ma[1])
        NZ = 4
        Z = [k.alloc([1024], BF16) for _ in range(NZ)]
        Yt = [k.alloc([1024], BF16) for _ in range(NZ)]
        late_jobs = prologue((1,))
        for t2 in range(64):
            if t2 % 3 == 1 and late_jobs:
                late_jobs.pop(0)()
            z, zb = rr(Z, t2)
            yt, ytb = rr(Yt, t2)
            for ri in range(2):
                k.dma(z[ri * 64:(ri + 1) * 64, :], Gs[:, t2, ri, :], (Gsb,), (zb,), key=(id(zb), ri))
            for hf in range(2):
                bk = (2 * t2 + hf) % 8
                ps = k.bank(bk)
                k.mm(ps, ma[0][:, t2, :], z[:, hf * 512:(hf + 1) * 512], True, True, (ma[1], zb), (k.pb[bk],))
                k.copy("act" if hf == 0 else "dve", yt[:, hf * 512:(hf + 1) * 512], ps, (k.pb[bk],), (ytb,))
            k.dma(Ys[t2], yt, (ytb,), (Ysb,), key=ytb)
        while late_jobs:
            late_jobs.pop(0)()

    if next_stage():
        bb = k.alloc([64], BF16)
        k.dma(bb[0], bb_d, (), (bb[1],), key=bb[1])
        fT = k.alloc([8, S], BF16)
        stg = k.alloc([4, 1024], F32)
        w = k.alloc([8, 1024], BF16)
        wv = fn_w_out.rearrange("(kc p) m -> p kc m", p=128)
        for hf in range(2):
            k.load_cast(w[0][:, hf * 4:(hf + 1) * 4, :], w[1], wv[:, hf * 4:(hf + 1) * 4, :], stg,
                        eng="act" if hf == 0 else "dve")
        brow_f = k.alloc([1024], F32)
        brow = k.alloc([1024], BF16)
        k.dma(brow_f[0][0:1, :], fn_b_out, (), (brow_f[1],), key=brow_f[1])
        k.copy("pool", brow[0][0:1, :], brow_f[0][0:1, :], (brow_f[1],), (brow[1],))
        ln.load_gb(ln_tok_g[0], ln_tok_b[0])
        NZ = 4
        YY = [k.alloc([1024], BF16) for _ in range(NZ)]
        for k1 in range(64):
            yy, yyb = rr(YY, k1)
            for ro in range(2):
                k.dma(yy[ro * 64:(ro + 1) * 64, :], Ys[:, ro * 64 + k1, :], (Ysb,), (yyb,), key=(id(yyb), ro))
            bk = k1 % 4
            ps = k.bank(bk)
            for cc in range(8):
                k.mm(ps[:, cc * 64:(cc + 1) * 64], yy[:, cc * 128:(cc + 1) * 128], bb[0], True, True,
                     (yyb, bb[1]), (k.pb[bk],))
            k.copy("act" if k1 % 2 == 0 else "dve", fT[0][:, :, k1::64], ps.rearrange("p (c k) -> p c k", c=8),
                   (k.pb[bk],), (fT[1],))
        h0t = [k.alloc([D], F32) for _ in range(2)]
        rt = [k.alloc([D], F32) for _ in range(2)]
        for t in range(NT):
            ht, htb = rr(h0t, t)
            r, rb = rr(rt, t)
            k.dma(ht, H[0][t * 128:(t + 1) * 128, :], (Hb[0],), (htb,), key=htb)
            b0 = 4 + 2 * (t % 2)
            ps = k.bank(b0, 2)
            for hf in range(2):
                for cc in range(8):
                    k.mm(ps[:, hf * 512:(hf + 1) * 512], fT[0][:, cc, t * 128:(t + 1) * 128],
                         w[0][:, cc, hf * 512:(hf + 1) * 512], cc == 0, False, (fT[1], w[1]), (k.pb[b0 + hf],))
                k.mm(ps[:, hf * 512:(hf + 1) * 512], ones[0][0:1, 0:128], brow[0][0:1, hf * 512:(hf + 1) * 512],
                     False, True, (ones[1], brow[1]), (k.pb[b0 + hf],))
            k.stt(r, ht, ALPHA, ps, ALU.mult, ALU.add, (htb, k.pb[b0], k.pb[b0 + 1]), (rb,))
            ln.apply(r, rb, H[1][t * 128:(t + 1) * 128, :], HT[1][:, :, t * 128:(t + 1) * 128], t % 2, Hb[1], HTb[1])
        cur = 1

    def ffn_stage(layer, src, dst):
        TB = 512
        cw = k.alloc([44, 4], F32)
        k.dma(cw[0].rearrange("p c k -> p (c k)"), ff_cw[layer], (), (cw[1],), key=cw[1])
        wd = k.alloc([NFC, D], BF16)
        stg = [k.alloc([2, D], F32) for _ in range(2)]
        wdv = ff_w_down[layer].rearrange("(fc p) m -> p fc m", p=128)
        for i in range(NFC // 2):
            k.load_cast(wd[0][:, 2 * i:2 * i + 2, :], wd[1], wdv[:, 2 * i:2 * i + 2, :], rr(stg, i),
                        eng="act" if i % 2 == 0 else "dve")
        ln.load_gb(ln_ffn_g[layer], ln_ffn_b[layer])
        hblk = [k.alloc([8, TB + 2], BF16) for _ in range(2)]
        aT = [k.alloc([NFC, TB], BF16) for _ in range(1)]
        wbf = [k.alloc([2, 8, 128], BF16) for _ in range(4)]
        acc = [[k.alloc([TB], F32) for _ in range(3)] for _ in range(2)]
        pend_gate = []
        sg = [k.alloc([TB], F32) for _ in range(2)]
        h0t = [k.alloc([D], F32) for _ in range(3)]
        rt = [k.alloc([D], F32) for _ in range(3)]
        it = 0
        pend_ln = []

        def flush_ln():
            while pend_ln:
                pend_ln.pop(0)()

        for sbk in range(S // TB):
            T0 = sbk * TB
            hb, hbb = rr(hblk, sbk)
            lo = max(T0 - 1, 0); hi = min(T0 + TB + 1, S)
            if sbk == 0:
                k.memset("pool", hb[:, :, 0:1], 0.0, (hbb,))
            if sbk == S // TB - 1:
                k.memset("pool", hb[:, :, TB + 1:TB + 2], 0.0, (hbb,))
            for kc in range(8):
                k.dma(hb[:, kc, lo - (T0 - 1):hi - (T0 - 1)], HT[src][:, kc, lo:hi], (HTb[src],), (hbb,),
                      key=(id(hbb), kc % 4))
            a, ab = rr(aT, sbk)
            for fc in range(NFC):
                wb, wbb = rr(wbf, it)
                for gv in range(2):
                    k.dma(wb[:, gv], WbUp[layer][gv * NFC + fc].rearrange("p (k m) -> p k m", k=8),
                          (WbUpb[layer],), (wbb,), key=(id(wbb), gv))
                accs = []
                for gv in range(2):
                    b0 = 4 * (it % 2) + 2 * gv
                    ps = k.bank(b0, 2)
                    for (c0, c1, bk) in ((0, 512, b0), (512, 514, b0 + 1)):
                        for kc in range(8):
                            k.mm(ps[:, c0:c1], wb[:, gv, kc, :], hb[:, kc, c0:c1],
                                 kc == 0, kc == 7, (wbb, hbb), (k.pb[bk],))
                    ac, acb = acc[gv][it % 3]
                    ch = gv * NFC + fc
                    pbs = (k.pb[b0], k.pb[b0 + 1])
                    k.act(ac, ps[:, 1:TB + 1], AF.Identity, pbs + (cw[1],), (acb,),
                          bias=cw[0][:, ch, 3:4], scale=cw[0][:, ch, 1:2])
                    if gv == 1:
                        while pend_gate:
                            pend_gate.pop(0)()
                    k.stt(ac, ps[:, 0:TB], cw[0][:, ch, 0:1], ac, ALU.mult, ALU.add, pbs + (cw[1], acb), (acb,))
                    k.stt(ac, ps[:, 2:TB + 2], cw[0][:, ch, 2:3], ac, ALU.mult, ALU.add, pbs + (cw[1], acb), (acb,))
                    accs.append((ac, acb))
                def gate(accs=accs, it=it, fc=fc, a=a, ab=ab):
                    s_, sb_ = rr(sg, it)
                    k.act(s_, accs[0][0], AF.Silu, (accs[0][1],), (sb_,))
                    k.tt("pool", a[:, fc, :], s_, accs[1][0], ALU.mult, (sb_, accs[1][1]), (ab,))
                pend_gate.append(gate)
                it += 1
                if fc == 1:
                    flush_ln()
            while pend_gate:
                pend_gate.pop(0)()
            for tt_ in range(TB // 128):
                t = sbk * (TB // 128) + tt_
                ht, htb = rr(h0t, t)
                r, rb = rr(rt, t)
                k.dma(ht, H[src][t * 128:(t + 1) * 128, :], (Hb[src],), (htb,), key=htb)
                b0 = 2 * (tt_ % 2) + (0 if (it % 2 == 0) else 4)
                ps = k.bank(b0, 2)
                for hf in range(2):
                    for fc in range(NFC):
                        k.mm(ps[:, hf * 512:(hf + 1) * 512], a[:, fc, tt_ * 128:(tt_ + 1) * 128],
                             wd[0][:, fc, hf * 512:(hf + 1) * 512], fc == 0, fc == NFC - 1, (ab, wd[1]),
                             (k.pb[b0 + hf],))
                k.stt(r, ht, ALPHA, ps, ALU.mult, ALU.add, (htb, k.pb[b0], k.pb[b0 + 1]), (rb,))
                flush_ln()
                tpb = (b0 + 4) % 8
                pend_ln.append(lambda r=r, rb=rb, t=t, tpb=tpb: ln.apply(
                    r, rb, H[dst][t * 128:(t + 1) * 128, :], HT[dst][:, :, t * 128:(t + 1) * 128], tpb,
                    Hb[dst], HTb[dst]))
        flush_ln()

    if next_stage():
        ffn_stage(0, 1, 2)
        cur = 2

    def bc(ap, dims, off=0):
        return bass.AP(ap.tensor, ap.offset + off, [list(ap.ap[0])] + [list(d) for d in dims])

    NSL = 5

    def ssd_s1(src):
        TB = 512
        wv = ssd_w_in.rearrange("(kc p) m -> p kc m", p=128)
        tri = k.alloc([4, 128], F32)
        k.dma(tri[0].rearrange("p a b -> p (a b)"), tri_d, (), (tri[1],), key=tri[1])
        tri16 = k.alloc([4, 128], BF16)
        k.copy("dve", tri16[0], tri[0], (tri[1],), (tri16[1],))
        vec = k.alloc([160], F32)
        k.dma(vec[0], ssd_vec.partition_broadcast(128), (), (vec[1],), key=vec[1])
        abc_ = k.alloc([64], F32)
        k.act(abc_[0], vec[0][:, 64:128], AF.Exp, (vec[1],), (abc_[1],))
        k.ts("dve", abc_[0], abc_[0], -1.0, None, ALU.mult, None, (abc_[1],), (abc_[1],))
        cwp = k.alloc([32, 6], F32)
        k.dma(cwp[0].rearrange("p c k -> p (c k)"), ssd_cwp, (), (cwp[1],), key=cwp[1])
        wz = k.alloc([8, 2048], BF16)
        wdt = k.alloc([8, 64], BF16)
        stg = [k.alloc([8, 512], F32) for _ in range(1)]
        uS = [k.alloc([TB + 4], F32) for _ in range(3)]
        for q in range(4):
            k.load_cast(wz[0][:, :, q * 512:(q + 1) * 512], wz[1], wv[:, :, q * 512:(q + 1) * 512], rr(stg, q),
                        eng="act" if q % 2 == 0 else "dve")
        k.load_cast(wdt[0], wdt[1], wv[:, :, 6144:6208], (stg[0][0][:, :, 0:64], stg[0][1]), eng="dve")
        hblk = [k.alloc([8, TB + 4], BF16) for _ in range(2)]
        wbf = [k.alloc([8, 128], BF16) for _ in range(4)]
        acc = [k.alloc([TB], F32) for _ in range(3)]
        xo = [k.alloc([TB], BF16) for _ in range(4)]
        Xblk = [k.alloc([4, 2048], BF16) for _ in range(1)]
        Bblk = [k.alloc([4, 1024], BF16) for _ in range(1)]
        zt = [k.alloc([2048], F32) for _ in range(2)]
        sm = k.alloc([4, 8, 64], F32)
        csb = k.alloc([4, 4, 32], F32)
        hl = k.alloc([2, 4, 64], BF16)
        DT = k.alloc([4, NSL, 64], F32)
        actT = k.alloc([1024], F32)
        it = 0
        for tb in range(S // TB):
            T0 = tb * TB
            hb, hbb = rr(hblk, tb)
            lo = max(T0 - 2, 0); hi = min(T0 + TB + 2, S)
            if tb == 0:
                k.memset("pool", hb[:, :, 0:2], 0.0, (hbb,))
            if tb == S // TB - 1:
                k.memset("pool", hb[:, :, TB + 2:TB + 4], 0.0, (hbb,))
            for kc in range(8):
                k.dma(hb[:, kc, lo - (T0 - 2):hi - (T0 - 2)], HT[src][:, kc, lo:hi], (HTb[src],), (hbb,),
                      key=(id(hbb), kc % 4))
            Xb_, Xbb = rr(Xblk, tb)
            Bb_, Bbb = rr(Bblk, tb)
            pend_a = []
            pend_b = []
            for ch in range(32):
                wb, wbb = rr(wbf, it)
                k.dma(wb, WbIn[ch].rearrange("p (k m) -> p k m", k=8), (WbInb,), (wbb,), key=wbb)
                b0 = (0, 2, 5)[it % 3]
                ps = k.bank(b0, 2)
                for (c0, c1, bk) in ((0, 512, b0), (512, 516, b0 + 1)):
                    for kc in range(8):
                        k.mm(ps[:, c0:c1], wb[:, kc, :], hb[:, kc, c0:c1], kc == 0, kc == 7, (wbb, hbb), (k.pb[bk],))
                ac, acb = rr(acc, it)
                us_, usb = rr(uS, it)
                pbs = (k.pb[b0], k.pb[b0 + 1])
                k.copy("act", us_, ps[:, 0:TB + 4], pbs, (usb,))
                k.act(ac, us_[:, 2:TB + 2], AF.Identity, (usb, cwp[1]), (acb,),
                      bias=cwp[0][:, ch, 5:6], scale=cwp[0][:, ch, 2:3])
                while pend_a:
                    pend_a.pop(0)()
                for tap in (0, 1, 3, 4):
                    k.stt(ac, us_[:, tap:TB + tap], cwp[0][:, ch, tap:tap + 1], ac, ALU.mult, ALU.add,
                          (usb, cwp[1], acb), (acb,))
                while len(pend_b) > 1:
                    pend_b.pop(0)()
                o, ob = rr(xo, it)

                def tail_a(ch=ch, o=o, ob=ob, T0=T0, ac=ac, acb=acb):
                    k.act(o, ac, AF.Silu, (acb,), (ob,))
                    if 16 <= ch < 24:
                        k.dma(BTs[:, ch - 16, T0:T0 + TB], o, (ob,), (BTsb,), key=ob)
                    if ch >= 24:
                        k.dma(CTs[:, ch - 24, T0:T0 + TB], o, (ob,), (CTsb,), key=ob)

                def tail(ch=ch, o=o, ob=ob, T0=T0, ac=ac, acb=acb):
                    if ch < 24:
                        pst = k.bank(4, 1, BF16)
                        for tt_ in range(4):
                            k.tr(pst[:, tt_ * 128:(tt_ + 1) * 128], o[:, tt_ * 128:(tt_ + 1) * 128], ident[0],
                                 (ob, ident[1]), (k.pb[4],))
                        if ch < 16:
                            k.copy("act", Xb_[:, :, ch * 128:(ch + 1) * 128],
                                   pst[:, 0:512].rearrange("p (t c) -> p t c", t=4), (k.pb[4],), (Xbb,))
                        else:
                            g = ch - 16
                            k.copy("act", Bb_[:, :, g * 128:(g + 1) * 128],
                                   pst[:, 0:512].rearrange("p (t c) -> p t c", t=4), (k.pb[4],), (Bbb,))
                pend_a.append(tail_a)
                pend_b.append(tail)
                it += 1
            for tt_ in range(4):
                t = tb * 4 + tt_
                tok = slice(2 + tt_ * 128, 2 + (tt_ + 1) * 128)
                z_, zb_ = rr(zt, t)
                for q in range(4):
                    bk = 1 + 2 * (q % 2)
                    ps = k.bank(bk)
                    for kc in range(8):
                        k.mm(ps, hb[:, kc, tok], wz[0][:, kc, q * 512:(q + 1) * 512], kc == 0, kc == 7,
                             (hbb, wz[1]), (k.pb[bk],))
                    if tt_ == 0 and q == 0:
                        while pend_a:
                            pend_a.pop(0)()
                    if tt_ == 0 and q == 1:
                        while pend_b:
                            pend_b.pop(0)()
                        k.dma(Xs[T0:T0 + TB, :].rearrange("(t p) c -> p t c", p=128), Xb_, (Xbb,), (Xsb,), key=Xbb)
                        k.dma(Bs[T0:T0 + TB, :].rearrange("(t p) c -> p t c", p=128), Bb_, (Bbb,), (Bsb,), key=Bbb)
                    k.act(z_[:, q * 512:(q + 1) * 512], ps, AF.Silu, (k.pb[bk],), (zb_,))
                k.dma(Zs[t * 128:(t + 1) * 128, :], z_, (zb_,), (Zsb,), key=zb_)
            p7 = k.bank(7)
            pb7 = k.pb[7]
            for tt_ in range(4):
                tok = slice(2 + tt_ * 128, 2 + (tt_ + 1) * 128)
                for kc in range(8):
                    k.mm(p7[:, tt_ * 64:(tt_ + 1) * 64], hb[:, kc, tok], wdt[0][:, kc, :], kc == 0, kc == 7,
                         (hbb, wdt[1]), (pb7,))
            s_, sb_ = sm
            d1 = s_[:, :, 0, :]; dtv = s_[:, :, 1, :]; lndt = s_[:, :, 2, :]; dA = s_[:, :, 3, :]
            tmp = s_[:, :, 4, :]; tmp2 = s_[:, :, 5, :]
            k.tt("dve", d1, p7[:, 0:256].rearrange("p (t c) -> p t c", t=4), bc(vec[0][:, 0:64], [(0, 4), (1, 64)]),
                 ALU.add, (pb7, vec[1]), (sb_,))
            k.act(d1, d1, AF.Exp, (sb_,), (sb_,))
            k.act(dtv, d1, AF.Ln, (sb_,), (sb_,), bias=1.0)
            k.act(lndt, dtv, AF.Ln, (sb_,), (sb_,))
            k.tt("dve", dA, dtv, bc(abc_[0], [(0, 4), (1, 64)]), ALU.mult, (sb_, abc_[1]), (sb_,))
            k.copy("dve", hl[0][:, 0], dA, (sb_,), (hl[1],))
            k.tt("dve", hl[0][:, 1], dA, hl[0][:, 0], ALU.subtract, (sb_, hl[1]), (hl[1],))
            pcs = k.bank(1)
            pT = k.bank(2, 2)
            for tt_ in range(4):
                c0 = tt_ * 128
                for (dst_c, lhs_sel, col_sel) in ((0, 0, slice(0, 32)), (32, 1, slice(32, 64))):
                    for pi in range(2):
                        k.mm(pcs[:, c0 + dst_c:c0 + dst_c + 32], tri16[0][:, lhs_sel, :], hl[0][:, pi, tt_, col_sel],
                             pi == 0, pi == 1, (tri16[1], hl[1]), (k.pb[1],))
                for pi in range(2):
                    k.mm(pcs[:, c0 + 64:c0 + 128], ones[0][:, 0:128], hl[0][:, pi, tt_, :], pi == 0, pi == 1,
                         (ones[1], hl[1]), (k.pb[1],))
                t0c = tt_ * 256
                for (dst_c, col_sel, rsel) in ((0, slice(0, 32), 0), (128, slice(32, 64), 2)):
                    for pi in range(2):
                        k.mm(pT[0:32, t0c + dst_c:t0c + dst_c + 128], hl[0][:, pi, tt_, col_sel], tri16[0][:, rsel, :],
                             pi == 0, pi == 1, (tri16[1], hl[1]), (k.pb[2 + tt_ // 2],))
            k.copy("act", csb[0].rearrange("p t q c -> p (t q c)"), pcs, (k.pb[1],), (csb[1],))
            k.copy("act", actT[0][0:32, :], pT[0:32, :], (k.pb[2], k.pb[3]), (actT[1],))
            k.dma(ACs[tb * 4:(tb + 1) * 4].rearrange("t d h l -> h t d l"),
                  actT[0][0:32, :].rearrange("p (t d l) -> p t d l", t=4, d=2), (actT[1],), (ACsb,), key=actT[1])
            cs_ = csb[0]
            acf = cs_[:, :, 0, :]; ecb = cs_[:, :, 1, :]; totf = cs_[:, :, 2, :]; totb = cs_[:, :, 3, :]
            dt_, dtb_ = DT
            cr = (csb[1],)
            k.copy("dve", dt_[:, :, 0, 0:32], acf, cr, (dtb_,))
            k.ts("dve", dt_[:, :, 0, 32:64], ecb, -1.0, None, ALU.mult, None, cr, (dtb_,))
            k.tt("dve", tmp[:, :, 0:32], totf, acf, ALU.subtract, cr, (sb_,))
            k.tt("dve", tmp[:, :, 0:32], tmp[:, :, 0:32], lndt[:, :, 0:32], ALU.add, (sb_,), (sb_,))
            k.tt("dve", tmp[:, :, 32:64], ecb, lndt[:, :, 32:64], ALU.add, cr + (sb_,), (sb_,))
            k.act(dt_[:, :, 1, :], tmp, AF.Exp, (sb_,), (dtb_,))
            k.copy("dve", tmp2[:, :, 0:32], acf, cr, (sb_,))
            k.tt("dve", tmp2[:, :, 32:64], totb, ecb, ALU.subtract, cr, (sb_,))
            k.act(dt_[:, :, 2, :], tmp2, AF.Exp, (sb_,), (dtb_,))
            k.act(dt_[:, :, 3, :], cs_[:, :, 2:4, :], AF.Exp, cr, (dtb_,))
            k.copy("dve", dt_[:, :, 4, :], dtv, (sb_,), (dtb_,))
            k.dma(DTs[T0:T0 + TB].rearrange("(t p) s c -> p t s c", p=128), dt_, (dtb_,), (DTsb,), key=dtb_)

    def ssd_scan(d):
        tri = k.alloc([4, 128], F32)
        k.dma(tri[0].rearrange("p a b -> p (a b)"), tri_d, (), (tri[1],), key=tri[1])
        mask = tri[0][:, 0, :] if d == 0 else tri[0][:, 3, :]
        Hf = k.alloc([2048], F32)
        Hbf = k.alloc([2048], BF16)
        NB = 3
        Xt = [k.alloc([2048], BF16) for _ in range(NB)]
        BTt = [k.alloc([8, 128], BF16) for _ in range(NB)]
        CTt = [k.alloc([8, 128], BF16) for _ in range(NB)]
        Bt = [k.alloc([1024], BF16) for _ in range(NB)]
        DTt = [k.alloc([NSL, 64], F32) for _ in range(NB)]
        Abc = [k.alloc([32, 128], F32) for _ in range(2)]
        Abc2 = [Buf(), Buf()]
        Eb = [k.alloc([32, 128], BF16) for _ in range(2)]
        Mt = [k.alloc([32, 128], BF16) for _ in range(3)]
        cbm = [k.alloc([8, 128], BF16) for _ in range(2)]
        yo = k.alloc([2048], F32)
        yv = [k.alloc([2048], F32) for _ in range(2)]
        Xw = [k.alloc([2048], BF16) for _ in range(2)]
        Xd = [k.alloc([2048], BF16) for _ in range(2)]
        order = list(range(NT)) if d == 0 else list(range(NT - 1, -1, -1))
        Yout = Yf if d == 0 else Yb
        Youtb = Yfb if d == 0 else Ybb
        def P1(i):
            c = order[i]
            last = (i == NT - 1)
            tok = slice(c * 128, (c + 1) * 128)
            X, Xb = rr(Xt, i); BT, BTb = rr(BTt, i); CT, CTb = rr(CTt, i); B_, Bb = rr(Bt, i)
            DTc, DTb_ = rr(DTt, i); A_, Ab = rr(Abc, i)
            k.dma(A_.rearrange("p h l -> p (h l)"),
                  ACs[c, d].rearrange("h l -> (h l)").partition_broadcast(128), (ACsb,), (Ab, rr(Abc2, i)), key=Ab)
            k.dma(DTc, DTs[tok], (DTsb,), (DTb_,), key=DTb_)
            k.dma(BT, BTs[:, :, tok], (BTsb,), (BTb,), key=BTb)
            k.dma(CT, CTs[:, :, tok], (CTsb,), (CTb,), key=CTb)
            k.dma(X, Xs[tok, :], (Xsb,), (Xb,), key=Xb)
            k.dma(B_, Bs[tok, :], (Bsb,), (Bb,), key=Bb)
            dsl = slice(d * 32, (d + 1) * 32)
            p45 = k.bank(4, 2)
            for g in range(8):
                k.mm(p45[:, g * 128:(g + 1) * 128], BT[:, g, :], CT[:, g, :], True, True, (BTb, CTb),
                     (k.pb[4 + g // 4],))
            Ab2 = rr(Abc2, i)
            k.tt("pool", A_[:, 0:18, :], bc(DTc[:, 0, d * 32:d * 32 + 18], [(1, 18), (0, 128)]), A_[:, 0:18, :],
                 ALU.subtract, (Ab, DTb_), (Ab,))
            k.tt("dve", A_[:, 18:32, :], bc(DTc[:, 0, d * 32 + 18:d * 32 + 32], [(1, 14), (0, 128)]), A_[:, 18:32, :],
                 ALU.subtract, (Ab2, DTb_), (Ab2,))
            k.act(A_, A_, AF.Relu, (Ab, Ab2), (Ab, Ab2))
            E_, Ebb = rr(Eb, i)
            k.act(E_, A_, AF.Exp, (Ab, Ab2), (Ebb,), scale=-1.0)
            cb_, cbb = rr(cbm, i)
            k.tt("dve", cb_, p45.rearrange("p (g l) -> p g l", g=8), bc(mask, [(0, 8), (1, 128)]), ALU.mult,
                 (k.pb[4], k.pb[5], tri[1]), (cbb,))
            M_, Mb = rr(Mt, i)
            k.tt("dve", M_.rearrange("p (g r) l -> p g r l", g=8), E_.rearrange("p (g r) l -> p g r l", g=8),
                 bc(cb_[:, 0, :], [(128, 8), (0, 4), (1, 128)]), ALU.mult, (Ebb, cbb), (Mb,))
            xd_, xdb = rr(Xd, i)
            k.tt("pool", xd_.rearrange("p (h q) -> p h q", h=32), X.rearrange("p (h q) -> p h q", h=32),
                 bc(DTc[:, 4, dsl], [(1, 32), (0, 64)]), ALU.mult, (Xb, DTb_), (xdb,))
            xw_, xwb = rr(Xw, i)
            if not last:
                k.tt("pool", xw_.rearrange("p (h q) -> p h q", h=32), X.rearrange("p (h q) -> p h q", h=32),
                     bc(DTc[:, 1, dsl], [(1, 32), (0, 64)]), ALU.mult, (Xb, DTb_), (xwb,))

        def P2(i):
            c = order[i]
            first = (i == 0)
            last = (i == NT - 1)
            tok = slice(c * 128, (c + 1) * 128)
            CT, CTb = rr(CTt, i); B_, Bb = rr(Bt, i)
            DTc, DTb_ = rr(DTt, i)
            M_, Mb = rr(Mt, i)
            xd_, xdb = rr(Xd, i)
            xw_, xwb = rr(Xw, i)
            dsl = slice(d * 32, (d + 1) * 32)
            p03 = k.bank(0, 4)
            pb03 = tuple(k.pb[0:4])
            if not first:
                for g in range(8):
                    k.mm(p03[:, g * 256:(g + 1) * 256], CT[:, g, :], Hbf[0][:, g * 256:(g + 1) * 256], True, True,
                         (CTb, Hbf[1]), (k.pb[g // 2],))
                k.tt("dve", yo[0].rearrange("p (h q) -> p h q", h=32), p03.rearrange("p (h q) -> p h q", h=32),
                     bc(DTc[:, 2, dsl], [(1, 32), (0, 64)]), ALU.mult, pb03 + (DTb_,), (yo[1],))

        def P2b(i):
            c = order[i]
            first = (i == 0)
            last = (i == NT - 1)
            tok = slice(c * 128, (c + 1) * 128)
            CT, CTb = rr(CTt, i); B_, Bb = rr(Bt, i)
            DTc, DTb_ = rr(DTt, i)
            M_, Mb = rr(Mt, i)
            xd_, xdb = rr(Xd, i)
            xw_, xwb = rr(Xw, i)
            dsl = slice(d * 32, (d + 1) * 32)
            p03 = k.bank(0, 4)
            pb03 = tuple(k.pb[0:4])
            for h in range(32):
                k.mm(p03[:, h * 64:(h + 1) * 64], M_[:, h, :], xd_[:, h * 64:(h + 1) * 64], True, True, (Mb, xdb),
                     (k.pb[h // 8],))
            y_, yb_ = rr(yv, i)
            if first:
                k.copy("act", y_, p03, pb03, (yb_,))
            else:
                k.tt("dve", y_, p03, yo[0], ALU.add, pb03 + (yo[1],), (yb_,))
            k.dma(Yout[tok, :], y_, (yb_,), (Youtb,), key=yb_)
            if not last:
                p47 = k.bank(4, 4)
                pb47 = tuple(k.pb[4:8])
                for g in range(8):
                    k.mm(p47[:, g * 256:(g + 1) * 256], B_[:, g * 128:(g + 1) * 128], xw_[:, g * 256:(g + 1) * 256],
                         True, True, (Bb, xwb), (k.pb[4 + g // 2],))
                if first:
                    k.copy("act", Hf[0], p47, pb47, (Hf[1],))
                else:
                    k.tt("pool", Hf[0].rearrange("p (h q) -> p h q", h=32), Hf[0].rearrange("p (h q) -> p h q", h=32),
                         bc(DTc[:, 3, dsl], [(1, 32), (0, 64)]), ALU.mult, (Hf[1], DTb_), (Hf[1],))
                    k.tt("dve", Hf[0], Hf[0], p47, ALU.add, pb47 + (Hf[1],), (Hf[1],))
                k.copy("act", Hbf[0], Hf[0], (Hf[1],), (Hbf[1],))

        P1(0)
        for i in range(NT):
            if i + 1 < NT:
                P1(i + 1)
            P2(i)
            P2b(i)

    def ssd_s3(src, dst, layer):
        vec = k.alloc([160], F32)
        k.dma(vec[0], ssd_vec.partition_broadcast(128), (), (vec[1],), key=vec[1])
        ng = k.alloc([2048], F32)
        k.dma(ng[0], ssd_norm_g.partition_broadcast(128), (), (ng[1],), key=ng[1])
        wo = k.alloc([16, D], BF16)
        stg = [k.alloc([2, D], F32) for _ in range(2)]
        wov = ssd_w_out.rearrange("(kc p) m -> p kc m", p=128)
        for i in range(8):
            k.load_cast(wo[0][:, 2 * i:2 * i + 2, :], wo[1], wov[:, 2 * i:2 * i + 2, :], rr(stg, i),
                        eng="act" if i % 2 == 0 else "dve")
        ln.load_gb(ln_tok_g[layer], ln_tok_b[layer])
        NB = 2
        Xt = [k.alloc([2048], BF16) for _ in range(NB)]
        Yt = [k.alloc([2048], F32) for _ in range(NB)]
        Y2t = [k.alloc([2048], F32) for _ in range(NB)]
        Zt = [k.alloc([2048], F32) for _ in range(NB)]
        t1 = k.alloc([2048], F32)
        ss = [k.alloc([8], F32) for _ in range(2)]
        ynb = [k.alloc([2048], BF16) for _ in range(3)]
        yT = [k.alloc([16, 128], BF16) for _ in range(2)]
        h0t = [k.alloc([D], F32) for _ in range(2)]
        rt = [k.alloc([D], F32) for _ in range(3)]

        def phaseA(t):
            tok = slice(t * 128, (t + 1) * 128)
            X, Xb = rr(Xt, t); Y, Yb_ = rr(Yt, t); Y2, Y2b = rr(Y2t, t); Z, Zb = rr(Zt, t)
            k.dma(X, Xs[tok, :], (Xsb,), (Xb,), key=Xb)
            k.dma(Y, Yf[tok, :], (Yfb,), (Yb_,), key=Yb_)
            k.dma(Y2, Yb[tok, :], (Ybb,), (Y2b,), key=Y2b)
            k.dma(Z, Zs[tok, :], (Zsb,), (Zb,), key=Zb)
            k.tt("pool", t1[0].rearrange("p (h q) -> p h q", h=32), X.rearrange("p (h q) -> p h q", h=32),
                 bc(vec[0][:, 128:160], [(1, 32), (0, 64)]), ALU.mult, (Xb, vec[1]), (t1[1],))
            k.tt("dve", Y, Y, Y2, ALU.add, (Yb_, Y2b), (Yb_,))
            k.tt("dve", Y, Y, t1[0], ALU.add, (Yb_, t1[1]), (Yb_,))
            k.tt("pool", Y, Y, Z, ALU.mult, (Yb_, Zb), (Yb_,))
            k.tt("dve", t1[0], Y, Y, ALU.mult, (Yb_,), (t1[1],))
            s_, sb_ = rr(ss, t)
            k.s.add("dve", lambda e, s_=s_: e.tensor_reduce(s_, t1[0].rearrange("p (g q) -> p g q", g=8),
                                                            mybir.AxisListType.X, ALU.add), (t1[1],), (sb_,))
            k.act(s_, s_, AF.Sqrt, (sb_,), (sb_,), bias=EPS, scale=1.0 / 256.0)
            k.s.add("dve", lambda e, s_=s_: e.reciprocal(s_, s_), (sb_,), (sb_,))
            k.tt("dve", Y.rearrange("p (g q) -> p g q", g=8), Y.rearrange("p (g q) -> p g q", g=8),
                 bc(s_, [(1, 8), (0, 256)]), ALU.mult, (Yb_, sb_), (Yb_,))
            yb16, yb16b = rr(ynb, t)
            k.tt("pool", yb16, Y, ng[0], ALU.mult, (Yb_, ng[1]), (yb16b,))

        def phaseB(t):
            tok = slice(t * 128, (t + 1) * 128)
            yb16, yb16b = rr(ynb, t)
            b0 = 2 * (t % 2)
            pst = k.bank(b0, 2, BF16)
            for c in range(16):
                k.tr(pst[:, c * 128:(c + 1) * 128], yb16[:, c * 128:(c + 1) * 128], ident[0], (yb16b, ident[1]),
                     (k.pb[b0 + c // 8],))
            yT_, yTb = rr(yT, t)
            k.copy("act", yT_.rearrange("p c t -> p (c t)"), pst, (k.pb[b0], k.pb[b0 + 1]), (yTb,))
            ht, htb = rr(h0t, t)
            r, rb = rr(rt, t)
            k.dma(ht, H[src][tok, :], (Hb[src],), (htb,), key=htb)
            b1 = 4 + 2 * (t % 2)
            ps = k.bank(b1, 2)
            for hf in range(2):
                for kc in range(16):
                    k.mm(ps[:, hf * 512:(hf + 1) * 512], yT_[:, kc, :], wo[0][:, kc, hf * 512:(hf + 1) * 512],
                         kc == 0, kc == 15, (yTb, wo[1]), (k.pb[b1 + hf],))
            k.stt(r, ht, ALPHA, ps, ALU.mult, ALU.add, (htb, k.pb[b1], k.pb[b1 + 1]), (rb,))

        def phaseC(t):
            tok = slice(t * 128, (t + 1) * 128)
            r, rb = rr(rt, t)
            ln.apply(r, rb, H[dst][tok, :], HT[dst][:, :, tok], 2 * (t % 2), Hb[dst], HTb[dst])

        for step in range(NT + 2):
            if step < NT:
                phaseA(step)
            if 0 <= step - 1 < NT:
                phaseB(step - 1)
            if 0 <= step - 2 < NT:
                phaseC(step - 2)

    if next_stage():
        ssd_s1(2)
    if next_stage():
        ssd_scan(0)
    if next_stage():
        ssd_scan(1)
    if next_stage():
        ssd_s3(2, 3, 1)
        cur = 3
    if next_stage():
        ffn_stage(1, 3, 4)
        cur = 4

    ln.flush()
    k.barrier()
    k.top = persist_top
    ft, fb = k.alloc([8, D], F32)
    for q in range(S // 1024):
        k.dma(ft, H[cur][q * 1024:(q + 1) * 1024, :].rearrange("(p a) d -> p a d", a=8), (Hb[cur],), (fb,), key=fb)
        k.dma(y[q * 1024:(q + 1) * 1024, :].rearrange("(p a) d -> p a d", a=8), ft, (fb,), (), key=fb)
    k.s.add("sp", None, (fb,), (fb,))

    k.s.emit(nc, es)
    es.close()
    return nc


def make_consts():
    c = {}
    bf = ml_dtypes.bfloat16
    c["c_ident"] = np.eye(128, dtype=np.float32).astype(bf)
    c["c_identf"] = np.eye(128, dtype=np.float32)
    j = np.arange(128, dtype=np.float64)
    ang = 2 * np.pi * np.outer(j, j) / 128.0
    sc = 1.0 / math.sqrt(128.0 * S)
    c["c_cs128"] = np.concatenate([np.cos(ang) * sc, -np.sin(ang) * sc], axis=1).astype(np.float32).astype(bf)
    t1 = np.arange(64, dtype=np.float64)[:, None, None]
    t2 = np.arange(64, dtype=np.float64)[None, :, None]
    k1 = np.arange(64, dtype=np.float64)[None, None, :]
    th = -2 * np.pi * (t2 * k1 / 4096.0 + t1 * k1 / 64.0)
    Pr = np.cos(th); Pi = np.sin(th)
    ma = np.zeros((2, 64, 64, 2, 64), dtype=np.float64)
    ma[0, :, :, 0, :] = Pr
    ma[1, :, :, 0, :] = -Pi
    ma[0, :, :, 1, :] = Pi
    ma[1, :, :, 1, :] = Pr
    c["c_ma"] = ma.reshape(128, 64 * 128).astype(np.float32).astype(bf)
    a = 2 * np.pi * np.outer(np.arange(64.0), np.arange(64.0)) / 64.0
    c["c_bb"] = np.concatenate([np.cos(a), np.sin(a)], axis=0).astype(np.float32).astype(bf)
    r = np.arange(128)[:, None]; cl = np.arange(128)[None, :]
    tri = np.stack([(cl >= r), (cl > r), -(cl > r).astype(np.float32), (cl <= r)], axis=1).astype(np.float32)
    c["c_tri"] = tri.reshape(128, 512)
    return c


def prep_inputs(inputs):
    f = lambda a: np.ascontiguousarray(a, dtype=np.float32)
    shared = {}
    shared["emb_ln_g"] = f(inputs["emb_ln_g"]); shared["emb_ln_b"] = f(inputs["emb_ln_b"])
    shared["fn_w_in"] = f(inputs["fn_w_in"][0]); shared["fn_b_in"] = f(inputs["fn_b_in"][0:1])
    shared["fn_w_out"] = f(inputs["fn_w_out"][0]); shared["fn_b_out"] = f(inputs["fn_b_out"][0:1])
    shared["ln_tok_g"] = f(inputs["ln_tok_g"]); shared["ln_tok_b"] = f(inputs["ln_tok_b"])
    shared["ff_w_up"] = f(inputs["ff_w_up"])
    cw4 = np.concatenate([inputs["ff_conv_w"], inputs["ff_conv_b"][:, None, :]], axis=1)
    shared["ff_cwp"] = f(cw4.reshape(cw4.shape[0], 4, 44, 128).transpose(0, 3, 2, 1).reshape(cw4.shape[0], 128, 176))
    shared["ff_w_down"] = f(inputs["ff_w_down"])
    shared["ln_ffn_g"] = f(inputs["ln_ffn_g"]); shared["ln_ffn_b"] = f(inputs["ln_ffn_b"])
    shared["ssd_w_in"] = f(inputs["ssd_w_in"][0])
    cw6 = np.concatenate([inputs["ssd_conv_w"][0], inputs["ssd_conv_b"][0][None, :]], axis=0)
    shared["ssd_cwp"] = f(cw6.reshape(6, 32, 128).transpose(2, 1, 0).reshape(128, 192))
    shared["ssd_vec"] = f(np.concatenate([inputs["ssd_dt_bias_fwd"][0], inputs["ssd_dt_bias_bwd"][0],
                                          inputs["ssd_a_log_fwd"][0], inputs["ssd_a_log_bwd"][0], inputs["ssd_d"][0]]))
    shared["ssd_norm_g"] = f(inputs["ssd_norm_g"][0]); shared["ssd_w_out"] = f(inputs["ssd_w_out"][0])
    shared.update(make_consts())
    x = f(inputs["x"])
    return [dict(shared, x=x[b]) for b in range(x.shape[0])]


def run(inputs, n_stages=99, debug=False):
    nc = bass.Bass("TRN2", target_bir_lowering=False)
    build(nc, n_stages=n_stages, debug=debug)
    in_maps = prep_inputs(inputs)
    res = run_bass_kernel_spmd(nc, in_maps, core_ids=list(range(len(in_maps))))
    return res.results


def kernel(**inputs):
    results = run(inputs)
    out = np.stack([np.asarray(r["y"]) for r in results], axis=0)
    return out.astype(np.float32)
```

```python
import math
from contextlib import ExitStack

import numpy as np
import ml_dtypes

import concourse.bass as bass
import concourse.mybir as mybir
from concourse.bass_utils import run_bass_kernel_spmd

F32 = mybir.dt.float32
BF16 = mybir.dt.bfloat16
AF = mybir.ActivationFunctionType
ALU = mybir.AluOpType

S = 4096
D = 1024
NT = S // 128
DEPTH = 2
ALPHA = (2.0 * DEPTH) ** 0.25
EPS = 1e-5
D_FF = 2816
NFC = D_FF // 128
D_INNER = 2048
CONV_DIM = 4096
D_IN_PROJ = 6208

SAME_ENGINE_SYNC = True
HOIST_WINDOW = 96


class Buf:
    __slots__ = ("name", "last_w", "readers")

    def __init__(self, name="b"):
        self.name = name
        self.last_w = None
        self.readers = {}


class Op:
    __slots__ = ("eng", "fn", "deps", "signal", "sigval", "sem", "is_dma", "idx", "key", "sp_pos", "sp_need",
                 "is_load", "fence", "max_dep")

    def __init__(self, eng, fn, is_dma, key):
        self.eng = eng
        self.fn = fn
        self.deps = []
        self.signal = False
        self.sigval = None
        self.sem = None
        self.is_dma = is_dma
        self.key = key
        self.sp_pos = -1
        self.sp_need = -1
        self.is_load = False
        self.fence = False
        self.max_dep = -1


class Sched:
    ENGS = ("pe", "act", "dve", "pool", "sp")

    def __init__(self):
        self.ops = {e: [] for e in self.ENGS}
        self.dma_keys = {}
        self.key_slot = {}
        self.slots = []
        self.free = []
        self.n = 0
        self.fence_pos = -1

    def barrier(self):
        deps = []
        for e in self.ENGS:
            for op in reversed(self.ops[e]):
                if op.fn is not None and not op.is_dma:
                    op.signal = True
                    deps.append(op)
                    break
        for sl in self.slots:
            if sl["last_op"] is not None:
                deps.append(sl["last_op"])
        for e in self.ENGS:
            w = Op(e, None, False, None)
            w.idx = self.n
            self.n += 1
            w.deps = list(deps)
            if e == "sp":
                w.sp_pos = len(self.ops[e])
                w.fence = True
                self.fence_pos = w.sp_pos
            self.ops[e].append(w)
        self.key_slot = {}
        self.free = list(range(len(self.slots)))

    def add(self, eng, fn, reads=(), writes=(), dma=False, key=None, is_load=False):
        op = Op(eng, fn, dma, key)
        op.idx = self.n
        self.n += 1
        deps = {}
        for b in reads:
            if b.last_w is not None:
                deps[id(b.last_w)] = b.last_w
        for b in writes:
            if b.last_w is not None:
                deps[id(b.last_w)] = b.last_w
            for r in b.readers.values():
                deps[id(r)] = r
        if dma:
            assert key is not None
            prev = self.dma_keys.get(key)
            if prev is not None:
                deps[id(prev)] = prev
            self.dma_keys[key] = op
            si = self.key_slot.get(key)
            if si is None:
                if self.free:
                    si = self.free.pop(0)
                else:
                    si = len(self.slots)
                    self.slots.append({"count": 0, "last_op": None, "sem": None})
                self.key_slot[key] = si
            sl = self.slots[si]
            sl["count"] += 1
            sl["last_op"] = op
            op.key = sl
            op.sigval = 16 * sl["count"]
            op.signal = True
        deps.pop(id(op), None)
        for d in deps.values():
            if d.is_dma:
                d.signal = True
            elif d.eng != eng or SAME_ENGINE_SYNC or dma:
                if d.eng == eng and not dma and eng == "pe":
                    continue
                d.signal = True
            else:
                continue
            op.deps.append(d)
        need = self.fence_pos
        for d in deps.values():
            v = d.sp_pos if d.eng == "sp" else d.sp_need
            if v > need:
                need = v
            if d.idx > op.max_dep:
                op.max_dep = d.idx
        op.sp_need = need
        op.is_load = is_load
        if eng == "sp":
            op.sp_pos = len(self.ops[eng])
            if fn is None:
                op.fence = True
        for b in reads:
            k = ("dma", id(op)) if dma else eng
            b.readers[k] = op
        for b in writes:
            b.last_w = op
            b.readers = {}
        self.ops[eng].append(op)
        return op

    def emit(self, nc, es):
        eng_sem = {}
        for e in self.ENGS:
            eng_sem[e] = es.enter_context(nc.semaphore("s_" + e))
        if HOIST_WINDOW > 0:
            newl = []
            nh = 0
            for op in self.ops["sp"]:
                if not op.is_load:
                    newl.append(op)
                    continue
                j = len(newl)
                steps = 0
                while j > 0 and steps < HOIST_WINDOW:
                    e = newl[j - 1]
                    if e.is_load or e.fence or e.sp_pos <= op.sp_need or op.max_dep >= e.idx:
                        break
                    j -= 1
                    steps += 1
                if j < len(newl):
                    nh += 1
                newl.insert(j, op)
            self.ops["sp"] = newl
            print("sched: hoisted %d loads" % nh)
        for i, sl in enumerate(self.slots):
            sl["sem"] = es.enter_context(nc.semaphore("d_%d" % i))
        print("sched: %d dma sem slots, ops per engine: %s" % (len(self.slots), {e: len(v) for e, v in self.ops.items()}))
        for e in self.ENGS:
            c = 0
            for op in self.ops[e]:
                if op.is_dma:
                    op.sem = op.key["sem"]
                    continue
                if op.signal:
                    c += 1
                    op.sigval = c
                    op.sem = eng_sem[e]
        block = es.enter_context(nc.Block())
        sched = self

        def run(engname):
            def body(eng):
                waited = {}
                for op in sched.ops[engname]:
                    need = {}
                    for d in op.deps:
                        sid = id(d.sem)
                        if need.get(sid, (None, 0))[1] < d.sigval:
                            need[sid] = (d.sem, d.sigval)
                    for sid, (sem, val) in need.items():
                        if waited.get(sid, 0) < val:
                            eng.wait_ge(sem, val)
                            waited[sid] = val
                    if op.fn is None:
                        continue
                    ins = op.fn(eng)
                    if op.signal:
                        ins.then_inc(op.sem, 16 if op.is_dma else 1)
            return body

        block.tensor(run("pe"))
        block.scalar(run("act"))
        block.vector(run("dve"))
        block.gpsimd(run("pool"))
        block.sync(run("sp"))


def rr(lst, i):
    return lst[i % len(lst)]


ARENA_W = 206 * 256


class K:
    def __init__(self, nc, es):
        self.nc = nc
        self.es = es
        self.s = Sched()
        self.arena = es.enter_context(nc.sbuf_tensor("arena", [128, ARENA_W], F32))
        self.top = 0
        self.psum = es.enter_context(nc.psum_tensor("psum", [128, 4096], F32))
        self.pb = [Buf("ps%d" % i) for i in range(8)]

    def alloc(self, free_shape, dtype):
        esz = 4 if dtype == F32 else 2
        n = 1
        for d in free_shape:
            n *= d
        nbytes = (n * esz + 63) // 64 * 64
        off = self.top
        self.top += nbytes
        assert self.top <= ARENA_W * 4, "SBUF arena overflow %d" % self.top
        ap = self.arena[:, off // 4: off // 4 + nbytes // 4]
        if dtype == BF16:
            ap = ap.bitcast(BF16)
        ap = ap[:, 0:n]
        if len(free_shape) == 2:
            ap = ap.rearrange("p (a b) -> p a b", a=free_shape[0])
        elif len(free_shape) == 3:
            ap = ap.rearrange("p (a b c) -> p a b c", a=free_shape[0], b=free_shape[1])
        return ap, Buf()

    def bank(self, i, n=1, dtype=F32):
        ap = self.psum[:, i * 512:(i + n) * 512]
        if dtype == BF16:
            ap = ap.bitcast(BF16)
        return ap

    def barrier(self):
        self.s.barrier()

    def dma(self, out, in_, reads, writes, key, eng="sp"):
        is_load = type(out.tensor).__name__.startswith("SB") and not type(in_.tensor).__name__.startswith("SB")
        return self.s.add(eng, lambda e: e.dma_start(out=out, in_=in_), reads, writes, dma=True, key=key,
                          is_load=is_load)

    def mm(self, out, lhsT, rhs, start, stop, reads, writes):
        return self.s.add("pe", lambda e: e.matmul(out, lhsT, rhs, start=start, stop=stop), reads, writes)

    def tr(self, out, in_, ident, reads, writes):
        return self.s.add("pe", lambda e: e.transpose(out, in_, ident), reads, writes)

    def act(self, out, in_, func, reads, writes, bias=0.0, scale=1.0):
        return self.s.add("act", lambda e: e.activation(out, in_, func, bias=bias, scale=scale), reads, writes)

    def tt(self, eng, out, in0, in1, op, reads, writes):
        return self.s.add(eng, lambda e: e.tensor_tensor(out, in0, in1, op), reads, writes)

    def ts(self, eng, out, in0, s1, s2, op0, op1, reads, writes):
        if s2 is None:
            return self.s.add(eng, lambda e: e.tensor_scalar(out, in0, s1, None, op0), reads, writes)
        return self.s.add(eng, lambda e: e.tensor_scalar(out, in0, s1, s2, op0, op1), reads, writes)

    def stt(self, out, in0, scalar, in1, op0, op1, reads, writes):
        return self.s.add("dve", lambda e: e.scalar_tensor_tensor(out, in0, scalar, in1, op0, op1), reads, writes)

    def copy(self, eng, out, in_, reads, writes):
        if eng == "act":
            return self.s.add("act", lambda e: e.copy(out, in_), reads, writes)
        return self.s.add(eng, lambda e: e.tensor_copy(out, in_), reads, writes)

    def memset(self, eng, ap, val, writes):
        return self.s.add(eng, lambda e: e.memset(ap, val), (), writes)

    def load_cast(self, dst, dstb, src, stg, eng="pool"):
        sa, sbuf_ = stg
        self.dma(sa, src, (), (sbuf_,), key=sbuf_)
        self.copy(eng, dst, sa, (sbuf_,), (dstb,))


class LNRes:
    def __init__(self, k, ident):
        self.k = k
        self.ident = ident
        NB = 3
        self.stats = [k.alloc([2, 6], F32) for i in range(NB)]
        self.mv = [k.alloc([2], F32) for i in range(NB)]
        self.rstd = [k.alloc([1], F32) for i in range(NB)]
        self.t1 = [k.alloc([D], F32) for i in range(2)]
        self.h = [k.alloc([D], F32) for i in range(2)]
        self.hb = [k.alloc([D], BF16) for i in range(NB)]
        self.hT = [k.alloc([8, 128], BF16) for i in range(2)]
        self.g = k.alloc([D], F32)
        self.b = k.alloc([D], F32)
        self.cnt = 0
        self.pb_ = []
        self.pc_ = []
        self.pool_pow = False
        self.mhalf = k.alloc([1], F32)
        k.memset("pool", self.mhalf[0], -0.5, (self.mhalf[1],))

    def load_gb(self, g_dram, b_dram):
        k = self.k
        self.flush()
        k.dma(self.g[0], g_dram.partition_broadcast(128), (), (self.g[1],), key=self.g[1])
        k.dma(self.b[0], b_dram.partition_broadcast(128), (), (self.b[1],), key=self.b[1])

    def _a(self, j):
        k = self.k
        i, r, rbuf = j["i"], j["r"], j["rbuf"]
        st, stb = rr(self.stats, i)
        mv, mvb = rr(self.mv, i)
        rs, rsb = rr(self.rstd, i)
        for q in range(2):
            k.s.add("dve", lambda e, q=q: e.bn_stats(st[:, q, :], r[:, q * 512:(q + 1) * 512]), (rbuf,), (stb,))
        k.s.add("dve", lambda e: e.bn_aggr(mv, st), (stb,), (mvb,))
        if self.pool_pow:
            k.ts("dve", rs, mv[:, 1:2], EPS, None, ALU.add, None, (mvb,), (rsb,))
            k.tt("pool", rs, rs, self.mhalf[0], ALU.pow, (rsb, self.mhalf[1]), (rsb,))
        else:
            k.act(rs, mv[:, 1:2], AF.Sqrt, (mvb,), (rsb,), bias=EPS)
            k.s.add("dve", lambda e: e.reciprocal(rs, rs), (rsb,), (rsb,))

    def _b(self, j):
        k = self.k
        i, r, rbuf = j["i"], j["r"], j["rbuf"]
        mv, mvb = rr(self.mv, i)
        rs, rsb = rr(self.rstd, i)
        t1, t1b = rr(self.t1, i)
        h, hb_ = rr(self.h, i)
        hb, hbb = rr(self.hb, i)
        k.stt(t1, r, mv[:, 0:1], self.g[0], ALU.subtract, ALU.mult, (rbuf, mvb, self.g[1]), (t1b,))
        k.stt(h, t1, rs[:, 0:1], self.b[0], ALU.mult, ALU.add, (t1b, rsb, self.b[1]), (hb_,))
        k.dma(j["H"], h, (hb_,), (j["hbuf"],), key=hb_)
        k.copy("act", hb, h, (hb_,), (hbb,))

    def _c(self, j):
        k = self.k
        i = j["i"]
        hb, hbb = rr(self.hb, i)
        hT, hTb = rr(self.hT, i)
        pst = k.bank(j["bank"], 1, BF16)
        pstb = k.pb[j["bank"]]
        for c in range(8):
            k.tr(pst[:, c * 128:(c + 1) * 128], hb[:, c * 128:(c + 1) * 128], self.ident[0],
                 (hbb, self.ident[1]), (pstb,))
        k.copy("act", hT.rearrange("p c t -> p (c t)"), pst, (pstb,), (hTb,))
        k.dma(j["HT"], hT, (hTb,), (j["htbuf"],), key=hTb)

    def apply(self, r, rbuf, H_dram_tile, HT_dram_tile, tp_bank, hbuf_dram, htbuf_dram):
        j = dict(i=self.cnt, r=r, rbuf=rbuf, H=H_dram_tile, HT=HT_dram_tile, bank=tp_bank, hbuf=hbuf_dram,
                 htbuf=htbuf_dram)
        self.cnt += 1
        self._a(j)
        if self.pc_:
            self._c(self.pc_.pop(0))
        if self.pb_:
            jb = self.pb_.pop(0)
            self._b(jb)
            self.pc_.append(jb)
        self.pb_.append(j)

    def flush(self):
        while self.pb_ or self.pc_:
            if self.pc_:
                self._c(self.pc_.pop(0))
            if self.pb_:
                jb = self.pb_.pop(0)
                self._b(jb)
                self.pc_.append(jb)


def build(nc, n_stages=99, debug=False, only=None):
    es = ExitStack()
    k = K(nc, es)

    def din(name, shape, dt=F32):
        return nc.dram_tensor(name, shape, dt, kind="ExternalInput").ap()

    def scratch(name, shape, dt):
        if debug:
            return nc.dram_tensor("dbg_" + name, shape, dt, kind="ExternalOutput").ap()
        return nc.dram_tensor(name, shape, dt).ap()

    x = din("x", [S, D])
    emb_g = din("emb_ln_g", [D]); emb_b = din("emb_ln_b", [D])
    fn_w_in = din("fn_w_in", [D, D]); fn_b_in = din("fn_b_in", [1, D])
    fn_w_out = din("fn_w_out", [D, D]); fn_b_out = din("fn_b_out", [1, D])
    ln_tok_g = din("ln_tok_g", [2, D]); ln_tok_b = din("ln_tok_b", [2, D])
    ff_w_up = din("ff_w_up", [2, D, 2 * D_FF]); ff_cw = din("ff_cwp", [2, 128, 176])
    ff_w_down = din("ff_w_down", [2, D_FF, D])
    ln_ffn_g = din("ln_ffn_g", [2, D]); ln_ffn_b = din("ln_ffn_b", [2, D])
    ssd_w_in = din("ssd_w_in", [D, D_IN_PROJ]); ssd_cwp = din("ssd_cwp", [128, 192])
    ssd_vec = din("ssd_vec", [160]); ssd_norm_g = din("ssd_norm_g", [2048]); ssd_w_out = din("ssd_w_out", [2048, D])
    tri_d = din("c_tri", [128, 512], F32)
    ident_d = din("c_ident", [128, 128], BF16)
    identf_d = din("c_identf", [128, 128], F32)
    cs128_d = din("c_cs128", [128, 256], BF16)
    ma_d = din("c_ma", [128, 64 * 128], BF16)
    bb_d = din("c_bb", [128, 64], BF16)
    y = nc.dram_tensor("y", [S, D], F32, kind="ExternalOutput").ap()

    H = [scratch("H%d" % i, [S, D], F32) for i in range(5)]
    HT = [nc.dram_tensor("HT%d" % i, [128, 8, S], BF16).ap() for i in range(5)]
    Hb = [Buf("b") for _ in range(5)]
    HTb = [Buf("b") for _ in range(5)]
    Gs = nc.dram_tensor("Gs", [64, 64, 2, 1024], BF16).ap()
    Gsb = Buf("b")
    Ys = nc.dram_tensor("Ys", [64, 128, 1024], BF16).ap()
    Ysb = Buf("b")
    Xs = nc.dram_tensor("Xs", [S, 2048], BF16).ap(); Xsb = Buf()
    Bs = nc.dram_tensor("Bs", [S, 1024], BF16).ap(); Bsb = Buf()
    BTs = nc.dram_tensor("BTs", [128, 8, S], BF16).ap(); BTsb = Buf()
    CTs = nc.dram_tensor("CTs", [128, 8, S], BF16).ap(); CTsb = Buf()
    Zs = nc.dram_tensor("Zs", [S, 2048], F32).ap(); Zsb = Buf()
    DTs = nc.dram_tensor("DTs", [S, 5, 64], F32).ap(); DTsb = Buf()
    ACs = nc.dram_tensor("ACs", [NT, 2, 32, 128], F32).ap(); ACsb = Buf()
    Yf = scratch("Yf", [S, 2048], F32); Yfb = Buf()
    Yb = scratch("Yb", [S, 2048], F32); Ybb = Buf()

    ident = k.alloc([128], BF16)
    k.dma(ident[0], ident_d, (), (ident[1],), key=ident[1])
    identf = k.alloc([128], F32)
    k.dma(identf[0], identf_d, (), (identf[1],), key=identf[1])
    ones = k.alloc([512], BF16)
    k.memset("pool", ones[0], 1.0, (ones[1],))
    ln = LNRes(k, ident)
    persist_top = k.top
    stage = [0]

    def next_stage():
        ln.flush()
        k.barrier()
        k.top = persist_top
        stage[0] += 1
        if only is not None:
            return stage[0] in only
        return stage[0] <= n_stages

    WbUp = [nc.dram_tensor("WbUp%d" % i, [44, 128, 1024], BF16).ap() for i in range(2)]
    WbUpb = [Buf(), Buf()]
    WbIn = nc.dram_tensor("WbIn", [32, 128, 1024], BF16).ap()
    WbInb = Buf()

    def prologue(which):
        stg = [k.alloc([8, 512], F32) for _ in range(3)]
        outb = [k.alloc([8, 512], BF16) for _ in range(3)]
        jobs = []
        if 0 in which:
            wv0 = ff_w_up[0].rearrange("(kc p) m -> p kc m", p=128)
            for pc in range(11):
                jobs.append((wv0[:, :, pc * 512:(pc + 1) * 512], WbUp[0][pc * 4:(pc + 1) * 4], WbUpb[0]))
        if 1 in which:
            wvi = ssd_w_in.rearrange("(kc p) m -> p kc m", p=128)
            for pc in range(8):
                jobs.append((wvi[:, :, 2048 + pc * 512:2048 + (pc + 1) * 512], WbIn[pc * 4:(pc + 1) * 4], WbInb))
            wv1 = ff_w_up[1].rearrange("(kc p) m -> p kc m", p=128)
            for pc in range(11):
                jobs.append((wv1[:, :, pc * 512:(pc + 1) * 512], WbUp[1][pc * 4:(pc + 1) * 4], WbUpb[1]))
        def job(i, src_, dst_, dbuf):
            s_, sb_ = rr(stg, i)
            o_, ob_ = rr(outb, i)
            k.dma(s_, src_, (), (sb_,), key=sb_)
            k.copy("act" if i % 2 == 0 else "dve", o_, s_, (sb_,), (ob_,))
            for c4 in range(4):
                k.dma(dst_[c4].rearrange("p (k m) -> p k m", k=8), o_[:, :, c4 * 128:(c4 + 1) * 128],
                      (ob_,), (dbuf,), key=(id(ob_), c4))
        return [(lambda i=i, jb=jb: job(i, *jb)) for i, jb in enumerate(jobs)]

    pro_jobs = prologue((0,)) if (only is None or 0 in only) else []
    ln.load_gb(emb_g, emb_b)
    xin = [k.alloc([D], F32) for i in range(3)]
    for t in range(NT if (only is None or 0 in only) else 0):
        xt, xb = rr(xin, t)
        k.dma(xt, x[t * 128:(t + 1) * 128, :], (), (xb,), key=xb)
        ln.apply(xt, xb, H[0][t * 128:(t + 1) * 128, :], HT[0][:, :, t * 128:(t + 1) * 128], t % 2, Hb[0], HTb[0])
        if pro_jobs:
            pro_jobs.pop(0)()
    while pro_jobs:
        pro_jobs.pop(0)()
    cur = 0

    if next_stage():
        hT = k.alloc([8, S], BF16)
        for kc in range(8):
            k.dma(hT[0][:, kc, :], HT[0][:, kc, :], (HTb[0],), (hT[1],), key=("hTl", kc % 4))
        stg = k.alloc([4, 1024], F32)
        w = k.alloc([8, 1024], BF16)
        wv = fn_w_in.rearrange("(kc p) m -> p kc m", p=128)
        for hf in range(2):
            k.load_cast(w[0][:, hf * 4:(hf + 1) * 4, :], w[1], wv[:, hf * 4:(hf + 1) * 4, :], stg,
                        eng="act" if hf == 0 else "dve")
        brow_f = k.alloc([1024], F32)
        brow = k.alloc([1024], BF16)
        k.dma(brow_f[0][0:1, :], fn_b_in, (), (brow_f[1],), key=brow_f[1])
        k.copy("pool", brow[0][0:1, :], brow_f[0][0:1, :], (brow_f[1],), (brow[1],))
        cs = k.alloc([256], BF16)
        k.dma(cs[0], cs128_d, (), (cs[1],), key=cs[1])
        h1 = [k.alloc([8, 512], BF16) for _ in range(2)]
        Gt = [k.alloc([2, 8, 128], BF16) for _ in range(2)]
        gi = 0
        for tb in range(8):
            h1a, h1b = rr(h1, tb)
            for g in range(8):
                bk = g % 2
                ps = k.bank(bk)
                for kc in range(8):
                    k.mm(ps, w[0][:, kc, g * 128:(g + 1) * 128], hT[0][:, kc, tb * 512:(tb + 1) * 512],
                         kc == 0, False, (w[1], hT[1]), (k.pb[bk],))
                k.mm(ps, brow[0][0:1, g * 128:(g + 1) * 128], ones[0][0:1, :], False, True,
                     (brow[1], ones[1]), (k.pb[bk],))
                k.copy("act" if g % 2 == 0 else "dve", h1a[:, g, :], ps, (k.pb[bk],), (h1b,))
            for tt_ in range(4):
                t = tb * 4 + tt_
                gt, gtb = rr(Gt, gi)
                for hf in range(2):
                    b0 = 2 + 2 * (gi % 2) if False else 2 + 2 * ((2 * gi + hf) % 3)
                    psg = k.bank(b0, 2)
                    for gg in range(4):
                        g = hf * 4 + gg
                        k.mm(psg[:, gg * 256:(gg + 1) * 256], h1a[:, g, tt_ * 128:(tt_ + 1) * 128], cs[0],
                             True, True, (h1b, cs[1]), (k.pb[b0], k.pb[b0 + 1]))
                    psv = psg.rearrange("p (g r j) -> p g r j", g=4, r=2)
                    for ri in range(2):
                        k.copy("act" if ri == 0 else "dve", gt[:, ri, hf * 4:(hf + 1) * 4, :], psv[:, :, ri, :],
                               (k.pb[b0], k.pb[b0 + 1]), (gtb,))
                k.dma(Gs[2 * t:2 * t + 2].rearrange("a b r c -> (a b) r c"), gt.rearrange("p r g j -> p r (g j)"),
                      (gtb,), (Gsb,), key=gtb)
                gi += 1

    if next_stage():
        ma = k.alloc([64, 128], BF16)
        k.dma(ma[0].rearrange("p a b -> p (a b)"), ma_d, (), (ma[1],), key=ma[1])
        NZ = 4
        Z = [k.alloc([1024], BF16) for _ in range(NZ)]
        Yt = [k.alloc([1024], BF16) for _ in range(NZ)]
        late_jobs = prologue((1,))
        for t2 in range(64):
            if t2 % 3 == 1 and late_jobs:
                late_jobs.pop(0)()
            z, zb = rr(Z, t2)
            yt, ytb = rr(Yt, t2)
            for ri in range(2):
                k.dma(z[ri * 64:(ri + 1) * 64, :], Gs[:, t2, ri, :], (Gsb,), (zb,), key=(id(zb), ri))
            for hf in range(2):
                bk = (2 * t2 + hf) % 8
                ps = k.bank(bk)
                k.mm(ps, ma[0][:, t2, :], z[:, hf * 512:(hf + 1) * 512], True, True, (ma[1], zb), (k.pb[bk],))
                k.copy("act" if hf == 0 else "dve", yt[:, hf * 512:(hf + 1) * 512], ps, (k.pb[bk],), (ytb,))
            k.dma(Ys[t2], yt, (ytb,), (Ysb,), key=ytb)
        while late_jobs:
            late_jobs.pop(0)()

    if next_stage():
        bb = k.alloc([64], BF16)
        k.dma(bb[0], bb_d, (), (bb[1],), key=bb[1])
        fT = k.alloc([8, S], BF16)
        stg = k.alloc([4, 1024], F32)
        w = k.alloc([8, 1024], BF16)
        wv = fn_w_out.rearrange("(kc p) m -> p kc m", p=128)
        for hf in range(2):
            k.load_cast(w[0][:, hf * 4:(hf + 1) * 4, :], w[1], wv[:, hf * 4:(hf + 1) * 4, :], stg,
                        eng="act" if hf == 0 else "dve")
        brow_f = k.alloc([1024], F32)
        brow = k.alloc([1024], BF16)
        k.dma(brow_f[0][0:1, :], fn_b_out, (), (brow_f[1],), key=brow_f[1])
        k.copy("pool", brow[0][0:1, :], brow_f[0][0:1, :], (brow_f[1],), (brow[1],))
        ln.load_gb(ln_tok_g[0], ln_tok_b[0])
        NZ = 4
        YY = [k.alloc([1024], BF16) for _ in range(NZ)]
        for k1 in range(64):
            yy, yyb = rr(YY, k1)
            for ro in range(2):
                k.dma(yy[ro * 64:(ro + 1) * 64, :], Ys[:, ro * 64 + k1, :], (Ysb,), (yyb,), key=(id(yyb), ro))
            bk = k1 % 4
            ps = k.bank(bk)
            for cc in range(8):
                k.mm(ps[:, cc * 64:(cc + 1) * 64], yy[:, cc * 128:(cc + 1) * 128], bb[0], True, True,
                     (yyb, bb[1]), (k.pb[bk],))
            k.copy("act" if k1 % 2 == 0 else "dve", fT[0][:, :, k1::64], ps.rearrange("p (c k) -> p c k", c=8),
                   (k.pb[bk],), (fT[1],))
        h0t = [k.alloc([D], F32) for _ in range(2)]
        rt = [k.alloc([D], F32) for _ in range(2)]
        for t in range(NT):
            ht, htb = rr(h0t, t)
            r, rb = rr(rt, t)
            k.dma(ht, H[0][t * 128:(t + 1) * 128, :], (Hb[0],), (htb,), key=htb)
            b0 = 4 + 2 * (t % 2)
            ps = k.bank(b0, 2)
            for hf in range(2):
                for cc in range(8):
                    k.mm(ps[:, hf * 512:(hf + 1) * 512], fT[0][:, cc, t * 128:(t + 1) * 128],
                         w[0][:, cc, hf * 512:(hf + 1) * 512], cc == 0, False, (fT[1], w[1]), (k.pb[b0 + hf],))
                k.mm(ps[:, hf * 512:(hf + 1) * 512], ones[0][0:1, 0:128], brow[0][0:1, hf * 512:(hf + 1) * 512],
                     False, True, (ones[1], brow[1]), (k.pb[b0 + hf],))
            k.stt(r, ht, ALPHA, ps, ALU.mult, ALU.add, (htb, k.pb[b0], k.pb[b0 + 1]), (rb,))
            ln.apply(r, rb, H[1][t * 128:(t + 1) * 128, :], HT[1][:, :, t * 128:(t + 1) * 128], t % 2, Hb[1], HTb[1])
        cur = 1

    def ffn_stage(layer, src, dst):
        TB = 512
        ln.pool_pow = True
        cw = k.alloc([44, 4], F32)
        k.dma(cw[0].rearrange("p c k -> p (c k)"), ff_cw[layer], (), (cw[1],), key=cw[1])
        wd = k.alloc([NFC, D], BF16)
        stg = [k.alloc([2, D], F32) for _ in range(2)]
        wdv = ff_w_down[layer].rearrange("(fc p) m -> p fc m", p=128)
        for i in range(NFC // 2):
            k.load_cast(wd[0][:, 2 * i:2 * i + 2, :], wd[1], wdv[:, 2 * i:2 * i + 2, :], rr(stg, i),
                        eng="act" if i % 2 == 0 else "dve")
        ln.load_gb(ln_ffn_g[layer], ln_ffn_b[layer])
        hblk = [k.alloc([8, TB + 2], BF16) for _ in range(2)]
        aT = [k.alloc([NFC, TB], BF16) for _ in range(1)]
        wbf = [k.alloc([2, 8, 128], BF16) for _ in range(4)]
        acc = [[k.alloc([TB], F32) for _ in range(3)] for _ in range(2)]
        pend_gate = []
        sg = [k.alloc([TB], F32) for _ in range(2)]
        h0t = [k.alloc([D], F32) for _ in range(3)]
        rt = [k.alloc([D], F32) for _ in range(3)]
        it = 0
        pend_ln = []

        def flush_ln():
            while pend_ln:
                pend_ln.pop(0)()

        for sbk in range(S // TB):
            T0 = sbk * TB
            hb, hbb = rr(hblk, sbk)
            lo = max(T0 - 1, 0); hi = min(T0 + TB + 1, S)
            if sbk == 0:
                k.memset("pool", hb[:, :, 0:1], 0.0, (hbb,))
            if sbk == S // TB - 1:
                k.memset("pool", hb[:, :, TB + 1:TB + 2], 0.0, (hbb,))
            for kc in range(8):
                k.dma(hb[:, kc, lo - (T0 - 1):hi - (T0 - 1)], HT[src][:, kc, lo:hi], (HTb[src],), (hbb,),
                      key=(id(hbb), kc % 4))
            a, ab = rr(aT, sbk)
            for fc in range(NFC):
                wb, wbb = rr(wbf, it)
                for gv in range(2):
                    k.dma(wb[:, gv], WbUp[layer][gv * NFC + fc].rearrange("p (k m) -> p k m", k=8),
                          (WbUpb[layer],), (wbb,), key=(id(wbb), gv))
                accs = []
                for gv in range(2):
                    b0 = 4 * (it % 2) + 2 * gv
                    ps = k.bank(b0, 2)
                    for (c0, c1, bk) in ((0, 512, b0), (512, 514, b0 + 1)):
                        for kc in range(8):
                            k.mm(ps[:, c0:c1], wb[:, gv, kc, :], hb[:, kc, c0:c1],
                                 kc == 0, kc == 7, (wbb, hbb), (k.pb[bk],))
                    ac, acb = acc[gv][it % 3]
                    ch = gv * NFC + fc
                    pbs = (k.pb[b0], k.pb[b0 + 1])
                    k.act(ac, ps[:, 1:TB + 1], AF.Identity, pbs + (cw[1],), (acb,),
                          bias=cw[0][:, ch, 3:4], scale=cw[0][:, ch, 1:2])
                    if gv == 1:
                        while pend_gate:
                            pend_gate.pop(0)()
                    k.stt(ac, ps[:, 0:TB], cw[0][:, ch, 0:1], ac, ALU.mult, ALU.add, pbs + (cw[1], acb), (acb,))
                    k.stt(ac, ps[:, 2:TB + 2], cw[0][:, ch, 2:3], ac, ALU.mult, ALU.add, pbs + (cw[1], acb), (acb,))
                    accs.append((ac, acb))
                def gate(accs=accs, it=it, fc=fc, a=a, ab=ab):
                    s_, sb_ = rr(sg, it)
                    k.act(s_, accs[0][0], AF.Silu, (accs[0][1],), (sb_,))
                    k.tt("pool", a[:, fc, :], s_, accs[1][0], ALU.mult, (sb_, accs[1][1]), (ab,))
                pend_gate.append(gate)
                it += 1
                if fc == 1:
                    flush_ln()
            while pend_gate:
                pend_gate.pop(0)()
            for tt_ in range(TB // 128):
                t = sbk * (TB // 128) + tt_
                ht, htb = rr(h0t, t)
                r, rb = rr(rt, t)
                k.dma(ht, H[src][t * 128:(t + 1) * 128, :], (Hb[src],), (htb,), key=htb)
                b0 = 2 * (tt_ % 2) + (0 if (it % 2 == 0) else 4)
                ps = k.bank(b0, 2)
                for hf in range(2):
                    for fc in range(NFC):
                        k.mm(ps[:, hf * 512:(hf + 1) * 512], a[:, fc, tt_ * 128:(tt_ + 1) * 128],
                             wd[0][:, fc, hf * 512:(hf + 1) * 512], fc == 0, fc == NFC - 1, (ab, wd[1]),
                             (k.pb[b0 + hf],))
                k.stt(r, ht, ALPHA, ps, ALU.mult, ALU.add, (htb, k.pb[b0], k.pb[b0 + 1]), (rb,))
                flush_ln()
                tpb = (b0 + 4) % 8
                pend_ln.append(lambda r=r, rb=rb, t=t, tpb=tpb: ln.apply(
                    r, rb, H[dst][t * 128:(t + 1) * 128, :], HT[dst][:, :, t * 128:(t + 1) * 128], tpb,
                    Hb[dst], HTb[dst]))
        flush_ln()

    if next_stage():
        ffn_stage(0, 1, 2)
        cur = 2

    def bc(ap, dims, off=0):
        return bass.AP(ap.tensor, ap.offset + off, [list(ap.ap[0])] + [list(d) for d in dims])

    NSL = 5

    def ssd_s1(src):
        TB = 512
        wv = ssd_w_in.rearrange("(kc p) m -> p kc m", p=128)
        tri = k.alloc([4, 128], F32)
        k.dma(tri[0].rearrange("p a b -> p (a b)"), tri_d, (), (tri[1],), key=tri[1])
        tri16 = k.alloc([4, 128], BF16)
        k.copy("dve", tri16[0], tri[0], (tri[1],), (tri16[1],))
        vec = k.alloc([160], F32)
        k.dma(vec[0], ssd_vec.partition_broadcast(128), (), (vec[1],), key=vec[1])
        abc_ = k.alloc([64], F32)
        k.act(abc_[0], vec[0][:, 64:128], AF.Exp, (vec[1],), (abc_[1],))
        k.ts("dve", abc_[0], abc_[0], -1.0, None, ALU.mult, None, (abc_[1],), (abc_[1],))
        cwp = k.alloc([32, 6], F32)
        k.dma(cwp[0].rearrange("p c k -> p (c k)"), ssd_cwp, (), (cwp[1],), key=cwp[1])
        wz = k.alloc([8, 2048], BF16)
        wdt = k.alloc([8, 64], BF16)
        stg = [k.alloc([8, 512], F32) for _ in range(1)]
        uS = [k.alloc([TB + 4], F32) for _ in range(3)]
        for q in range(4):
            k.load_cast(wz[0][:, :, q * 512:(q + 1) * 512], wz[1], wv[:, :, q * 512:(q + 1) * 512], rr(stg, q),
                        eng="act" if q % 2 == 0 else "dve")
        k.load_cast(wdt[0], wdt[1], wv[:, :, 6144:6208], (stg[0][0][:, :, 0:64], stg[0][1]), eng="dve")
        hblk = [k.alloc([8, TB + 4], BF16) for _ in range(2)]
        wbf = [k.alloc([8, 128], BF16) for _ in range(4)]
        acc = [k.alloc([TB], F32) for _ in range(3)]
        xo = [k.alloc([TB], BF16) for _ in range(4)]
        Xblk = [k.alloc([4, 2048], BF16) for _ in range(1)]
        Bblk = [k.alloc([4, 1024], BF16) for _ in range(1)]
        zt = [k.alloc([2048], F32) for _ in range(2)]
        sm = k.alloc([4, 8, 64], F32)
        csb = k.alloc([4, 4, 32], F32)
        hl = k.alloc([2, 4, 64], BF16)
        DT = k.alloc([4, NSL, 64], F32)
        actT = k.alloc([1024], F32)
        it = 0
        for tb in range(S // TB):
            T0 = tb * TB
            hb, hbb = rr(hblk, tb)
            lo = max(T0 - 2, 0); hi = min(T0 + TB + 2, S)
            if tb == 0:
                k.memset("pool", hb[:, :, 0:2], 0.0, (hbb,))
            if tb == S // TB - 1:
                k.memset("pool", hb[:, :, TB + 2:TB + 4], 0.0, (hbb,))
            for kc in range(8):
                k.dma(hb[:, kc, lo - (T0 - 2):hi - (T0 - 2)], HT[src][:, kc, lo:hi], (HTb[src],), (hbb,),
                      key=(id(hbb), kc % 4))
            Xb_, Xbb = rr(Xblk, tb)
            Bb_, Bbb = rr(Bblk, tb)
            pend_a = []
            pend_b = []
            for ch in range(32):
                wb, wbb = rr(wbf, it)
                k.dma(wb, WbIn[ch].rearrange("p (k m) -> p k m", k=8), (WbInb,), (wbb,), key=wbb)
                b0 = (0, 2, 5)[it % 3]
                ps = k.bank(b0, 2)
                for (c0, c1, bk) in ((0, 512, b0), (512, 516, b0 + 1)):
                    for kc in range(8):
                        k.mm(ps[:, c0:c1], wb[:, kc, :], hb[:, kc, c0:c1], kc == 0, kc == 7, (wbb, hbb), (k.pb[bk],))
                ac, acb = rr(acc, it)
                us_, usb = rr(uS, it)
                pbs = (k.pb[b0], k.pb[b0 + 1])
                k.copy("act", us_, ps[:, 0:TB + 4], pbs, (usb,))
                k.act(ac, us_[:, 2:TB + 2], AF.Identity, (usb, cwp[1]), (acb,),
                      bias=cwp[0][:, ch, 5:6], scale=cwp[0][:, ch, 2:3])
                while pend_a:
                    pend_a.pop(0)()
                for tap in (0, 1, 3, 4):
                    k.stt(ac, us_[:, tap:TB + tap], cwp[0][:, ch, tap:tap + 1], ac, ALU.mult, ALU.add,
                          (usb, cwp[1], acb), (acb,))
                while len(pend_b) > 1:
                    pend_b.pop(0)()
                o, ob = rr(xo, it)

                def tail_a(ch=ch, o=o, ob=ob, T0=T0, ac=ac, acb=acb):
                    k.act(o, ac, AF.Silu, (acb,), (ob,))
                    if 16 <= ch < 24:
                        k.dma(BTs[:, ch - 16, T0:T0 + TB], o, (ob,), (BTsb,), key=ob)
                    if ch >= 24:
                        k.dma(CTs[:, ch - 24, T0:T0 + TB], o, (ob,), (CTsb,), key=ob)

                def tail(ch=ch, o=o, ob=ob, T0=T0, ac=ac, acb=acb):
                    if ch < 24:
                        pst = k.bank(4, 1, BF16)
                        for tt_ in range(4):
                            k.tr(pst[:, tt_ * 128:(tt_ + 1) * 128], o[:, tt_ * 128:(tt_ + 1) * 128], ident[0],
                                 (ob, ident[1]), (k.pb[4],))
                        if ch < 16:
                            k.copy("act", Xb_[:, :, ch * 128:(ch + 1) * 128],
                                   pst[:, 0:512].rearrange("p (t c) -> p t c", t=4), (k.pb[4],), (Xbb,))
                        else:
                            g = ch - 16
                            k.copy("act", Bb_[:, :, g * 128:(g + 1) * 128],
                                   pst[:, 0:512].rearrange("p (t c) -> p t c", t=4), (k.pb[4],), (Bbb,))
                pend_a.append(tail_a)
                pend_b.append(tail)
                it += 1
            for tt_ in range(4):
                t = tb * 4 + tt_
                tok = slice(2 + tt_ * 128, 2 + (tt_ + 1) * 128)
                z_, zb_ = rr(zt, t)
                for q in range(4):
                    bk = 1 + 2 * (q % 2)
                    ps = k.bank(bk)
                    for kc in range(8):
                        k.mm(ps, hb[:, kc, tok], wz[0][:, kc, q * 512:(q + 1) * 512], kc == 0, kc == 7,
                             (hbb, wz[1]), (k.pb[bk],))
                    if tt_ == 0 and q == 0:
                        while pend_a:
                            pend_a.pop(0)()
                    if tt_ == 0 and q == 1:
                        while pend_b:
                            pend_b.pop(0)()
                        k.dma(Xs[T0:T0 + TB, :].rearrange("(t p) c -> p t c", p=128), Xb_, (Xbb,), (Xsb,), key=Xbb)
                        k.dma(Bs[T0:T0 + TB, :].rearrange("(t p) c -> p t c", p=128), Bb_, (Bbb,), (Bsb,), key=Bbb)
                    k.act(z_[:, q * 512:(q + 1) * 512], ps, AF.Silu, (k.pb[bk],), (zb_,))
                k.dma(Zs[t * 128:(t + 1) * 128, :], z_, (zb_,), (Zsb,), key=zb_)
            p7 = k.bank(7)
            pb7 = k.pb[7]
            for tt_ in range(4):
                tok = slice(2 + tt_ * 128, 2 + (tt_ + 1) * 128)
                for kc in range(8):
                    k.mm(p7[:, tt_ * 64:(tt_ + 1) * 64], hb[:, kc, tok], wdt[0][:, kc, :], kc == 0, kc == 7,
                         (hbb, wdt[1]), (pb7,))
            s_, sb_ = sm
            d1 = s_[:, :, 0, :]; dtv = s_[:, :, 1, :]; lndt = s_[:, :, 2, :]; dA = s_[:, :, 3, :]
            tmp = s_[:, :, 4, :]; tmp2 = s_[:, :, 5, :]
            k.tt("dve", d1, p7[:, 0:256].rearrange("p (t c) -> p t c", t=4), bc(vec[0][:, 0:64], [(0, 4), (1, 64)]),
                 ALU.add, (pb7, vec[1]), (sb_,))
            k.act(d1, d1, AF.Exp, (sb_,), (sb_,))
            k.act(dtv, d1, AF.Ln, (sb_,), (sb_,), bias=1.0)
            k.act(lndt, dtv, AF.Ln, (sb_,), (sb_,))
            k.tt("dve", dA, dtv, bc(abc_[0], [(0, 4), (1, 64)]), ALU.mult, (sb_, abc_[1]), (sb_,))
            k.copy("dve", hl[0][:, 0], dA, (sb_,), (hl[1],))
            k.tt("dve", hl[0][:, 1], dA, hl[0][:, 0], ALU.subtract, (sb_, hl[1]), (hl[1],))
            pcs = k.bank(1)
            pT = k.bank(2, 2)
            for tt_ in range(4):
                c0 = tt_ * 128
                for (dst_c, lhs_sel, col_sel) in ((0, 0, slice(0, 32)), (32, 1, slice(32, 64))):
                    for pi in range(2):
                        k.mm(pcs[:, c0 + dst_c:c0 + dst_c + 32], tri16[0][:, lhs_sel, :], hl[0][:, pi, tt_, col_sel],
                             pi == 0, pi == 1, (tri16[1], hl[1]), (k.pb[1],))
                for pi in range(2):
                    k.mm(pcs[:, c0 + 64:c0 + 128], ones[0][:, 0:128], hl[0][:, pi, tt_, :], pi == 0, pi == 1,
                         (ones[1], hl[1]), (k.pb[1],))
                t0c = tt_ * 256
                for (dst_c, col_sel, rsel) in ((0, slice(0, 32), 0), (128, slice(32, 64), 2)):
                    for pi in range(2):
                        k.mm(pT[0:32, t0c + dst_c:t0c + dst_c + 128], hl[0][:, pi, tt_, col_sel], tri16[0][:, rsel, :],
                             pi == 0, pi == 1, (tri16[1], hl[1]), (k.pb[2 + tt_ // 2],))
            k.copy("act", csb[0].rearrange("p t q c -> p (t q c)"), pcs, (k.pb[1],), (csb[1],))
            k.copy("act", actT[0][0:32, :], pT[0:32, :], (k.pb[2], k.pb[3]), (actT[1],))
            k.dma(ACs[tb * 4:(tb + 1) * 4].rearrange("t d h l -> h t d l"),
                  actT[0][0:32, :].rearrange("p (t d l) -> p t d l", t=4, d=2), (actT[1],), (ACsb,), key=actT[1])
            cs_ = csb[0]
            acf = cs_[:, :, 0, :]; ecb = cs_[:, :, 1, :]; totf = cs_[:, :, 2, :]; totb = cs_[:, :, 3, :]
            dt_, dtb_ = DT
            cr = (csb[1],)
            k.copy("dve", dt_[:, :, 0, 0:32], acf, cr, (dtb_,))
            k.ts("dve", dt_[:, :, 0, 32:64], ecb, -1.0, None, ALU.mult, None, cr, (dtb_,))
            k.tt("dve", tmp[:, :, 0:32], totf, acf, ALU.subtract, cr, (sb_,))
            k.tt("dve", tmp[:, :, 0:32], tmp[:, :, 0:32], lndt[:, :, 0:32], ALU.add, (sb_,), (sb_,))
            k.tt("dve", tmp[:, :, 32:64], ecb, lndt[:, :, 32:64], ALU.add, cr + (sb_,), (sb_,))
            k.act(dt_[:, :, 1, :], tmp, AF.Exp, (sb_,), (dtb_,))
            k.copy("dve", tmp2[:, :, 0:32], acf, cr, (sb_,))
            k.tt("dve", tmp2[:, :, 32:64], totb, ecb, ALU.subtract, cr, (sb_,))
            k.act(dt_[:, :, 2, :], tmp2, AF.Exp, (sb_,), (dtb_,))
            k.act(dt_[:, :, 3, :], cs_[:, :, 2:4, :], AF.Exp, cr, (dtb_,))
            k.copy("dve", dt_[:, :, 4, :], dtv, (sb_,), (dtb_,))
            k.dma(DTs[T0:T0 + TB].rearrange("(t p) s c -> p t s c", p=128), dt_, (dtb_,), (DTsb,), key=dtb_)

    def ssd_scan(d):
        tri = k.alloc([4, 128], F32)
        k.dma(tri[0].rearrange("p a b -> p (a b)"), tri_d, (), (tri[1],), key=tri[1])
        mask = tri[0][:, 0, :] if d == 0 else tri[0][:, 3, :]
        Hf = k.alloc([2048], F32)
        Hbf = k.alloc([2048], BF16)
        NB = 3
        Xt = [k.alloc([2048], BF16) for _ in range(NB)]
        BTt = [k.alloc([8, 128], BF16) for _ in range(NB)]
        CTt = [k.alloc([8, 128], BF16) for _ in range(NB)]
        Bt = [k.alloc([1024], BF16) for _ in range(NB)]
        DTt = [k.alloc([NSL, 64], F32) for _ in range(NB)]
        Abc = [k.alloc([32, 128], F32) for _ in range(2)]
        Abc2 = [Buf(), Buf()]
        Eb = [k.alloc([32, 128], BF16) for _ in range(2)]
        Mt = [k.alloc([32, 128], BF16) for _ in range(3)]
        cbm = [k.alloc([8, 128], BF16) for _ in range(2)]
        yo = k.alloc([2048], F32)
        yv = [k.alloc([2048], F32) for _ in range(2)]
        Xw = [k.alloc([2048], BF16) for _ in range(2)]
        Xd = [k.alloc([2048], BF16) for _ in range(2)]
        order = list(range(NT)) if d == 0 else list(range(NT - 1, -1, -1))
        Yout = Yf if d == 0 else Yb
        Youtb = Yfb if d == 0 else Ybb
        def P1(i):
            c = order[i]
            last = (i == NT - 1)
            tok = slice(c * 128, (c + 1) * 128)
            X, Xb = rr(Xt, i); BT, BTb = rr(BTt, i); CT, CTb = rr(CTt, i); B_, Bb = rr(Bt, i)
            DTc, DTb_ = rr(DTt, i); A_, Ab = rr(Abc, i)
            k.dma(A_.rearrange("p h l -> p (h l)"),
                  ACs[c, d].rearrange("h l -> (h l)").partition_broadcast(128), (ACsb,), (Ab, rr(Abc2, i)), key=Ab)
            k.dma(DTc, DTs[tok], (DTsb,), (DTb_,), key=DTb_)
            k.dma(BT, BTs[:, :, tok], (BTsb,), (BTb,), key=BTb)
            k.dma(CT, CTs[:, :, tok], (CTsb,), (CTb,), key=CTb)
            k.dma(X, Xs[tok, :], (Xsb,), (Xb,), key=Xb)
            k.dma(B_, Bs[tok, :], (Bsb,), (Bb,), key=Bb)
            dsl = slice(d * 32, (d + 1) * 32)
            p45 = k.bank(4, 2)
            for g in range(8):
                k.mm(p45[:, g * 128:(g + 1) * 128], BT[:, g, :], CT[:, g, :], True, True, (BTb, CTb),
                     (k.pb[4 + g // 4],))
            Ab2 = rr(Abc2, i)
            k.tt("pool", A_[:, 0:18, :], bc(DTc[:, 0, d * 32:d * 32 + 18], [(1, 18), (0, 128)]), A_[:, 0:18, :],
                 ALU.subtract, (Ab, DTb_), (Ab,))
            k.tt("dve", A_[:, 18:32, :], bc(DTc[:, 0, d * 32 + 18:d * 32 + 32], [(1, 14), (0, 128)]), A_[:, 18:32, :],
                 ALU.subtract, (Ab2, DTb_), (Ab2,))
            k.act(A_, A_, AF.Relu, (Ab, Ab2), (Ab, Ab2))
            E_, Ebb = rr(Eb, i)
            k.act(E_, A_, AF.Exp, (Ab, Ab2), (Ebb,), scale=-1.0)
            cb_, cbb = rr(cbm, i)
            k.tt("dve", cb_, p45.rearrange("p (g l) -> p g l", g=8), bc(mask, [(0, 8), (1, 128)]), ALU.mult,
                 (k.pb[4], k.pb[5], tri[1]), (cbb,))
            M_, Mb = rr(Mt, i)
            k.tt("dve", M_.rearrange("p (g r) l -> p g r l", g=8), E_.rearrange("p (g r) l -> p g r l", g=8),
                 bc(cb_[:, 0, :], [(128, 8), (0, 4), (1, 128)]), ALU.mult, (Ebb, cbb), (Mb,))
            xd_, xdb = rr(Xd, i)
            k.tt("pool", xd_.rearrange("p (h q) -> p h q", h=32), X.rearrange("p (h q) -> p h q", h=32),
                 bc(DTc[:, 4, dsl], [(1, 32), (0, 64)]), ALU.mult, (Xb, DTb_), (xdb,))
            xw_, xwb = rr(Xw, i)
            if not last:
                k.tt("pool", xw_.rearrange("p (h q) -> p h q", h=32), X.rearrange("p (h q) -> p h q", h=32),
                     bc(DTc[:, 1, dsl], [(1, 32), (0, 64)]), ALU.mult, (Xb, DTb_), (xwb,))

        def P2(i):
            c = order[i]
            first = (i == 0)
            last = (i == NT - 1)
            tok = slice(c * 128, (c + 1) * 128)
            CT, CTb = rr(CTt, i); B_, Bb = rr(Bt, i)
            DTc, DTb_ = rr(DTt, i)
            M_, Mb = rr(Mt, i)
            xd_, xdb = rr(Xd, i)
            xw_, xwb = rr(Xw, i)
            dsl = slice(d * 32, (d + 1) * 32)
            p03 = k.bank(0, 4)
            pb03 = tuple(k.pb[0:4])
            if not first:
                for g in range(8):
                    k.mm(p03[:, g * 256:(g + 1) * 256], CT[:, g, :], Hbf[0][:, g * 256:(g + 1) * 256], True, True,
                         (CTb, Hbf[1]), (k.pb[g // 2],))
                k.tt("dve", yo[0].rearrange("p (h q) -> p h q", h=32), p03.rearrange("p (h q) -> p h q", h=32),
                     bc(DTc[:, 2, dsl], [(1, 32), (0, 64)]), ALU.mult, pb03 + (DTb_,), (yo[1],))

        def P2b(i):
            c = order[i]
            first = (i == 0)
            last = (i == NT - 1)
            tok = slice(c * 128, (c + 1) * 128)
            CT, CTb = rr(CTt, i); B_, Bb = rr(Bt, i)
            DTc, DTb_ = rr(DTt, i)
            M_, Mb = rr(Mt, i)
            xd_, xdb = rr(Xd, i)
            xw_, xwb = rr(Xw, i)
            dsl = slice(d * 32, (d + 1) * 32)
            p03 = k.bank(0, 4)
            pb03 = tuple(k.pb[0:4])
            for h in range(32):
                k.mm(p03[:, h * 64:(h + 1) * 64], M_[:, h, :], xd_[:, h * 64:(h + 1) * 64], True, True, (Mb, xdb),
                     (k.pb[h // 8],))
            y_, yb_ = rr(yv, i)
            if first:
                k.copy("act", y_, p03, pb03, (yb_,))
            else:
                k.tt("dve", y_, p03, yo[0], ALU.add, pb03 + (yo[1],), (yb_,))
            k.dma(Yout[tok, :], y_, (yb_,), (Youtb,), key=yb_)
            if not last:
                p47 = k.bank(4, 4)
                pb47 = tuple(k.pb[4:8])
                for g in range(8):
                    k.mm(p47[:, g * 256:(g + 1) * 256], B_[:, g * 128:(g + 1) * 128], xw_[:, g * 256:(g + 1) * 256],
                         True, True, (Bb, xwb), (k.pb[4 + g // 2],))
                if first:
                    k.copy("act", Hf[0], p47, pb47, (Hf[1],))
                else:
                    k.tt("pool", Hf[0].rearrange("p (h q) -> p h q", h=32), Hf[0].rearrange("p (h q) -> p h q", h=32),
                         bc(DTc[:, 3, dsl], [(1, 32), (0, 64)]), ALU.mult, (Hf[1], DTb_), (Hf[1],))
                    k.tt("dve", Hf[0], Hf[0], p47, ALU.add, pb47 + (Hf[1],), (Hf[1],))
                k.copy("act", Hbf[0], Hf[0], (Hf[1],), (Hbf[1],))

        P1(0)
        for i in range(NT):
            if i + 1 < NT:
                P1(i + 1)
            P2(i)
            P2b(i)

    def ssd_s3(src, dst, layer):
        ln.flush()
        ln.pool_pow = False
        vec = k.alloc([160], F32)
        k.dma(vec[0], ssd_vec.partition_broadcast(128), (), (vec[1],), key=vec[1])
        ng = k.alloc([2048], F32)
        k.dma(ng[0], ssd_norm_g.partition_broadcast(128), (), (ng[1],), key=ng[1])
        wo = k.alloc([16, D], BF16)
        stg = [k.alloc([2, D], F32) for _ in range(2)]
        wov = ssd_w_out.rearrange("(kc p) m -> p kc m", p=128)
        for i in range(8):
            k.load_cast(wo[0][:, 2 * i:2 * i + 2, :], wo[1], wov[:, 2 * i:2 * i + 2, :], rr(stg, i),
                        eng="act" if i % 2 == 0 else "dve")
        ln.load_gb(ln_tok_g[layer], ln_tok_b[layer])
        NB = 2
        Xt = [k.alloc([2048], BF16) for _ in range(NB)]
        Yt = [k.alloc([2048], F32) for _ in range(NB)]
        Y2t = [k.alloc([2048], F32) for _ in range(NB)]
        Zt = [k.alloc([2048], F32) for _ in range(NB)]
        t1 = k.alloc([2048], F32)
        ss = [k.alloc([8], F32) for _ in range(2)]
        ynb = [k.alloc([2048], BF16) for _ in range(3)]
        yT = [k.alloc([16, 128], BF16) for _ in range(2)]
        h0t = [k.alloc([D], F32) for _ in range(2)]
        rt = [k.alloc([D], F32) for _ in range(3)]

        def phaseA(t):
            tok = slice(t * 128, (t + 1) * 128)
            X, Xb = rr(Xt, t); Y, Yb_ = rr(Yt, t); Y2, Y2b = rr(Y2t, t); Z, Zb = rr(Zt, t)
            k.dma(X, Xs[tok, :], (Xsb,), (Xb,), key=Xb)
            k.dma(Y, Yf[tok, :], (Yfb,), (Yb_,), key=Yb_)
            k.dma(Y2, Yb[tok, :], (Ybb,), (Y2b,), key=Y2b)
            k.dma(Z, Zs[tok, :], (Zsb,), (Zb,), key=Zb)
            k.tt("pool", t1[0].rearrange("p (h q) -> p h q", h=32), X.rearrange("p (h q) -> p h q", h=32),
                 bc(vec[0][:, 128:160], [(1, 32), (0, 64)]), ALU.mult, (Xb, vec[1]), (t1[1],))
            k.tt("dve", Y, Y, Y2, ALU.add, (Yb_, Y2b), (Yb_,))
            k.tt("dve", Y, Y, t1[0], ALU.add, (Yb_, t1[1]), (Yb_,))
            k.tt("pool", Y, Y, Z, ALU.mult, (Yb_, Zb), (Yb_,))
            k.tt("dve", t1[0], Y, Y, ALU.mult, (Yb_,), (t1[1],))
            s_, sb_ = rr(ss, t)
            k.s.add("dve", lambda e, s_=s_: e.tensor_reduce(s_, t1[0].rearrange("p (g q) -> p g q", g=8),
                                                            mybir.AxisListType.X, ALU.add), (t1[1],), (sb_,))
            k.act(s_, s_, AF.Sqrt, (sb_,), (sb_,), bias=EPS, scale=1.0 / 256.0)
            k.s.add("dve", lambda e, s_=s_: e.reciprocal(s_, s_), (sb_,), (sb_,))
            k.tt("dve", Y.rearrange("p (g q) -> p g q", g=8), Y.rearrange("p (g q) -> p g q", g=8),
                 bc(s_, [(1, 8), (0, 256)]), ALU.mult, (Yb_, sb_), (Yb_,))
            yb16, yb16b = rr(ynb, t)
            k.tt("pool", yb16, Y, ng[0], ALU.mult, (Yb_, ng[1]), (yb16b,))

        def phaseB(t):
            tok = slice(t * 128, (t + 1) * 128)
            yb16, yb16b = rr(ynb, t)
            b0 = 2 * (t % 2)
            pst = k.bank(b0, 2, BF16)
            for c in range(16):
                k.tr(pst[:, c * 128:(c + 1) * 128], yb16[:, c * 128:(c + 1) * 128], ident[0], (yb16b, ident[1]),
                     (k.pb[b0 + c // 8],))
            yT_, yTb = rr(yT, t)
            k.copy("act", yT_.rearrange("p c t -> p (c t)"), pst, (k.pb[b0], k.pb[b0 + 1]), (yTb,))
            ht, htb = rr(h0t, t)
            r, rb = rr(rt, t)
            k.dma(ht, H[src][tok, :], (Hb[src],), (htb,), key=htb)
            b1 = 4 + 2 * (t % 2)
            ps = k.bank(b1, 2)
            for hf in range(2):
                for kc in range(16):
                    k.mm(ps[:, hf * 512:(hf + 1) * 512], yT_[:, kc, :], wo[0][:, kc, hf * 512:(hf + 1) * 512],
                         kc == 0, kc == 15, (yTb, wo[1]), (k.pb[b1 + hf],))
            k.stt(r, ht, ALPHA, ps, ALU.mult, ALU.add, (htb, k.pb[b1], k.pb[b1 + 1]), (rb,))

        def phaseC(t):
            tok = slice(t * 128, (t + 1) * 128)
            r, rb = rr(rt, t)
            ln.apply(r, rb, H[dst][tok, :], HT[dst][:, :, tok], 2 * (t % 2), Hb[dst], HTb[dst])

        for step in range(NT + 2):
            if step < NT:
                phaseA(step)
            if 0 <= step - 1 < NT:
                phaseB(step - 1)
            if 0 <= step - 2 < NT:
                phaseC(step - 2)

    if next_stage():
        ssd_s1(2)
    if next_stage():
        ssd_scan(0)
    if next_stage():
        ssd_scan(1)
    if next_stage():
        ssd_s3(2, 3, 1)
        cur = 3
    if next_stage():
        ffn_stage(1, 3, 4)
        cur = 4

    ln.flush()
    k.barrier()
    k.top = persist_top
    ft, fb = k.alloc([8, D], F32)
    for q in range(S // 1024):
        k.dma(ft, H[cur][q * 1024:(q + 1) * 1024, :].rearrange("(p a) d -> p a d", a=8), (Hb[cur],), (fb,), key=fb)
        k.dma(y[q * 1024:(q + 1) * 1024, :].rearrange("(p a) d -> p a d", a=8), ft, (fb,), (), key=fb)
    k.s.add("sp", None, (fb,), (fb,))

    k.s.emit(nc, es)
    es.close()
    return nc


def make_consts():
    c = {}
    bf = ml_dtypes.bfloat16
    c["c_ident"] = np.eye(128, dtype=np.float32).astype(bf)
    c["c_identf"] = np.eye(128, dtype=np.float32)
    j = np.arange(128, dtype=np.float64)
    ang = 2 * np.pi * np.outer(j, j) / 128.0
    sc = 1.0 / math.sqrt(128.0 * S)
    c["c_cs128"] = np.concatenate([np.cos(ang) * sc, -np.sin(ang) * sc], axis=1).astype(np.float32).astype(bf)
    t1 = np.arange(64, dtype=np.float64)[:, None, None]
    t2 = np.arange(64, dtype=np.float64)[None, :, None]
    k1 = np.arange(64, dtype=np.float64)[None, None, :]
    th = -2 * np.pi * (t2 * k1 / 4096.0 + t1 * k1 / 64.0)
    Pr = np.cos(th); Pi = np.sin(th)
    ma = np.zeros((2, 64, 64, 2, 64), dtype=np.float64)
    ma[0, :, :, 0, :] = Pr
    ma[1, :, :, 0, :] = -Pi
    ma[0, :, :, 1, :] = Pi
    ma[1, :, :, 1, :] = Pr
    c["c_ma"] = ma.reshape(128, 64 * 128).astype(np.float32).astype(bf)
    a = 2 * np.pi * np.outer(np.arange(64.0), np.arange(64.0)) / 64.0
    c["c_bb"] = np.concatenate([np.cos(a), np.sin(a)], axis=0).astype(np.float32).astype(bf)
    r = np.arange(128)[:, None]; cl = np.arange(128)[None, :]
    tri = np.stack([(cl >= r), (cl > r), -(cl > r).astype(np.float32), (cl <= r)], axis=1).astype(np.float32)
    c["c_tri"] = tri.reshape(128, 512)
    return c


def prep_inputs(inputs):
    f = lambda a: np.ascontiguousarray(a, dtype=np.float32)
    shared = {}
    shared["emb_ln_g"] = f(inputs["emb_ln_g"]); shared["emb_ln_b"] = f(inputs["emb_ln_b"])
    shared["fn_w_in"] = f(inputs["fn_w_in"][0]); shared["fn_b_in"] = f(inputs["fn_b_in"][0:1])
    shared["fn_w_out"] = f(inputs["fn_w_out"][0]); shared["fn_b_out"] = f(inputs["fn_b_out"][0:1])
    shared["ln_tok_g"] = f(inputs["ln_tok_g"]); shared["ln_tok_b"] = f(inputs["ln_tok_b"])
    shared["ff_w_up"] = f(inputs["ff_w_up"])
    cw4 = np.concatenate([inputs["ff_conv_w"], inputs["ff_conv_b"][:, None, :]], axis=1)
    shared["ff_cwp"] = f(cw4.reshape(cw4.shape[0], 4, 44, 128).transpose(0, 3, 2, 1).reshape(cw4.shape[0], 128, 176))
    shared["ff_w_down"] = f(inputs["ff_w_down"])
    shared["ln_ffn_g"] = f(inputs["ln_ffn_g"]); shared["ln_ffn_b"] = f(inputs["ln_ffn_b"])
    shared["ssd_w_in"] = f(inputs["ssd_w_in"][0])
    cw6 = np.concatenate([inputs["ssd_conv_w"][0], inputs["ssd_conv_b"][0][None, :]], axis=0)
    shared["ssd_cwp"] = f(cw6.reshape(6, 32, 128).transpose(2, 1, 0).reshape(128, 192))
    shared["ssd_vec"] = f(np.concatenate([inputs["ssd_dt_bias_fwd"][0], inputs["ssd_dt_bias_bwd"][0],
                                          inputs["ssd_a_log_fwd"][0], inputs["ssd_a_log_bwd"][0], inputs["ssd_d"][0]]))
    shared["ssd_norm_g"] = f(inputs["ssd_norm_g"][0]); shared["ssd_w_out"] = f(inputs["ssd_w_out"][0])
    shared.update(make_consts())
    x = f(inputs["x"])
    return [dict(shared, x=x[b]) for b in range(x.shape[0])]


def run(inputs, n_stages=99, debug=False):
    nc = bass.Bass("TRN2", target_bir_lowering=False)
    build(nc, n_stages=n_stages, debug=debug)
    in_maps = prep_inputs(inputs)
    res = run_bass_kernel_spmd(nc, in_maps, core_ids=list(range(len(in_maps))))
    return res.results


def kernel(**inputs):
    results = run(inputs)
    out = np.stack([np.asarray(r["y"]) for r in results], axis=0)
    return out.astype(np.float32)
```
